# Optimizing a Trainium2 kernel written in Bass

```python
import math
import jax, jax.numpy as jnp
from jax import lax
import numpy as np

D_MODEL = 1024
BATCH = 16
SEQ = 4096
DEPTH = 2

GRID_W = 64
PLE_DIM = 256
AB_WIDTH = D_MODEL // 2
NA_HEADS = 8
NA_HEAD_DIM = AB_WIDTH // NA_HEADS
NA_WIN_ROWS = 8
NA_WIN_COLS = 16
S5_GROUP = 16
S5_GROUPS = AB_WIDTH // S5_GROUP
S5_STATE = 64
RET_HEADS = 4
RET_QK_DIM = D_MODEL // RET_HEADS
RET_V_DIM = 2 * RET_QK_DIM
RET_CHUNK = 128
ROPE_BASE = 10000.0
D_FF = 2816
CONV_W = 3
N_EVEN = (DEPTH + 1) // 2
N_ODD = DEPTH // 2
EPS = 1e-6

kernel_name = 'hybrid_natten_s5_retnet_encoder'

F32 = jnp.float32


def rms_norm(x, g):
    xf = x.astype(F32)
    y = xf * lax.rsqrt(jnp.mean(xf * xf, axis=-1, keepdims=True) + EPS)
    return (y * g.astype(F32)).astype(x.dtype)


def head_rms(y):
    yf = y.astype(F32)
    return (yf * lax.rsqrt(jnp.mean(yf * yf, axis=-1, keepdims=True) + EPS)).astype(y.dtype)


def neighbourhood_attention(q, k, v, rpb):
    b, s, h, dh = q.shape
    rows = s // GRID_W
    kr = min(NA_WIN_ROWS, rows)
    kc = NA_WIN_COLS
    qg = q.reshape(b, rows, GRID_W, h, dh) * (dh ** -0.5)
    kg = k.reshape(b, rows, GRID_W, h, dh)
    vg = v.reshape(b, rows, GRID_W, h, dh)
    cols = jnp.arange(GRID_W)
    col_start = jnp.clip(cols - kc // 2, 0, GRID_W - kc)
    col_idx = col_start[:, None] + jnp.arange(kc)[None, :]
    dc_idx = col_idx - cols[:, None] + (NA_WIN_COLS - 1)

    def row_fn(r):
        rs = jnp.clip(r - kr // 2, 0, rows - kr)
        q_r = lax.dynamic_index_in_dim(qg, r, axis=1, keepdims=False)
        k_win = lax.dynamic_slice_in_dim(kg, rs, kr, axis=1)[:, :, col_idx]
        v_win = lax.dynamic_slice_in_dim(vg, rs, kr, axis=1)[:, :, col_idx]
        dr_idx = rs + jnp.arange(kr) - r + (NA_WIN_ROWS - 1)
        bias = rpb[:, dr_idx[:, None, None], dc_idx[None]].transpose(0, 2, 1, 3)
        logits = jnp.einsum('bchd,bicjhd->bhcij', q_r, k_win).astype(F32) + bias.astype(F32)
        probs = jax.nn.softmax(logits.reshape(b, h, GRID_W, kr * kc), axis=-1)
        probs = probs.reshape(logits.shape).astype(v.dtype)
        return jnp.einsum('bhcij,bicjhd->bchd', probs, v_win)

    out = lax.map(row_fn, jnp.arange(rows))
    return out.transpose(1, 0, 2, 3, 4).reshape(b, s, h * dh)


def _linear_recurrence(e1, e2):
    a1, b1 = e1
    a2, b2 = e2
    return a1 * a2, a2 * b1 + b2


def s5_bidirectional(u, lam_re, lam_im, log_dt, b_re, b_im, c_re, c_im, d_skip):
    bsz, s, _ = u.shape
    ug = u.astype(F32).reshape(bsz, s, S5_GROUPS, S5_GROUP)
    y = ug * d_skip.astype(F32).reshape(S5_GROUPS, S5_GROUP)
    for direction in range(2):
        lam = lax.complex(lam_re[direction].astype(F32), lam_im[direction].astype(F32))
        dt = jnp.exp(log_dt[direction].astype(F32))[:, None]
        lam_bar = jnp.exp(lam * dt)
        b_c = lax.complex(b_re[direction].astype(F32), b_im[direction].astype(F32))
        b_bar = ((lam_bar - 1.0) / lam)[:, :, None] * b_c
        bu = lax.complex(jnp.einsum('gph,bsgh->bsgp', jnp.real(b_bar), ug),
                         jnp.einsum('gph,bsgh->bsgp', jnp.imag(b_bar), ug))
        a = jnp.broadcast_to(lam_bar, (1, s, S5_GROUPS, S5_STATE))
        _, states = lax.associative_scan(_linear_recurrence, (a, bu), reverse=(direction == 1), axis=1)
        c = lax.complex(c_re[direction].astype(F32), c_im[direction].astype(F32))
        y = y + jnp.real(jnp.einsum('gnp,bsgp->bsgn', c, states))
    return y.reshape(bsz, s, AB_WIDTH).astype(u.dtype)


def s5_glu(y, w_glu, b_glu):
    yg = jax.nn.gelu(y, approximate=False)
    return yg * jax.nn.sigmoid(yg @ w_glu + b_glu)


def na_s5_mixer(h, w_in, rpb, lam_re, lam_im, log_dt, b_re, b_im, c_re, c_im, d_skip, w_glu, b_glu, w_out):
    b, s, _ = h.shape
    z = h @ w_in
    q, k, v, u = jnp.split(z, 4, axis=-1)
    shp = (b, s, NA_HEADS, NA_HEAD_DIM)
    a_out = neighbourhood_attention(q.reshape(shp), k.reshape(shp), v.reshape(shp), rpb)
    b_out = s5_glu(s5_bidirectional(u, lam_re, lam_im, log_dt, b_re, b_im, c_re, c_im, d_skip), w_glu, b_glu)
    return jnp.concatenate([a_out, b_out], axis=-1) @ w_out


def rotary(x):
    s, d = x.shape[1], x.shape[-1]
    half = d // 2
    inv_freq = ROPE_BASE ** (-jnp.arange(half, dtype=F32) / half)
    ang = jnp.arange(s, dtype=F32)[:, None] * inv_freq[None, :]
    cos = jnp.cos(ang)[None, :, None, :]
    sin = jnp.sin(ang)[None, :, None, :]
    xf = x.astype(F32)
    x1, x2 = xf[..., :half], xf[..., half:]
    return jnp.concatenate([x1 * cos - x2 * sin, x1 * sin + x2 * cos], axis=-1).astype(x.dtype)


def retention_bidirectional(q, k, v, log_gamma):
    b, s, h, dk = q.shape
    dv = v.shape[-1]
    L = RET_CHUNK
    n = s // L

    def chunks(t):
        return t.transpose(0, 2, 1, 3).reshape(b, h, n, L, t.shape[-1])

    qc, kc, vc = chunks(q), chunks(k), chunks(v)
    lg_f = log_gamma[0].astype(F32)[:, None]
    lg_b = log_gamma[1].astype(F32)[:, None]
    pos = jnp.arange(L, dtype=F32)
    diff = pos[:, None] - pos[None, :]
    decay = jnp.where(diff >= 0, jnp.exp(lg_f[:, :, None] * jnp.abs(diff)),
                      jnp.exp(lg_b[:, :, None] * jnp.abs(diff)))
    scores = jnp.einsum('bhnid,bhnjd->bhnij', qc, kc) * decay[:, None].astype(qc.dtype)
    out = jnp.einsum('bhnij,bhnje->bhnie', scores, vc)

    def to_scan(t):
        return t.transpose(2, 0, 1, 3, 4)

    vs = to_scan(vc)
    state0 = jnp.zeros((b, h, dk, dv), dtype=q.dtype)

    def cross(q_dec, k_dec, c_dec, reverse):
        cdec = c_dec[None, :, :, None].astype(q.dtype)

        def step(state, inp):
            qn, kn, vn = inp
            y = jnp.einsum('bhld,bhde->bhle', qn, state)
            state = cdec * state + jnp.einsum('bhld,bhle->bhde', kn, vn)
            return state, y

        qs = to_scan(qc * q_dec[None, :, None, :, None].astype(q.dtype))
        ks = to_scan(kc * k_dec[None, :, None, :, None].astype(q.dtype))
        _, ys = lax.scan(step, state0, (qs, ks, vs), reverse=reverse)
        return ys.transpose(1, 2, 0, 3, 4)

    fwd = cross(jnp.exp(lg_f * (pos + 1.0)), jnp.exp(lg_f * (L - 1.0 - pos)), jnp.exp(lg_f * L), False)
    bwd = cross(jnp.exp(lg_b * (L - pos)), jnp.exp(lg_b * pos), jnp.exp(lg_b * L), True)
    out = out + fwd + bwd
    return out.reshape(b, h, s, dv).transpose(0, 2, 1, 3)


def retention_mixer(h, w_in, decay_param, w_out):
    b, s, _ = h.shape
    z = h @ w_in
    q, k, v, g = jnp.split(z, [D_MODEL, 2 * D_MODEL, 2 * D_MODEL + RET_HEADS * RET_V_DIM], axis=-1)
    q = rotary(q.reshape(b, s, RET_HEADS, RET_QK_DIM))
    k = rotary(k.reshape(b, s, RET_HEADS, RET_QK_DIM)) * (RET_QK_DIM ** -0.5)
    v = v.reshape(b, s, RET_HEADS, RET_V_DIM)
    log_gamma = -jnp.exp(decay_param.astype(F32))
    y = head_rms(retention_bidirectional(q, k, v, log_gamma))
    y = jax.nn.silu(g) * y.reshape(b, s, RET_HEADS * RET_V_DIM)
    return y @ w_out


def conv_ffn(h, w_up, conv_w, conv_b, w_down):
    u = h @ w_up
    u = lax.conv_general_dilated(
        u, conv_w[:, None, :].astype(u.dtype), window_strides=(1,),
        padding=((CONV_W // 2, CONV_W // 2),), dimension_numbers=('NWC', 'WIO', 'NWC'),
        feature_group_count=u.shape[-1]) + conv_b
    a, g = jnp.split(u, 2, axis=-1)
    return (jax.nn.gelu(g, approximate=False) * a) @ w_down


def setup_inputs(seed: int = 0) -> dict:
    key = jax.random.key(seed)
    ks = jax.random.split(key, 32)

    def nrm(k, shape, scale):
        return scale * jax.random.normal(k, shape, F32)

    d = D_MODEL
    lam_im_base = jnp.pi * jnp.arange(S5_STATE, dtype=F32)
    ret_base = jnp.log(-jnp.log(1.0 - 2.0 ** (-5.0 - jnp.arange(RET_HEADS, dtype=F32))))
    ret_w_in_cols = 2 * d + 2 * RET_HEADS * RET_V_DIM
    return {
        'x': nrm(ks[0], (BATCH, SEQ, d), 1.0),
        'p': nrm(ks[1], (DEPTH, BATCH, SEQ, PLE_DIM), 1.0),
        'ab_norm': 1.0 + nrm(ks[2], (N_EVEN, d), 0.02),
        'ab_w_in': nrm(ks[3], (N_EVEN, d, 4 * AB_WIDTH), d ** -0.5),
        'na_rpb': nrm(ks[4], (N_EVEN, NA_HEADS, 2 * NA_WIN_ROWS - 1, 2 * NA_WIN_COLS - 1), 0.02),
        's5_lambda_re': -0.5 + nrm(ks[5], (N_EVEN, 2, S5_GROUPS, S5_STATE), 0.01),
        's5_lambda_im': lam_im_base + nrm(ks[6], (N_EVEN, 2, S5_GROUPS, S5_STATE), 0.01),
        's5_log_dt': jax.random.uniform(ks[7], (N_EVEN, 2, S5_GROUPS), F32, math.log(0.001), math.log(0.1)),
        's5_b_re': nrm(ks[8], (N_EVEN, 2, S5_GROUPS, S5_STATE, S5_GROUP), (2 * S5_GROUP) ** -0.5),
        's5_b_im': nrm(ks[9], (N_EVEN, 2, S5_GROUPS, S5_STATE, S5_GROUP), (2 * S5_GROUP) ** -0.5),
        's5_c_re': nrm(ks[10], (N_EVEN, 2, S5_GROUPS, S5_GROUP, S5_STATE), (2 * S5_STATE) ** -0.5),
        's5_c_im': nrm(ks[11], (N_EVEN, 2, S5_GROUPS, S5_GROUP, S5_STATE), (2 * S5_STATE) ** -0.5),
        's5_d': nrm(ks[12], (N_EVEN, AB_WIDTH), 1.0),
        's5_w_glu': nrm(ks[13], (N_EVEN, AB_WIDTH, AB_WIDTH), AB_WIDTH ** -0.5),
        's5_b_glu': nrm(ks[14], (N_EVEN, AB_WIDTH), 0.02),
        'ab_w_out': nrm(ks[15], (N_EVEN, d, d), d ** -0.5),
        'ret_norm': 1.0 + nrm(ks[16], (N_ODD, d), 0.02),
        'ret_w_in': nrm(ks[17], (N_ODD, d, ret_w_in_cols), d ** -0.5),
        'ret_decay': ret_base + nrm(ks[18], (N_ODD, 2, RET_HEADS), 0.05),
        'ret_w_out': nrm(ks[19], (N_ODD, RET_HEADS * RET_V_DIM, d), (RET_HEADS * RET_V_DIM) ** -0.5),
        'ffn_norm': 1.0 + nrm(ks[20], (DEPTH, d), 0.02),
        'ffn_w_up': nrm(ks[21], (DEPTH, d, 2 * D_FF), d ** -0.5),
        'ffn_conv_w': nrm(ks[22], (DEPTH, CONV_W, 2 * D_FF), CONV_W ** -0.5),
        'ffn_conv_b': nrm(ks[23], (DEPTH, 2 * D_FF), 0.02),
        'ffn_w_down': nrm(ks[24], (DEPTH, D_FF, d), D_FF ** -0.5),
        'ple_norm': 1.0 + nrm(ks[25], (DEPTH, d), 0.02),
        'ple_w_gate': nrm(ks[26], (DEPTH, d, d), d ** -0.5),
        'ple_w_proj': nrm(ks[27], (DEPTH, PLE_DIM, d), PLE_DIM ** -0.5),
        'final_norm': 1.0 + nrm(ks[28], (d,), 0.02),
    }


def reference(x, p, ab_norm, ab_w_in, na_rpb, s5_lambda_re, s5_lambda_im, s5_log_dt, s5_b_re, s5_b_im,
              s5_c_re, s5_c_im, s5_d, s5_w_glu, s5_b_glu, ab_w_out, ret_norm, ret_w_in, ret_decay, ret_w_out,
              ffn_norm, ffn_w_up, ffn_conv_w, ffn_conv_b, ffn_w_down, ple_norm, ple_w_gate, ple_w_proj,
              final_norm):
    for i in range(DEPTH):
        j = i // 2
        if i % 2 == 0:
            h = rms_norm(x, ab_norm[j])
            x = x + na_s5_mixer(h, ab_w_in[j], na_rpb[j], s5_lambda_re[j], s5_lambda_im[j], s5_log_dt[j],
                                s5_b_re[j], s5_b_im[j], s5_c_re[j], s5_c_im[j], s5_d[j], s5_w_glu[j],
                                s5_b_glu[j], ab_w_out[j])
        else:
            h = rms_norm(x, ret_norm[j])
            x = x + retention_mixer(h, ret_w_in[j], ret_decay[j], ret_w_out[j])
        x = x + conv_ffn(rms_norm(x, ffn_norm[i]), ffn_w_up[i], ffn_conv_w[i], ffn_conv_b[i], ffn_w_down[i])
        gate = jax.nn.sigmoid(rms_norm(x, ple_norm[i]) @ ple_w_gate[i])
        x = x + gate * (p[i] @ ple_w_proj[i])
    return rms_norm(x, final_norm)
```

```python
import numpy as np
from contextlib import ExitStack
import concourse.bass as bass
import concourse.mybir as mybir
from concourse.bass_utils import run_bass_kernel_spmd

F32 = mybir.dt.float32
BF16 = mybir.dt.bfloat16
I32 = mybir.dt.int32
ALU = mybir.AluOpType
AF = mybir.ActivationFunctionType
AX = mybir.AxisListType

COMPUTE = ("pe", "act", "dve", "pool")
NDMA_SEMS = {"sp": 20, "pool": 10, "act": 6}


class Op:
    __slots__ = ("eng", "fn", "dma", "idx", "deps", "adeps", "inc", "semval", "sem", "waits", "eidx", "cost", "fin")


class Prog:
    def __init__(self, nc):
        self.nc = nc
        self.ops = []
        self.lastw = {}
        self.rd = {}
        self.per_eng = {e: [] for e in ("pe", "act", "dve", "pool", "sp")}

    def _add(self, eng, fn, reads, writes, dma):
        op = Op()
        op.eng, op.fn, op.dma = eng, fn, dma
        op.idx = len(self.ops)
        op.inc = False
        op.semval = None
        op.sem = None
        op.cost = None
        ad = {}
        for r in reads:
            d = self.lastw.get(r)
            if d is not None:
                ad[d.idx] = (d, True)
        for r in writes:
            d = self.lastw.get(r)
            if d is not None and d.idx not in ad:
                ad[d.idx] = (d, False)
            for d in self.rd.get(r, ()):
                if d.idx not in ad:
                    ad[d.idx] = (d, False)
        ad.pop(op.idx, None)
        op.adeps = ad
        for r in reads:
            self.rd.setdefault(r, []).append(op)
        for r in writes:
            self.lastw[r] = op
            self.rd[r] = []
        self.ops.append(op)
        op.eidx = len(self.per_eng[eng])
        self.per_eng[eng].append(op)
        return op

    def finalize_deps(self):
        for e, lst in self.per_eng.items():
            for i, op in enumerate(lst):
                op.eidx = i
        for op in self.ops:
            best = {}
            deps = []
            for d, raw in op.adeps.values():
                if d.dma:
                    deps.append(d)
                    continue
                if d.eng == op.eng and not op.dma:
                    if (not raw) or op.eng == "pe":
                        continue
                b = best.get(d.eng)
                if b is None or d.eidx > b.eidx:
                    best[d.eng] = d
            op.deps = deps + list(best.values())

    def op(self, eng, fn, reads=(), writes=()):
        return self._add(eng, fn, reads, writes, False)

    def dma(self, queue, fn, reads=(), writes=()):
        return self._add(queue, fn, reads, writes, True)

    def schedule(self, window=64):
        COST = {"pe": 0.25, "act": 0.6, "dve": 0.6, "pool": 0.9, "sp": 0.05}
        import heapq
        engs = list(self.per_eng)
        src = {e: self.per_eng[e] for e in engs}
        nxt = {e: 0 for e in engs}
        buf = {e: [] for e in engs}
        new = {e: [] for e in engs}
        free = {e: 0.0 for e in engs}
        for op in self.ops:
            op.fin = None
        remaining = len(self.ops)
        cand = {e: None for e in engs}
        dirty = set(engs)

        def refill(e):
            b, sl = buf[e], src[e]
            while len(b) < window and nxt[e] < len(sl):
                b.append(sl[nxt[e]])
                nxt[e] += 1

        def find(e):
            refill(e)
            best = None
            fe = free[e]
            n = 0
            for op in buf[e]:
                n += 1
                ok = True
                rdy = fe
                for d, raw in op.adeps.values():
                    f = d.fin
                    if f is None:
                        ok = False
                        break
                    if f > rdy and (raw or d.eng != e or d.dma):
                        rdy = f
                if op.fn is None:
                    if n == 1 and ok:
                        best = (rdy, op)
                    break
                if ok and (best is None or rdy < best[0]):
                    best = (rdy, op)
                    if rdy <= fe:
                        break
            return best

        while remaining:
            for e in dirty:
                cand[e] = find(e)
            dirty = set()
            be = None
            for e in engs:
                c = cand[e]
                if c is not None and (be is None or c[0] < cand[be][0]):
                    be = e
            if be is None:
                for e in engs:
                    cand[e] = find(e)
                    if cand[e] is not None:
                        be = e
                        break
                assert be is not None, "scheduler stuck"
            rdy, op = cand[be]
            c = COST[be] if not op.dma else 0.05
            op.fin = rdy + (c if not op.dma else 4.0)
            free[be] = rdy + c
            buf[be].remove(op)
            new[be].append(op)
            remaining -= 1
            dirty = set(engs) if True else {be}
        self.per_eng = new

    def emit(self, es):
        nc = self.nc
        if getattr(self, "do_schedule", True):
            self.schedule()
        self.finalize_deps()
        for op in self.ops:
            for d in op.deps:
                if not d.dma:
                    d.inc = True
        csem = {e: es.enter_context(nc.semaphore("s_" + e)) for e in COMPUTE}
        for e in COMPUTE:
            c = 0
            for op in self.per_eng[e]:
                if op.dma:
                    continue
                if op.inc:
                    c += 1
                    op.semval = c
                    op.sem = csem[e]
        dsems = {q: [es.enter_context(nc.semaphore("d_%s%d" % (q, i))) for i in range(n)]
                 for q, n in NDMA_SEMS.items()}
        dcur = {q: [0] * n for q, n in NDMA_SEMS.items()}
        dnext = {q: 0 for q in NDMA_SEMS}
        pre_wait = {}
        for op in [o for e in self.per_eng for o in self.per_eng[e]]:
            if op.dma:
                q = op.eng
                i = dnext[q]
                dnext[q] = (i + 1) % len(dsems[q])
                pre_wait[op.idx] = (dsems[q][i], dcur[q][i])
                dcur[q][i] += 16
                op.sem = dsems[q][i]
                op.semval = dcur[q][i]
        known = {e: {} for e in self.per_eng}
        for op in [o for e in self.per_eng for o in self.per_eng[e]]:
            k = known[op.eng]
            w = {}
            cand = [(d.sem, d.semval) for d in op.deps]
            if op.dma and pre_wait[op.idx][1] > 0:
                cand.append(pre_wait[op.idx])
            for sem, val in cand:
                key = id(sem)
                if k.get(key, (None, 0))[1] >= val:
                    continue
                if key in w and w[key][1] >= val:
                    continue
                w[key] = (sem, val)
            for key, sv in w.items():
                k[key] = sv
            op.waits = list(w.values())
        self.n_waits = sum(len(o.waits) for o in self.ops)
        block = es.enter_context(nc.Block())

        def run(engobj, lst):
            for op in lst:
                for sem, val in op.waits:
                    engobj.wait_ge(sem, val)
                if op.fn is None:
                    if op.inc:
                        engobj.nop().then_inc(op.sem, 1)
                    continue
                ins = op.fn(engobj)
                if op.dma:
                    ins.then_inc(op.sem, 16)
                elif op.inc:
                    ins.then_inc(op.sem, 1)

        @block.tensor
        def _(e):
            run(e, self.per_eng["pe"])

        @block.scalar
        def _(e):
            run(e, self.per_eng["act"])

        @block.vector
        def _(e):
            run(e, self.per_eng["dve"])

        @block.gpsimd
        def _(e):
            run(e, self.per_eng["pool"])
            for q in dsems:
                for s, v in zip(dsems[q], dcur[q]):
                    if v > 0:
                        e.wait_ge(s, v)

        @block.sync
        def _(e):
            run(e, self.per_eng["sp"])


NCORES = 8
SEQ = 4096
D = 1024
NSEQ = 2
NTOK = NSEQ * SEQ
TT = 512
NT = NTOK // TT
DFF = 2816
EPS = 1e-6

IN_SHAPES = {
    "x": ([NTOK, D], F32), "p": ([2, NTOK, 256], F32),
    "ab_norm": ([1, 1024], F32), "ab_w_in": ([1, 1024, 2048], F32), "na_rpb": ([1, 8, 15, 31], F32),
    "s5_lambda_re": ([1, 2, 32, 64], F32), "s5_lambda_im": ([1, 2, 32, 64], F32), "s5_log_dt": ([1, 2, 32], F32),
    "s5_b_re": ([1, 2, 32, 64, 16], F32), "s5_b_im": ([1, 2, 32, 64, 16], F32),
    "s5_c_re": ([1, 2, 32, 16, 64], F32), "s5_c_im": ([1, 2, 32, 16, 64], F32),
    "s5_d": ([1, 512], F32), "s5_w_glu": ([1, 512, 512], F32), "s5_b_glu": ([1, 512], F32),
    "ab_w_out": ([1, 1024, 1024], F32), "ret_norm": ([1, 1024], F32), "ret_w_in": ([1, 1024, 6144], F32),
    "ret_decay": ([1, 2, 4], F32), "ret_w_out": ([1, 2048, 1024], F32), "ffn_norm": ([2, 1024], F32),
    "ffn_w_up": ([2, 1024, 5632], F32), "ffn_conv_w": ([2, 3, 5632], F32), "ffn_conv_b": ([2, 5632], F32),
    "ffn_w_down": ([2, 2816, 1024], F32), "ple_norm": ([2, 1024], F32), "ple_w_gate": ([2, 1024, 1024], F32),
    "ple_w_proj": ([2, 256, 1024], F32), "final_norm": ([1024], F32),
    "c_ident": ([128, 128], F32), "c_identb": ([128, 128], BF16),
    "c_cos": ([128, SEQ], F32), "c_sin": ([128, SEQ], F32),
    "c_D1": ([128, 128], F32), "c_M1": ([128, 128], F32), "c_D2": ([128, 128], F32), "c_M2": ([128, 128], F32),
    "c_iota1": ([128, 128], F32), "c_iota2": ([128, 128], F32), "c_pidx": ([128, 2], F32),
    "c_sel": ([128, 8, 8, 128], BF16), "c_selT": ([128, 8, 8, 128], BF16), "c_maskF": ([128, 128], F32), "c_maskB": ([128, 128], F32),
    "l_lre": ([128, 32], F32), "l_lim": ([128, 32], F32), "l_ldt": ([128, 32], F32),
    "l_bre": ([128, 32, 16], F32), "l_bim": ([128, 32, 16], F32), "l_cre": ([128, 32, 16], F32), "l_cim": ([128, 32, 16], F32),
    "l_d": ([128, 32], F32),
}

SCR_SHAPES = {
    "XT0": ([1024, NTOK], F32), "QT": ([512, NTOK], BF16), "KT": ([512, NTOK], BF16), "VN": ([NTOK, 512], BF16),
    "UT": ([512, NTOK], BF16), "AT": ([512, NTOK], BF16), "BT": ([512, NTOK], BF16),
    "XT1": ([1024, NTOK], F32), "MT0": ([DFF, NTOK], BF16), "XT3": ([1024, NTOK], F32),
    "RQ": ([1024, NTOK], BF16), "RK": ([1024, NTOK], BF16), "RKN": ([NTOK, 1024], BF16),
    "RV": ([NTOK, 2048], BF16), "RG": ([NTOK, 2048], BF16), "YB": ([NTOK, 2048], F32), "RY": ([2048, NTOK], BF16),
    "XT4": ([1024, NTOK], F32), "MT1": ([DFF, NTOK], BF16), "OUT": ([NTOK, 1024], F32),
}


class KB:
    def __init__(self, ext_in, ext_out):
        self.nc = bass.Bass("TRN2", target_bir_lowering=False)
        self.P = Prog(self.nc)
        self.es = ExitStack()
        self.ext_in, self.ext_out = set(ext_in), set(ext_out)
        self._d = {}
        self.used_inputs = []
        self.nbank = 0
        nc = self.nc
        self.banks = [self.es.enter_context(nc.psum_tensor("psf%d" % i, [128, 512], F32)) for i in range(6)]
        self.bankbs = [self.es.enter_context(nc.psum_tensor("psb%d" % i, [128, 1024], BF16)) for i in range(2)]
        self.bankb = self.bankbs[0]
        self.uid = 0

    def I(self, name):
        if name not in self._d:
            shape, dt = IN_SHAPES[name]
            self._d[name] = self.nc.dram_tensor(name, list(shape), dt, kind="ExternalInput").ap()
            self.used_inputs.append(name)
        return self._d[name]

    def S(self, name):
        if name not in self._d:
            shape, dt = SCR_SHAPES[name]
            if name in self.ext_in:
                kind = "ExternalInput"
                self.used_inputs.append(name)
            elif name in self.ext_out:
                kind = "ExternalOutput"
            else:
                kind = "Internal"
            self._d[name] = self.nc.dram_tensor(name, list(shape), dt, kind=kind).ap()
        return self._d[name]

    def bank(self, lo=0, hi=5):
        i = lo + self.nbank % (hi - lo)
        self.nbank += 1
        return self.banks[i], ("ps", i)

    def barrier(self):
        P = self.P
        deps = []
        for e in COMPUTE:
            for op in reversed(P.per_eng[e]):
                if not op.dma:
                    deps.append(op)
                    break
        start = getattr(self, "_bar_start", 0)
        deps += [op for op in P.ops[start:] if op.dma]
        self._bar_start = len(P.ops)
        for e in ("pe", "act", "dve", "pool", "sp"):
            op = P._add(e, None, (), (), False)
            op.adeps = {d.idx: (d, True) for d in deps if not ((not d.dma) and d.eng == e)}

    def sballoc(self, stack):
        def sb(name, shape, dt):
            self.uid += 1
            return stack.enter_context(self.nc.sbuf_tensor("%s_%d" % (name, self.uid), list(shape), dt))
        return sb

    def load_w(self, sb, src, K, N, name, CB=2048, stg_sb=None):
        P = self.P
        nk = K // 128
        wt = sb(name, [128, nk, N], BF16)
        if getattr(self, "_stg_owner", None) is not sb:
            self._stg_owner = sb
            self._stg = [(stg_sb or sb)("stg%d" % i, [128, CB], F32) for i in range(3)]
            self._stg_i = 0
        stg = self._stg
        i = self._stg_i
        for k in range(nk):
            for c0 in range(0, N, CB):
                cn = min(CB, N - c0)
                s = stg[i % 3]
                sr = ("stg", i % 3)
                P.dma("sp", lambda e, s=s, k=k, c0=c0, cn=cn: e.dma_start(out=s[:, :cn], in_=src[k * 128:(k + 1) * 128, c0:c0 + cn]),
                      writes=[sr])
                eng = ("pool", "dve", "act")[i % 3]
                if eng == "act":
                    fn = lambda e, s=s, k=k, c0=c0, cn=cn: e.activation(out=wt[:, k, c0:c0 + cn], in_=s[:, :cn], func=AF.Copy)
                else:
                    fn = lambda e, s=s, k=k, c0=c0, cn=cn: e.tensor_copy(out=wt[:, k, c0:c0 + cn], in_=s[:, :cn])
                P.op(eng, fn, reads=[sr], writes=[(name, k, c0)])
                i += 1
        self._stg_i = i
        P._add("pe", None, [(name, k, c0) for k in range(nk) for c0 in range(0, N, CB)], [name], False)
        return wt

    def load_vec(self, sb, src, n, name):
        t = sb(name, [128, n // 128], F32)
        self.P.dma("sp", lambda e: e.dma_start(out=t[:], in_=src.rearrange("(k p) -> p k", p=128), allow_slow_non_contiguous=True),
                   writes=[name])
        return t

    def load_const(self, sb, cname, name=None):
        shape, dt = IN_SHAPES[cname]
        name = name or cname
        t = sb(name, shape, dt)
        src = self.I(cname)
        self.P.dma("sp", lambda e: e.dma_start(out=t[:], in_=src), writes=[name])
        return t

    def rmsnorm(self, xT, xres, g, gname, h, hname, ones, sqb, rstd, nk=8, n=TT):
        P = self.P
        bank, br = self.banks[5], ("ps", 5)
        for k in range(nk):
            s = sqb[k % 2]
            sr = ("sqb", k % 2)
            P.op("act", lambda e, s=s, k=k: e.activation(out=s[:, :n], in_=xT[:, k, :n], func=AF.Square), reads=[xres(k)], writes=[sr])
            P.op("pe", lambda e, s=s, k=k: e.matmul(bank[:, :n], lhsT=ones[:], rhs=s[:, :n], start=(k == 0), stop=(k == nk - 1)),
                 reads=[sr, "ones"], writes=[br])
        P.op("act", lambda e: e.activation(out=rstd[:, :n], in_=bank[:, :n], func=AF.Sqrt, bias=EPS, scale=1.0), reads=[br], writes=["rstd"])
        P.op("dve", lambda e: e.reciprocal(out=rstd[:, :n], in_=rstd[:, :n]), reads=["rstd"], writes=["rstd"])
        for k in range(nk):
            P.op("dve", lambda e, k=k: e.scalar_tensor_tensor(out=h[:, k, :n], in0=xT[:, k, :n], scalar=g[:, k:k + 1], in1=rstd[:, :n],
                                                             op0=ALU.mult, op1=ALU.mult),
                 reads=[xres(k), gname, "rstd"], writes=[(hname, k)])

    def norm_consts(self, sb):
        ones = sb("ones", [128, 128], F32)
        self.P.op("pool", lambda e: e.memset(ones[:], 1.0 / 1024), writes=["ones"])
        sqb = [sb("sqb%d" % i, [128, TT], F32) for i in range(2)]
        rstd = sb("rstd", [128, TT], F32)
        return ones, sqb, rstd

    def evac(self, i, out, in_, reads, writes):
        if i % 2 == 0:
            self.P.op("act", lambda e: e.activation(out=out, in_=in_, func=AF.Copy), reads=reads, writes=writes)
        else:
            self.P.op("dve", lambda e: e.tensor_copy(out=out, in_=in_), reads=reads, writes=writes)


def fm_view(ap, c0, n):
    return ap.rearrange("(k p) t -> p k t", p=128)[:, :, c0:c0 + n]


def phase_A0(K):
    P, nc = K.P, K.nc
    with ExitStack() as st:
        sb = K.sballoc(st)
        w = K.load_w(sb, K.I("ab_w_in")[0], 1024, 2048, "w_in")
        g = K.load_vec(sb, K.I("ab_norm")[0], 1024, "g_ab")
        ident = K.load_const(sb, "c_ident")
        ones, sqb, rstd = K.norm_consts(sb)
        xt = [sb("xt%d" % i, [128, 4, 1024], F32) for i in range(2)]
        xT = sb("xT", [128, 8, TT], F32)
        h = sb("h", [128, 8, TT], BF16)
        ofm = [sb("ofm%d" % i, [128, 4, TT], BF16) for i in range(2)]
        otm = [sb("otm%d" % i, [128, 512], BF16) for i in range(2)]
        x = K.I("x")
        XT0, QT, KT, UT, VN = K.S("XT0"), K.S("QT"), K.S("KT"), K.S("UT"), K.S("VN")
        no = 0
        nv = 0
        for t in range(NT):
            c0 = t * TT
            buf = xt[t % 2]
            br_ = ("xt", t % 2)
            P.dma("sp", lambda e, buf=buf, c0=c0: e.dma_start(out=buf[:], in_=x[c0:c0 + TT, :].rearrange("(n p) d -> p n d", p=128)),
                  writes=[br_])
            for k in range(8):
                bank, bres = K.bank()
                for n in range(4):
                    P.op("pe", lambda e, bank=bank, buf=buf, n=n, k=k: e.transpose(out=bank[:, n * 128:(n + 1) * 128],
                                                                                  in_=buf[:, n, k * 128:(k + 1) * 128], identity=ident[:]),
                         reads=[br_, "c_ident"], writes=[bres])
                K.evac(k, xT[:, k, :], bank[:], [bres], [("xT", k)])
            P.dma("pool", lambda e, c0=c0: e.dma_start(out=fm_view(XT0, c0, TT), in_=xT[:]),
                  reads=[("xT", k) for k in range(8)], writes=[("XT0", t)])
            K.rmsnorm(xT, lambda k: ("xT", k), g, "g_ab", h, "h", ones, sqb, rstd)
            for dst, dn, col0 in ((QT, "QT", 0), (KT, "KT", 512), (UT, "UT", 1536)):
                o = ofm[no % 2]
                ores = ("ofm", no % 2)
                no += 1
                for oc in range(4):
                    bank, bres = K.bank()
                    for k in range(8):
                        P.op("pe", lambda e, bank=bank, k=k, cc=col0 + oc * 128: e.matmul(bank[:], lhsT=w[:, k, cc:cc + 128], rhs=h[:, k, :],
                                                                                        start=(k == 0), stop=(k == 7)),
                             reads=["w_in", ("h", k)], writes=[bres])
                    K.evac(oc, o[:, oc, :], bank[:], [bres], [ores + (oc,)])
                P.dma("pool", lambda e, dst=dst, o=o, c0=c0: e.dma_start(out=fm_view(dst, c0, TT), in_=o[:]),
                      reads=[ores + (oc,) for oc in range(4)], writes=[(dn, t)])
            for sub in range(4):
                bank, bres = K.bank()
                for k in range(8):
                    P.op("pe", lambda e, bank=bank, k=k, sub=sub: e.matmul(bank[:], lhsT=h[:, k, sub * 128:(sub + 1) * 128], rhs=w[:, k, 1024:1536],
                                                                         start=(k == 0), stop=(k == 7)),
                         reads=["w_in", ("h", k)], writes=[bres])
                o = otm[nv % 2]
                ores = ("otm", nv % 2)
                nv += 1
                K.evac(sub, o[:], bank[:], [bres], [ores])
                r0 = c0 + sub * 128
                P.dma("pool", lambda e, o=o, r0=r0: e.dma_start(out=VN[r0:r0 + 128, :], in_=o[:]), reads=[ores], writes=[("VN", t, sub)])
    K.barrier()


PHASES = {}
PHASE_IO = {}
PHASES["A0"] = phase_A0
PHASE_IO["A0"] = ((), ("XT0", "QT", "KT", "UT", "VN"))


def host_consts():
    import ml_dtypes
    c = {}
    c["c_ident"] = np.eye(128, dtype=np.float32)
    c["c_identb"] = np.eye(128, dtype=np.float32).astype(ml_dtypes.bfloat16)
    inv = (np.float32(10000.0) ** (-np.arange(128, dtype=np.float32) / np.float32(128))).astype(np.float32)
    ang = (np.arange(SEQ, dtype=np.float32)[None, :] * inv[:, None]).astype(np.float32)
    c["c_cos"] = np.cos(ang.astype(np.float64)).astype(np.float32)
    c["c_sin"] = np.sin(ang.astype(np.float64)).astype(np.float32)
    jj = np.arange(128, dtype=np.float32)[:, None]
    ii = np.arange(128, dtype=np.float32)[None, :]
    c["c_D1"] = np.maximum(ii - jj, 0).astype(np.float32)
    c["c_M1"] = (ii >= jj).astype(np.float32)
    c["c_D2"] = np.maximum(jj - ii, 0).astype(np.float32)
    c["c_M2"] = (jj > ii).astype(np.float32)
    c["c_iota1"] = np.broadcast_to(ii + 1, (128, 128)).astype(np.float32).copy()
    c["c_iota2"] = np.broadcast_to(128 - ii, (128, 128)).astype(np.float32).copy()
    c["c_pidx"] = np.concatenate([127 - jj, jj], axis=1).astype(np.float32)
    sel = np.zeros((128, 8, 8, 128), np.float32)
    for gl in range(8):
        for j in range(8):
            for h in range(16):
                sel[gl * 16 + h, gl, j, j * 16 + h] = 1.0
    c["c_sel"] = sel.astype(ml_dtypes.bfloat16)
    c["c_selT"] = np.ascontiguousarray(sel.transpose(3, 1, 2, 0)).astype(ml_dtypes.bfloat16)
    jq = (np.arange(128) // 16)[:, None]
    iq = (np.arange(128) // 16)[None, :]
    c["c_maskF"] = (iq >= jq).astype(np.float32)
    c["c_maskB"] = (jq >= iq).astype(np.float32)
    return c


def host_layout(inp):
    o = {}
    def dp(a):
        return np.ascontiguousarray(np.asarray(a).transpose(0, 2, 1).reshape(128, 32))
    o["l_lre"] = dp(inp["s5_lambda_re"][0])
    o["l_lim"] = dp(inp["s5_lambda_im"][0])
    o["l_ldt"] = np.ascontiguousarray(np.repeat(np.asarray(inp["s5_log_dt"][0])[:, None, :], 64, axis=1).reshape(128, 32))
    o["l_bre"] = np.ascontiguousarray(np.asarray(inp["s5_b_re"][0]).transpose(0, 2, 1, 3).reshape(128, 32, 16))
    o["l_bim"] = np.ascontiguousarray(np.asarray(inp["s5_b_im"][0]).transpose(0, 2, 1, 3).reshape(128, 32, 16))
    o["l_cre"] = np.ascontiguousarray(np.asarray(inp["s5_c_re"][0]).transpose(0, 3, 1, 2).reshape(128, 32, 16))
    o["l_cim"] = np.ascontiguousarray(np.asarray(inp["s5_c_im"][0]).transpose(0, 3, 1, 2).reshape(128, 32, 16))
    dd = np.asarray(inp["s5_d"][0]).reshape(32, 16)
    o["l_d"] = np.ascontiguousarray(np.tile(dd.T[None, :, :], (8, 1, 1)).reshape(128, 32))
    return o


def build(phases, ext_in=(), ext_out=()):
    K = KB(ext_in, ext_out)
    with K.es:
        for ph in phases:
            PHASES[ph](K)
        K.P.emit(K.es)
    return K


def run_launch(phases, core_inputs, ext_in=(), ext_out=()):
    K = build(phases, ext_in, ext_out)
    consts = host_consts()
    in_maps = []
    for ci in core_inputs:
        m = {}
        for n in K.used_inputs:
            m[n] = consts[n] if n in consts else ci[n]
        in_maps.append(m)
    res = run_bass_kernel_spmd(K.nc, in_maps, core_ids=list(range(len(core_inputs))))
    return res.results


def phase_outproj(K, srcs, w_ap, kdim, xin_name, xout_name):
    P = K.P
    nk = kdim // 128
    with ExitStack() as st:
        sb = K.sballoc(st)
        w = K.load_w(sb, w_ap, kdim, 1024, "w_o")
        ain = [sb("ain%d" % i, [128, nk, TT], BF16) for i in range(2)]
        xin = [sb("xin%d" % i, [128, 8, TT], F32) for i in range(2)]
        XI, XO = K.S(xin_name), K.S(xout_name)
        for t in range(NT):
            c0 = t * TT
            a, ar = ain[t % 2], ("ain", t % 2)
            xi, xr = xin[t % 2], ("xin", t % 2)
            k0 = 0
            for sname, nch in srcs:
                S_ = K.S(sname)
                P.dma("sp", lambda e, a=a, S_=S_, k0=k0, nch=nch, c0=c0: e.dma_start(out=a[:, k0:k0 + nch, :], in_=fm_view(S_, c0, TT)),
                      writes=[ar + (k,) for k in range(k0, k0 + nch)])
                k0 += nch
            P.dma("sp", lambda e, xi=xi, c0=c0: e.dma_start(out=xi[:], in_=fm_view(XI, c0, TT)), writes=[xr + (k,) for k in range(8)])
            for oc in range(8):
                bank, bres = K.bank()
                for k in range(nk):
                    P.op("pe", lambda e, bank=bank, k=k, oc=oc, a=a: e.matmul(bank[:], lhsT=w[:, k, oc * 128:(oc + 1) * 128], rhs=a[:, k, :],
                                                                            start=(k == 0), stop=(k == nk - 1)),
                         reads=["w_o", ar + (k,)], writes=[bres])
                P.op("dve", lambda e, bank=bank, xi=xi, oc=oc: e.tensor_tensor(out=xi[:, oc, :], in0=bank[:], in1=xi[:, oc, :], op=ALU.add),
                     reads=[bres, xr + (oc,)], writes=[xr + (oc,)])
            P.dma("pool", lambda e, xi=xi, c0=c0: e.dma_start(out=fm_view(XO, c0, TT), in_=xi[:]),
                  reads=[xr + (k,) for k in range(8)], writes=[(xout_name, t)])
    K.barrier()


def phase_ffn_up(K, layer, xin_name, mout_name):
    P = K.P
    W = TT + 1
    with ExitStack() as st:
        sb = K.sballoc(st)
        w = K.load_w(sb, K.I("ffn_w_up")[layer], 1024, 2 * DFF, "w_up")
        g = K.load_vec(sb, K.I("ffn_norm")[layer], 1024, "g_ffn")
        cb = K.load_vec(sb, K.I("ffn_conv_b")[layer], 2 * DFF, "cb")
        cw = sb("cw", [128, 3, 44], F32)
        cwsrc = K.I("ffn_conv_w")[layer]
        P.dma("sp", lambda e: e.dma_start(out=cw[:], in_=cwsrc.rearrange("j (k p) -> p j k", p=128), allow_slow_non_contiguous=True), writes=["cw"])
        ones, sqb, rstd = K.norm_consts(sb)
        xin = [sb("xin%d" % i, [128, 8, TT], F32) for i in range(2)]
        h = sb("h", [128, 8, TT], BF16)
        halo = sb("halo", [128, 44, 2], F32)
        acc = {wh: [sb("acc%s%d" % (wh, i), [128, W], F32) for i in range(3)] for wh in "ag"}
        gel = [sb("gel%d" % i, [128, W], F32) for i in range(3)]
        mt = [sb("mt%d" % i, [128, 22, W + 1], BF16) for i in range(1)]
        XI, MO = K.S(xin_name), K.S(mout_name)
        tiles_per_seq = SEQ // TT
        P.dma("sp", lambda e: e.dma_start(out=xin[0][:], in_=fm_view(XI, 0, TT)), writes=[("xin", 0, k) for k in range(8)])
        for t in range(NT):
            c0 = t * TT
            first = (t % tiles_per_seq == 0)
            last = (t % tiles_per_seq == tiles_per_seq - 1)
            xi, xr = xin[t % 2], ("xin", t % 2)
            if t + 1 < NT:
                P.dma("sp", lambda e, c1=c0 + TT, xn=xin[(t + 1) % 2]: e.dma_start(out=xn[:], in_=fm_view(XI, c1, TT)),
                      writes=[("xin", (t + 1) % 2, k) for k in range(8)])
            K.rmsnorm(xi, lambda k: xr + (k,), g, "g_ffn", h, "h", ones, sqb, rstd)
            m_, mr = mt[0], ("mt", 0)
            for c in range(22):
                i = c % 3
                for wh, ch in (("a", c), ("g", c + 22)):
                    bank, bres = K.bank()
                    for k in range(8):
                        P.op("pe", lambda e, bank=bank, k=k, ch=ch: e.matmul(bank[:], lhsT=w[:, k, ch * 128:(ch + 1) * 128], rhs=h[:, k, :],
                                                                          start=(k == 0), stop=(k == 7)),
                             reads=["w_up", ("h", k)], writes=[bres])
                    ac, acr = acc[wh][i], ("acc", wh, i)
                    hres = ("halo", ch)
                    P.op("act", lambda e, ac=ac, bank=bank, ch=ch: e.activation(out=ac[:, 0:TT], in_=bank[:], func=AF.Identity, bias=cb[:, ch:ch + 1],
                                                                              scale=cw[:, 2, ch:ch + 1]),
                         reads=[bres, "cb", "cw"], writes=[acr])
                    if last:
                        P.op("act", lambda e, ac=ac, bank=bank, ch=ch: e.activation(out=ac[:, TT:W], in_=bank[:, TT - 1:TT], func=AF.Identity, bias=cb[:, ch:ch + 1],
                                                                                  scale=cw[:, 1, ch:ch + 1]),
                             reads=[bres, "cb", "cw"], writes=[acr])
                    P.op("dve", lambda e, ac=ac, bank=bank, ch=ch: e.scalar_tensor_tensor(out=ac[:, 1:TT], in0=bank[:, 0:TT - 1], scalar=cw[:, 1, ch:ch + 1], in1=ac[:, 1:TT],
                                                                                        op0=ALU.mult, op1=ALU.add), reads=[bres, "cw", acr], writes=[acr])
                    hi_ = W if last else TT
                    P.op("dve", lambda e, ac=ac, bank=bank, ch=ch, hi_=hi_: e.scalar_tensor_tensor(out=ac[:, 2:hi_], in0=bank[:, 0:hi_ - 2], scalar=cw[:, 0, ch:ch + 1],
                                                                                                 in1=ac[:, 2:hi_], op0=ALU.mult, op1=ALU.add),
                         reads=[bres, "cw", acr], writes=[acr])
                    if not first:
                        P.op("dve", lambda e, ac=ac, ch=ch: e.scalar_tensor_tensor(out=ac[:, 0:2], in0=halo[:, ch, :], scalar=cw[:, 0, ch:ch + 1], in1=ac[:, 0:2],
                                                                                   op0=ALU.mult, op1=ALU.add), reads=[hres, "cw", acr], writes=[acr])
                        P.op("dve", lambda e, ac=ac, ch=ch: e.scalar_tensor_tensor(out=ac[:, 0:1], in0=halo[:, ch, 1:2], scalar=cw[:, 1, ch:ch + 1], in1=ac[:, 0:1],
                                                                                   op0=ALU.mult, op1=ALU.add), reads=[hres, "cw", acr], writes=[acr])
                    if not last:
                        P.op("act", lambda e, bank=bank, ch=ch: e.activation(out=halo[:, ch, :], in_=bank[:, TT - 2:TT], func=AF.Copy), reads=[bres], writes=[hres])
                ge, ger = gel[i], ("gel", i)
                P.op("act", lambda e, ge=ge, ac=acc["g"][i]: e.activation(out=ge[:], in_=ac[:], func=AF.Gelu), reads=[("acc", "g", i)], writes=[ger])
                P.op("dve", lambda e, ge=ge, ac=acc["a"][i], c=c, m_=m_: e.tensor_tensor(out=m_[:, c, 0:W], in0=ac[:], in1=ge[:], op=ALU.mult),
                     reads=[("acc", "a", i), ger], writes=[mr + (c,)])
            lo = 1 if first else 0
            hi = W if last else TT
            P.dma("pool", lambda e, lo=lo, hi=hi, c0=c0, m_=m_: e.dma_start(out=MO.rearrange("(k p) t -> p k t", p=128)[:, :, c0 - 1 + lo:c0 - 1 + hi],
                                                                   in_=m_[:, :, lo:hi]),
                  reads=[mr + (c,) for c in range(22)], writes=[(mout_name, t)])
    K.barrier()


def phase_ffn_down(K, layer, min_name, xin_name, xout_name, final):
    P = K.P
    with ExitStack() as st:
        sb = K.sballoc(st)
        w = K.load_w(sb, K.I("ffn_w_down")[layer], DFF, 1024, "w_dn")
        wg = K.load_w(sb, K.I("ple_w_gate")[layer], 1024, 1024, "w_pg")
        wp = K.load_w(sb, K.I("ple_w_proj")[layer], 256, 1024, "w_pp")
        gp = K.load_vec(sb, K.I("ple_norm")[layer], 1024, "g_ple")
        if final:
            gf = K.load_vec(sb, K.I("final_norm"), 1024, "g_fin")
            hf = sb("hf", [128, 8, TT], F32)
            otok = [sb("otok%d" % i, [128, 1024], F32) for i in range(2)]
        ident = K.load_const(sb, "c_ident")
        ones, sqb, rstd = K.norm_consts(sb)
        min_ = [sb("min%d" % i, [128, 22, TT], BF16) for i in range(1)]
        xin = [sb("xin%d" % i, [128, 8, TT], F32) for i in range(1)]
        pin = [sb("pin%d" % i, [128, 4, 256], F32) for i in range(2)]
        pT = sb("pT", [128, 2, TT], BF16)
        h = sb("h", [128, 8, TT], BF16)
        sig = [sb("sig%d" % i, [128, TT], F32) for i in range(2)]
        MI, XI = K.S(min_name), K.S(xin_name)
        XO = K.S(xout_name)
        pd = K.I("p")[layer]
        no = 0
        for t in range(NT):
            c0 = t * TT
            m, mr = min_[0], ("min", 0)
            xi, xr = xin[0], ("xin", 0)
            pi, pr = pin[t % 2], ("pin", t % 2)
            P.dma("sp", lambda e, m=m, c0=c0: e.dma_start(out=m[:], in_=fm_view(MI, c0, TT)), writes=[mr])
            P.dma("sp", lambda e, xi=xi, c0=c0: e.dma_start(out=xi[:], in_=fm_view(XI, c0, TT)), writes=[xr + (k,) for k in range(8)])
            P.dma("sp", lambda e, pi=pi, c0=c0: e.dma_start(out=pi[:], in_=pd[c0:c0 + TT, :].rearrange("(n p) d -> p n d", p=128)), writes=[pr])
            for kc in range(2):
                bank, bres = K.bank()
                for n in range(4):
                    P.op("pe", lambda e, bank=bank, pi=pi, n=n, kc=kc: e.transpose(out=bank[:, n * 128:(n + 1) * 128],
                                                                                  in_=pi[:, n, kc * 128:(kc + 1) * 128], identity=ident[:]),
                         reads=[pr, "c_ident"], writes=[bres])
                K.evac(kc, pT[:, kc, :], bank[:], [bres], [("pT", kc)])
            for oc in range(8):
                bank, bres = K.bank()
                for k in range(22):
                    P.op("pe", lambda e, bank=bank, k=k, oc=oc, m=m: e.matmul(bank[:], lhsT=w[:, k, oc * 128:(oc + 1) * 128], rhs=m[:, k, :],
                                                                            start=(k == 0), stop=(k == 21)),
                         reads=["w_dn", mr], writes=[bres])
                P.op("dve", lambda e, bank=bank, xi=xi, oc=oc: e.tensor_tensor(out=xi[:, oc, :], in0=bank[:], in1=xi[:, oc, :], op=ALU.add),
                     reads=[bres, xr + (oc,)], writes=[xr + (oc,)])
            K.rmsnorm(xi, lambda k: xr + (k,), gp, "g_ple", h, "h", ones, sqb, rstd)
            for oc in range(8):
                bankg, bgres = K.bank()
                for k in range(8):
                    P.op("pe", lambda e, bankg=bankg, k=k, oc=oc: e.matmul(bankg[:], lhsT=wg[:, k, oc * 128:(oc + 1) * 128], rhs=h[:, k, :],
                                                                         start=(k == 0), stop=(k == 7)),
                         reads=["w_pg", ("h", k)], writes=[bgres])
                bankp, bpres = K.bank()
                for k in range(2):
                    P.op("pe", lambda e, bankp=bankp, k=k, oc=oc: e.matmul(bankp[:], lhsT=wp[:, k, oc * 128:(oc + 1) * 128], rhs=pT[:, k, :],
                                                                         start=(k == 0), stop=(k == 1)),
                         reads=["w_pp", ("pT", k)], writes=[bpres])
                sg, sgr = sig[oc % 2], ("sig", oc % 2)
                P.op("act", lambda e, sg=sg, bankg=bankg: e.activation(out=sg[:], in_=bankg[:], func=AF.Sigmoid), reads=[bgres], writes=[sgr])
                P.op("dve", lambda e, sg=sg, bankp=bankp: e.tensor_tensor(out=sg[:], in0=bankp[:], in1=sg[:], op=ALU.mult), reads=[bpres, sgr], writes=[sgr])
                P.op("dve", lambda e, sg=sg, xi=xi, oc=oc: e.tensor_tensor(out=xi[:, oc, :], in0=xi[:, oc, :], in1=sg[:], op=ALU.add),
                     reads=[sgr, xr + (oc,)], writes=[xr + (oc,)])
            if not final:
                P.dma("pool", lambda e, xi=xi, c0=c0: e.dma_start(out=fm_view(XO, c0, TT), in_=xi[:]),
                      reads=[xr + (k,) for k in range(8)], writes=[(xout_name, t)])
            else:
                K.rmsnorm(xi, lambda k: xr + (k,), gf, "g_fin", hf, "hf", ones, sqb, rstd)
                for n in range(4):
                    o, ores = otok[no % 2], ("otok", no % 2)
                    no += 1
                    for hb in range(2):
                        bank, bres = K.bank()
                        for kk in range(4):
                            k = hb * 4 + kk
                            P.op("pe", lambda e, bank=bank, k=k, kk=kk, n=n: e.transpose(out=bank[:, kk * 128:(kk + 1) * 128],
                                                                                          in_=hf[:, k, n * 128:(n + 1) * 128], identity=ident[:]),
                                 reads=[("hf", k), "c_ident"], writes=[bres])
                        K.evac(hb, o[:, hb * 512:(hb + 1) * 512], bank[:], [bres], [ores + (hb,)])
                    r0 = c0 + n * 128
                    P.dma("pool", lambda e, o=o, r0=r0: e.dma_start(out=XO[r0:r0 + 128, :], in_=o[:]),
                          reads=[ores + (0,), ores + (1,)], writes=[(xout_name, t, n)])
    K.barrier()


PHASES["A3"] = lambda K: phase_outproj(K, [("AT", 4), ("BT", 4)], K.I("ab_w_out")[0], 1024, "XT0", "XT1")
PHASES["A4"] = lambda K: phase_ffn_up(K, 0, "XT1", "MT0")
PHASES["A5"] = lambda K: phase_ffn_down(K, 0, "MT0", "XT1", "XT3", False)
PHASES["B3"] = lambda K: phase_outproj(K, [("RY", 16)], K.I("ret_w_out")[0], 2048, "XT3", "XT4")
PHASES["B4"] = lambda K: phase_ffn_up(K, 1, "XT4", "MT1")
PHASES["B5"] = lambda K: phase_ffn_down(K, 1, "MT1", "XT4", "OUT", True)


def phase_na(K):
    P = K.P
    NEG = -30000.0
    NB = 4
    with ExitStack() as st:
        sb = K.sballoc(st)
        ident = K.load_const(sb, "c_ident")
        identb = K.load_const(sb, "c_identb")
        KTs = sb("KTs", [64, 8, SEQ], BF16)
        Vs = sb("Vs", [64, 64, 512], BF16)
        Qb = [sb("Qb%d" % i, [64, 8, 512], BF16) for i in range(2)]
        Bf = sb("Bf", [128, 4 * 960], F32)
        sc = [sb("sc%d" % i, [128, 512], F32) for i in range(NB)]
        pr = [sb("pr%d" % i, [128, 512], BF16) for i in range(NB)]
        prT = [sb("prT%d" % i, [64, 8, 128], BF16) for i in range(3)]
        stt = sb("stt", [128, NB, 4], F32)
        aout = [sb("aout%d" % i, [128, 256], F32) for i in range(2)]
        aT = [sb("aT%d" % i, [64, 8, 512], BF16) for i in range(1)]
        QT, KT, VN, AT = K.S("QT"), K.S("KT"), K.S("VN"), K.S("AT")
        hv = lambda ap, c0, n: ap.rearrange("(h p) t -> p h t", p=64)[:, :, c0:c0 + n]
        rpb = K.I("na_rpb")[0]
        P.op("pool", lambda e: e.memset(Bf[:], NEG), writes=["Bf"])
        Bf4 = Bf[:].rearrange("p (h r c) -> p h r c", h=4, r=15)
        nq = 0
        for hp in range(2):
            for c in range(64):
                cs = min(max(c - 8, 0), 48)
                dcs = cs - c + 15
                P.dma(("sp", "act", "pool")[nq % 3], lambda e, c=c, cs=cs, dcs=dcs, hp=hp: e.dma_start(out=Bf4[hp * 64 + c:hp * 64 + c + 1, :, :, cs:cs + 16],
                                                                                                   in_=rpb[hp::2, :, dcs:dcs + 16][None]),
                      reads=[], writes=["Bf"])
                nq += 1
        cnt = 0
        nT = 0
        for s_ in range(NSEQ):
            tb = s_ * SEQ
            P.dma("sp", lambda e, tb=tb: e.dma_start(out=KTs[:], in_=hv(KT, tb, SEQ)), writes=["KTs"])
            for rq in range(4):
                P.dma("sp", lambda e, tb=tb, rq=rq: e.dma_start(out=Vs[:, rq * 16:(rq + 1) * 16, :],
                                                              in_=VN[tb + rq * 1024:tb + (rq + 1) * 1024, :].rearrange("(r c) f -> c r f", c=64)),
                      writes=[("Vs", rq)])
            for r in range(64):
                rr = r % 8
                qb, qbr = Qb[(r // 8) % 2], ("Qb", (r // 8) % 2)
                if rr == 0:
                    P.dma("sp", lambda e, qb=qb, c0=tb + r * 64: e.dma_start(out=qb[:], in_=hv(QT, c0, 512)), writes=[qbr])
                rs = min(max(r - 4, 0), 56)
                dr0 = rs - r + 7
                bankO, bOres = K.bank()
                ao, aor = aout[r % 2], ("aout", r % 2)
                for pp in range(4):
                    i2 = cnt % NB
                    i3 = cnt % 3
                    ib = cnt % 2
                    cnt += 1
                    bankS, bSres = K.bank()
                    for hp in range(2):
                        hd = 2 * pp + hp
                        P.op("pe", lambda e, bankS=bankS, hd=hd, hp=hp, rr=rr, rs=rs, qb=qb: e.matmul(bankS[hp * 64:(hp + 1) * 64, :], lhsT=qb[:, hd, rr * 64:(rr + 1) * 64],
                                                                                                 rhs=KTs[:, hd, rs * 64:rs * 64 + 512], start=True, stop=True,
                                                                                                 tile_position=(0, hp * 64)),
                             reads=[qbr, "KTs"], writes=[bSres])
                    b0 = pp * 960 + dr0 * 64
                    P.op("dve", lambda e, bankS=bankS, i2=i2, b0=b0: e.scalar_tensor_tensor(out=sc[i2][:], in0=bankS[:], scalar=0.125, in1=Bf[:, b0:b0 + 512],
                                                                                          op0=ALU.mult, op1=ALU.add),
                         reads=[bSres, "Bf"], writes=[("sc", i2)])
                    P.op("dve", lambda e, i2=i2: e.tensor_reduce(out=stt[:, i2, 0:1], in_=sc[i2][:], axis=AX.X, op=ALU.max, negate=True),
                         reads=[("sc", i2)], writes=[("nmx", i2)])
                    P.op("act", lambda e, i2=i2: e.activation(out=pr[i2][:], in_=sc[i2][:], func=AF.Exp, bias=stt[:, i2, 0:1], scale=1.0,
                                                             accum_out=stt[:, i2, 1:2]),
                         reads=[("sc", i2), ("nmx", i2)], writes=[("pr", i2), ("rsum", i2)])
                    bb = K.bankbs[ib]
                    for i in range(8):
                        P.op("pe", lambda e, i=i, i2=i2, bb=bb: e.transpose(out=bb[0:64, i * 128:(i + 1) * 128], in_=pr[i2][:, i * 64:(i + 1) * 64], identity=identb[:]),
                             reads=[("pr", i2), "c_identb"], writes=[("psb", ib)])
                    K.evac(cnt, prT[i3][:], bb[0:64, :].rearrange("p (i q) -> p i q", i=8), [("psb", ib)], [("prT", i3)])
                    for hp in range(2):
                        hd = 2 * pp + hp
                        for i in range(8):
                            P.op("pe", lambda e, bankO=bankO, i=i, i3=i3, hd=hd, hp=hp, pp=pp, rs=rs: e.matmul(bankO[hp * 64:(hp + 1) * 64, pp * 64:(pp + 1) * 64],
                                                                                                          lhsT=prT[i3][:, i, hp * 64:(hp + 1) * 64],
                                                                                                          rhs=Vs[:, rs + i, hd * 64:(hd + 1) * 64], start=(i == 0), stop=(i == 7),
                                                                                                          tile_position=(0, hp * 64)),
                                 reads=[("prT", i3), ("Vs", (rs + i) // 16)], writes=[bOres])
                    P.op("dve", lambda e, i2=i2: e.reciprocal(out=stt[:, i2, 2:3], in_=stt[:, i2, 1:2]), reads=[("rsum", i2)], writes=[("rinv", i2)])
                    P.op("act", lambda e, bankO=bankO, ao=ao, pp=pp, i2=i2: e.activation(out=ao[:, pp * 64:(pp + 1) * 64], in_=bankO[:, pp * 64:(pp + 1) * 64],
                                                                                        func=AF.Copy, scale=stt[:, i2, 2:3]),
                         reads=[bOres, ("rinv", i2)], writes=[aor])
                a, ar = aT[0], ("aT", 0)
                bankT, bTres = K.bank()
                for pp in range(4):
                    P.op("pe", lambda e, bankT=bankT, ao=ao, pp=pp: e.transpose(out=bankT[0:64, pp * 128:(pp + 1) * 128], in_=ao[:, pp * 64:(pp + 1) * 64], identity=ident[:]),
                         reads=[aor, "c_ident"], writes=[bTres])
                K.evac(r, a[:, :, rr * 64:(rr + 1) * 64], bankT[0:64, :].rearrange("p (h q) -> p h q", h=8), [bTres], [ar + (rr,)])
                if rr == 7:
                    c0 = tb + (r - 7) * 64
                    P.dma("pool", lambda e, a=a, c0=c0: e.dma_start(out=hv(AT, c0, 512), in_=a[:]),
                          reads=[ar + (q,) for q in range(8)], writes=[("AT", c0)])
    K.barrier()


PHASES["A1"] = phase_na


def phase_ret_in(K):
    P = K.P
    with ExitStack() as st:
        sb = K.sballoc(st)
        w = K.load_w(sb, K.I("ret_w_in")[0], 1024, 6144, "w_ri")
        g = K.load_vec(sb, K.I("ret_norm")[0], 1024, "g_ret")
        identb = K.load_const(sb, "c_identb")
        ones, sqb, rstd = K.norm_consts(sb)
        xin = sb("xin", [128, 8, TT], F32)
        h = sb("h", [128, 8, TT], BF16)
        cs_ = [sb("cos%d" % i, [128, TT], F32) for i in range(2)]
        sn_ = [sb("sin%d" % i, [128, TT], F32) for i in range(2)]
        t1 = [sb("t1_%d" % i, [128, TT], F32) for i in range(2)]
        t2 = [sb("t2_%d" % i, [128, TT], F32) for i in range(2)]
        qk = [sb("qk%d" % i, [128, 8, TT], BF16) for i in range(2)]
        ktok = [sb("ktok%d" % i, [128, 1024], BF16) for i in range(2)]
        vtok = [sb("vtok%d" % i, [128, 2048], BF16) for i in range(2)]
        XI = K.S("XT3")
        RQ, RK, RKN, RV, RG = K.S("RQ"), K.S("RK"), K.S("RKN"), K.S("RV"), K.S("RG")
        ccos, csin = K.I("c_cos"), K.I("c_sin")
        nrot = 0
        nvt = 0
        nkt = 0
        for t in range(NT):
            c0 = t * TT
            pos0 = c0 % SEQ
            cs, sn = cs_[t % 2], sn_[t % 2]
            P.dma("sp", lambda e, c0=c0: e.dma_start(out=xin[:], in_=fm_view(XI, c0, TT)), writes=[("xin", k) for k in range(8)])
            P.dma("sp", lambda e, cs=cs, pos0=pos0: e.dma_start(out=cs[:], in_=ccos[:, pos0:pos0 + TT]), writes=[("cos", t % 2)])
            P.dma("sp", lambda e, sn=sn, pos0=pos0: e.dma_start(out=sn[:], in_=csin[:, pos0:pos0 + TT]), writes=[("sin", t % 2)])
            K.rmsnorm(xin, lambda k: ("xin", k), g, "g_ret", h, "h", ones, sqb, rstd)
            for qi, (dst, dn, scl) in enumerate(((RQ, "RQ", 1.0), (RK, "RK", 0.0625))):
                o, ores = qk[qi], ("qk", qi)
                for hh in range(4):
                    bk = []
                    for half in range(2):
                        bank, bres = K.bank()
                        cc = qi * 1024 + hh * 256 + half * 128
                        for k in range(8):
                            P.op("pe", lambda e, bank=bank, k=k, cc=cc: e.matmul(bank[:], lhsT=w[:, k, cc:cc + 128], rhs=h[:, k, :], start=(k == 0), stop=(k == 7)),
                                 reads=["w_ri", ("h", k)], writes=[bres])
                        bk.append((bank, bres))
                    (b1, b1r), (b2, b2r) = bk
                    for half, (A, B, op) in enumerate((((b1, b1r, cs, ("cos", t % 2)), (b2, b2r, sn, ("sin", t % 2)), ALU.subtract),
                                                       ((b1, b1r, sn, ("sin", t % 2)), (b2, b2r, cs, ("cos", t % 2)), ALU.add))):
                        i2 = nrot % 2
                        nrot += 1
                        P.op("dve", lambda e, A=A, i2=i2, scl=scl: e.scalar_tensor_tensor(out=t1[i2][:], in0=A[0][:], scalar=scl, in1=A[2][:], op0=ALU.mult, op1=ALU.mult),
                             reads=[A[1], A[3]], writes=[("t1", i2)])
                        P.op("dve", lambda e, B=B, i2=i2, scl=scl: e.scalar_tensor_tensor(out=t2[i2][:], in0=B[0][:], scalar=scl, in1=B[2][:], op0=ALU.mult, op1=ALU.mult),
                             reads=[B[1], B[3]], writes=[("t2", i2)])
                        P.op("pool", lambda e, i2=i2, o=o, hh=hh, half=half, op=op: e.tensor_tensor(out=o[:, hh * 2 + half, :], in0=t1[i2][:], in1=t2[i2][:], op=op),
                             reads=[("t1", i2), ("t2", i2)], writes=[ores + (hh * 2 + half,)])
                P.dma("pool", lambda e, dst=dst, o=o, c0=c0: e.dma_start(out=fm_view(dst, c0, TT), in_=o[:]),
                      reads=[ores + (k,) for k in range(8)], writes=[(dn, t)])
            for sub in range(4):
                for k in range(8):
                    P.op("pe", lambda e, k=k, sub=sub: e.transpose(out=K.bankb[:, k * 128:(k + 1) * 128], in_=qk[1][:, k, sub * 128:(sub + 1) * 128], identity=identb[:]),
                         reads=[("qk", 1, k), "c_identb"], writes=[("psb", 0)])
                kt, ktr = ktok[nkt % 2], ("ktok", nkt % 2)
                nkt += 1
                K.evac(sub, kt[:], K.bankb[:], [("psb", 0)], [ktr])
                r0 = c0 + sub * 128
                P.dma("pool", lambda e, kt=kt, r0=r0: e.dma_start(out=RKN[r0:r0 + 128, :], in_=kt[:]), reads=[ktr], writes=[("RKN", r0)])
            for which, (dst, dn, colb) in enumerate(((RV, "RV", 2048), (RG, "RG", 4096))):
                for sub in range(4):
                    vt, vtr = vtok[nvt % 2], ("vtok", nvt % 2)
                    nvt += 1
                    for cbk in range(4):
                        bank, bres = K.bank()
                        cc = colb + cbk * 512
                        for k in range(8):
                            P.op("pe", lambda e, bank=bank, k=k, sub=sub, cc=cc: e.matmul(bank[:], lhsT=h[:, k, sub * 128:(sub + 1) * 128], rhs=w[:, k, cc:cc + 512],
                                                                                     start=(k == 0), stop=(k == 7)),
                                 reads=["w_ri", ("h", k)], writes=[bres])
                        if which == 0:
                            K.evac(cbk, vt[:, cbk * 512:(cbk + 1) * 512], bank[:], [bres], [vtr + (cbk,)])
                        else:
                            P.op("act", lambda e, vt=vt, bank=bank, cbk=cbk: e.activation(out=vt[:, cbk * 512:(cbk + 1) * 512], in_=bank[:], func=AF.Silu),
                                 reads=[bres], writes=[vtr + (cbk,)])
                    r0 = c0 + sub * 128
                    P.dma("pool", lambda e, dst=dst, vt=vt, r0=r0: e.dma_start(out=dst[r0:r0 + 128, :], in_=vt[:]),
                          reads=[vtr + (q,) for q in range(4)], writes=[(dn, r0)])
    K.barrier()


def phase_ret(K):
    P = K.P
    L = 128
    NCH = SEQ // L
    with ExitStack() as st:
        sb = K.sballoc(st)
        ident_b = K.load_const(sb, "c_identb")
        cD1, cM1, cD2, cM2 = [K.load_const(sb, n) for n in ("c_D1", "c_M1", "c_D2", "c_M2")]
        io1, io2, pidx = [K.load_const(sb, n) for n in ("c_iota1", "c_iota2", "c_pidx")]
        lg = sb("lg", [128, 8], F32)
        dsrc = K.I("ret_decay")[0].rearrange("a b -> (a b)").partition_broadcast(128)
        P.dma("sp", lambda e: e.dma_start(out=lg[:], in_=dsrc), writes=["lg"])
        P.op("act", lambda e: e.activation(out=lg[:], in_=lg[:], func=AF.Exp), reads=["lg"], writes=["lg"])
        P.op("dve", lambda e: e.tensor_scalar(out=lg[:], in0=lg[:], scalar1=-1.0, scalar2=None, op0=ALU.mult), reads=["lg"], writes=["lg"])
        decT = sb("decT", [128, 4, 128], F32)
        tmpd = sb("tmpd", [128, 128], F32)
        qd = sb("qd", [128, 2, 4, 128], F32)
        kd = sb("kd", [128, 2, 4], F32)
        cd = sb("cd", [128, 2, 4], F32)
        for hh in range(4):
            lf, lb = lg[:, hh:hh + 1], lg[:, 4 + hh:5 + hh]
            P.op("act", lambda e, hh=hh, lf=lf: e.activation(out=decT[:, hh, :], in_=cD1[:], func=AF.Exp, scale=lf), reads=["lg", "c_D1"], writes=[("decT", hh)])
            P.op("dve", lambda e, hh=hh: e.tensor_tensor(out=decT[:, hh, :], in0=decT[:, hh, :], in1=cM1[:], op=ALU.mult), reads=[("decT", hh), "c_M1"], writes=[("decT", hh)])
            P.op("act", lambda e, lb=lb: e.activation(out=tmpd[:], in_=cD2[:], func=AF.Exp, scale=lb), reads=["lg", "c_D2"], writes=["tmpd"])
            P.op("dve", lambda e: e.tensor_tensor(out=tmpd[:], in0=tmpd[:], in1=cM2[:], op=ALU.mult), reads=["tmpd", "c_M2"], writes=["tmpd"])
            P.op("dve", lambda e, hh=hh: e.tensor_tensor(out=decT[:, hh, :], in0=decT[:, hh, :], in1=tmpd[:], op=ALU.add), reads=[("decT", hh), "tmpd"], writes=[("decT", hh)])
            P.op("act", lambda e, hh=hh, lf=lf: e.activation(out=qd[:, 0, hh, :], in_=io1[:], func=AF.Exp, scale=lf), reads=["lg", "c_iota1"], writes=["qd"])
            P.op("act", lambda e, hh=hh, lb=lb: e.activation(out=qd[:, 1, hh, :], in_=io2[:], func=AF.Exp, scale=lb), reads=["lg", "c_iota2"], writes=["qd"])
            P.op("act", lambda e, hh=hh, lf=lf: e.activation(out=kd[:, 0, hh:hh + 1], in_=pidx[:, 0:1], func=AF.Exp, scale=lf), reads=["lg", "c_pidx"], writes=["kd"])
            P.op("act", lambda e, hh=hh, lb=lb: e.activation(out=kd[:, 1, hh:hh + 1], in_=pidx[:, 1:2], func=AF.Exp, scale=lb), reads=["lg", "c_pidx"], writes=["kd"])
            P.op("act", lambda e, hh=hh: e.activation(out=cd[:, 0, hh:hh + 1], in_=lg[:, hh:hh + 1], func=AF.Exp, scale=float(L)), reads=["lg"], writes=["cd"])
            P.op("act", lambda e, hh=hh: e.activation(out=cd[:, 1, hh:hh + 1], in_=lg[:, 4 + hh:5 + hh], func=AF.Exp, scale=float(L)), reads=["lg"], writes=["cd"])
        qcol = sb("qcol", [128, 2, 4], F32)
        for hh in range(4):
            P.op("act", lambda e, hh=hh: e.activation(out=qcol[:, 0, hh:hh + 1], in_=pidx[:, 1:2], func=AF.Exp, scale=lg[:, hh:hh + 1], bias=lg[:, hh:hh + 1]),
                 reads=["lg", "c_pidx"], writes=["qcol"])
            P.op("act", lambda e, hh=hh: e.activation(out=qcol[:, 1, hh:hh + 1], in_=pidx[:, 0:1], func=AF.Exp, scale=lg[:, 4 + hh:5 + hh], bias=lg[:, 4 + hh:5 + hh]),
                 reads=["lg", "c_pidx"], writes=["qcol"])
        S32 = sb("S32", [128, 8, 512], F32)
        S16 = sb("S16", [128, 8, 512], BF16)
        Qc = [sb("Qc%d" % i, [128, 8, L], BF16) for i in range(2)]
        Kc = [sb("Kc%d" % i, [128, 8, L], BF16) for i in range(2)]
        Kn = [sb("Kn%d" % i, [128, 1024], BF16) for i in range(2)]
        Vn = [sb("Vn%d" % i, [128, 2048], BF16) for i in range(2)]
        Gn = [sb("Gn%d" % i, [128, 2048], BF16) for i in range(2)]
        Yn = [sb("Yn%d" % i, [128, 2048], F32) for i in range(2)]
        Qd = [sb("Qd%d" % i, [128, 2, L], BF16) for i in range(2)]
        Kd = [sb("Kd%d" % i, [128, 256], BF16) for i in range(2)]
        STt = [sb("ST%d" % i, [128, L], BF16) for i in range(2)]
        o32 = [sb("o32_%d" % i, [128, 512], F32) for i in range(2)]
        junk = sb("junk", [128, 512], F32)
        hst = sb("hst", [128, 2, 2], F32)
        ytok = [sb("ytok%d" % i, [128, 2048], BF16) for i in range(2)]
        yT = [sb("yT%d" % i, [128, 16, 512], BF16) for i in range(2)]
        RQ, RK, RKN, RV, RG, YB, RY = [K.S(n) for n in ("RQ", "RK", "RKN", "RV", "RG", "YB", "RY")]
        it = 0
        nh = 0
        for s in range(NSEQ):
            tb = s * SEQ
            for sweep in (1, 0):
                d = sweep
                P.op("pool", lambda e: e.memset(S32[:], 0.0), writes=[("S32", q) for q in range(8)])
                P.op("pool", lambda e: e.memset(S16[:], 0.0), writes=[("S16", q) for q in range(8)])
                order = range(NCH - 1, -1, -1) if sweep == 1 else range(NCH)
                for n in order:
                    tc = tb + n * L
                    b2 = it % 2
                    it += 1
                    qc, qcr = Qc[b2], ("Qc", b2)
                    kn, knr = Kn[b2], ("Kn", b2)
                    vn, vnr = Vn[b2], ("Vn", b2)
                    P.dma("sp", lambda e, qc=qc, tc=tc: e.dma_start(out=qc[:], in_=fm_view(RQ, tc, L)), writes=[qcr])
                    P.dma("sp", lambda e, kn=kn, tc=tc: e.dma_start(out=kn[:], in_=RKN[tc:tc + L, :]), writes=[knr])
                    P.dma("sp", lambda e, vn=vn, tc=tc: e.dma_start(out=vn[:], in_=RV[tc:tc + L, :]), writes=[vnr])
                    if sweep == 0:
                        kc, kcr = Kc[b2], ("Kc", b2)
                        gn, gnr = Gn[b2], ("Gn", b2)
                        yn, ynr = Yn[b2], ("Yn", b2)
                        P.dma("sp", lambda e, kc=kc, tc=tc: e.dma_start(out=kc[:], in_=fm_view(RK, tc, L)), writes=[kcr])
                        P.dma("sp", lambda e, gn=gn, tc=tc: e.dma_start(out=gn[:], in_=RG[tc:tc + L, :]), writes=[gnr])
                        P.dma("sp", lambda e, yn=yn, tc=tc: e.dma_start(out=yn[:], in_=YB[tc:tc + L, :]), reads=[("YB", s, n)], writes=[ynr + (q,) for q in range(4)])
                        yt, ytr = ytok[b2], ("ytok", b2)
                    else:
                        yn, ynr = Yn[b2], ("Yn", b2)
                    for hh in range(4):
                        h2 = nh % 2
                        nh += 1
                        qdt, qdr = Qd[h2], ("Qd", h2)
                        kdt, kdr = Kd[h2], ("Kd", h2)
                        P.op("act", lambda e, kdt=kdt, kn=kn, hh=hh, d=d: e.activation(out=kdt[:], in_=kn[:, hh * 256:(hh + 1) * 256], func=AF.Copy, scale=kd[:, d, hh:hh + 1]),
                             reads=[knr, "kd"], writes=[kdr])
                        bankC, bCres = K.bank()
                        for dc in range(2):
                            P.op("pe", lambda e, bankC=bankC, qc=qc, dc=dc, hh=hh: e.matmul(bankC[:], lhsT=qc[:, 2 * hh + dc, :], rhs=S16[:, hh * 2 + dc, :],
                                                                                       start=(dc == 0), stop=(dc == 1)),
                                 reads=[qcr, ("S16", hh * 2 + dc)], writes=[bCres])
                        if sweep == 1:
                            P.op("act", lambda e, bankC=bankC, yn=yn, hh=hh: e.activation(out=yn[:, hh * 512:(hh + 1) * 512], in_=bankC[:], func=AF.Copy, scale=qcol[:, 1, hh:hh + 1]),
                                 reads=[bCres, "qcol"], writes=[ynr + (hh,)])
                        else:
                            bankS, bSres = K.bank()
                            for dc in range(2):
                                P.op("pe", lambda e, bankS=bankS, kc=kc, qc=qc, dc=dc, hh=hh: e.matmul(bankS[:, 0:L], lhsT=kc[:, 2 * hh + dc, :], rhs=qc[:, 2 * hh + dc, :],
                                                                                                  start=(dc == 0), stop=(dc == 1)),
                                     reads=[kcr, qcr], writes=[bSres])
                            stt_, strr = STt[h2], ("ST", h2)
                            P.op("dve", lambda e, stt_=stt_, bankS=bankS, hh=hh: e.tensor_tensor(out=stt_[:], in0=bankS[:, 0:L], in1=decT[:, hh, :], op=ALU.mult),
                                 reads=[bSres, ("decT", hh)], writes=[strr])
                            bankO, bOres = K.bank()
                            P.op("pe", lambda e, bankO=bankO, stt_=stt_, vn=vn, hh=hh: e.matmul(bankO[:], lhsT=stt_[:], rhs=vn[:, hh * 512:(hh + 1) * 512], start=True, stop=True),
                                 reads=[strr, vnr], writes=[bOres])
                            ot, otr = o32[h2], ("o32", h2)
                            P.op("dve", lambda e, ot=ot, bankC=bankC, yn=yn, hh=hh: e.scalar_tensor_tensor(out=ot[:], in0=bankC[:], scalar=qcol[:, 0, hh:hh + 1],
                                                                                                          in1=yn[:, hh * 512:(hh + 1) * 512], op0=ALU.mult, op1=ALU.add),
                                 reads=[bCres, ynr + (hh,), "qcol"], writes=[otr])
                            P.op("dve", lambda e, ot=ot, bankO=bankO: e.tensor_tensor(out=ot[:], in0=bankO[:], in1=ot[:], op=ALU.add),
                                 reads=[bOres, otr], writes=[otr])
                            P.op("act", lambda e, ot=ot, h2=h2: e.activation(out=junk[:], in_=ot[:], func=AF.Square, accum_out=hst[:, h2, 0:1]),
                                 reads=[otr], writes=["junk", ("hss", h2)])
                            P.op("act", lambda e, h2=h2: e.activation(out=hst[:, h2, 1:2], in_=hst[:, h2, 0:1], func=AF.Sqrt, bias=EPS, scale=1.0 / 512),
                                 reads=[("hss", h2)], writes=[("hrs", h2)])
                            P.op("dve", lambda e, h2=h2: e.reciprocal(out=hst[:, h2, 1:2], in_=hst[:, h2, 1:2]), reads=[("hrs", h2)], writes=[("hrs", h2)])
                            P.op("dve", lambda e, ot=ot, yt=yt, gn=gn, hh=hh, h2=h2: e.scalar_tensor_tensor(out=yt[:, hh * 512:(hh + 1) * 512], in0=ot[:], scalar=hst[:, h2, 1:2],
                                                                                                           in1=gn[:, hh * 512:(hh + 1) * 512], op0=ALU.mult, op1=ALU.mult),
                                 reads=[otr, ("hrs", h2), gnr], writes=[ytr + (hh,)])
                        for dc in range(2):
                            bankK, bKres = K.bank()
                            q = hh * 2 + dc
                            P.op("pe", lambda e, bankK=bankK, kdt=kdt, vn=vn, dc=dc, hh=hh: e.matmul(bankK[:], lhsT=kdt[:, dc * 128:(dc + 1) * 128], rhs=vn[:, hh * 512:(hh + 1) * 512],
                                                                                                 start=True, stop=True),
                                 reads=[kdr, vnr], writes=[bKres])
                            P.op("dve", lambda e, bankK=bankK, q=q, hh=hh, d=d: e.scalar_tensor_tensor(out=S32[:, q, :], in0=S32[:, q, :], scalar=cd[:, d, hh:hh + 1], in1=bankK[:],
                                                                                                      op0=ALU.mult, op1=ALU.add),
                                 reads=[bKres, ("S32", q), "cd"], writes=[("S32", q)])
                            P.op("act", lambda e, q=q: e.activation(out=S16[:, q, :], in_=S32[:, q, :], func=AF.Copy), reads=[("S32", q)], writes=[("S16", q)])
                    if sweep == 1:
                        P.dma("pool", lambda e, yn=yn, tc=tc: e.dma_start(out=YB[tc:tc + L, :], in_=yn[:]),
                              reads=[ynr + (q,) for q in range(4)], writes=[("YB", s, n)])
                    else:
                        ytile, ytiler = yT[(n // 4) % 2], ("yT", (n // 4) % 2)
                        for half in range(2):
                            for f8 in range(8):
                                f = half * 8 + f8
                                P.op("pe", lambda e, yt=yt, f=f, f8=f8: e.transpose(out=K.bankb[:, f8 * 128:(f8 + 1) * 128], in_=yt[:, f * 128:(f + 1) * 128], identity=ident_b[:]),
                                     reads=[ytr + (f // 4,), "c_identb"], writes=[("psb", 0)])
                            K.evac(half, ytile[:, half * 8:(half + 1) * 8, (n % 4) * L:(n % 4 + 1) * L], K.bankb[:].rearrange("p (f t) -> p f t", f=8),
                                   [("psb", 0)], [ytiler + (n % 4, half)])
                        if n % 4 == 3:
                            c0 = tb + (n - 3) * L
                            P.dma("pool", lambda e, ytile=ytile, c0=c0: e.dma_start(out=fm_view(RY, c0, 512), in_=ytile[:]),
                                  reads=[ytiler + (q, hf) for q in range(4) for hf in range(2)], writes=[("RY", c0)])
    K.barrier()


PHASES["B0"] = phase_ret_in
PHASES["B1"] = phase_ret


def phase_s5(K):
    P = K.P
    PI = float(np.pi)
    with ExitStack() as st:
        sb = K.sballoc(st)
        T = sb("T", [128, 32, 128], BF16)
        GS = sb("GS", [128, 32, 2, 128], BF16)
        H = sb("H", [128, 2, 32, 2, 128], BF16)
        A1 = sb("A1", [128, 2, 2, 16], F32)
        Bm = sb("Bm", [128, 2, 2, 16], F32)
        NSEG, LS = 16, 32
        ALs = sb("ALs", [128, 2, 2, 16], F32)
        BLs = sb("BLs", [128, 2, 2, 16], F32)
        PA = sb("PA", [128, 2, 2, 16, LS], BF16)
        PB = sb("PB", [128, 2, 2, 16, LS], BF16)
        ident = K.load_const(sb, "c_ident")
        bglu = K.load_vec(sb, K.I("s5_b_glu")[0], 512, "b_glu")
        with ExitStack() as st2:
            sb2 = K.sballoc(st2)
            wglu = K.load_w(sb, K.I("s5_w_glu")[0], 512, 512, "w_glu", CB=512, stg_sb=sb2)
            ld = lambda n: K.load_const(sb2, n)
            lre, lim, ldt, bre, bim, cre, cim, dtl = [ld(n) for n in ("l_lre", "l_lim", "l_ldt", "l_bre", "l_bim", "l_cre", "l_cim", "l_d")]
            maskF, maskB = ld("c_maskF"), ld("c_maskB")
            allc = ["l_lre", "l_lim", "l_ldt", "l_bre", "l_bim", "l_cre", "l_cim", "l_d", "c_maskF", "c_maskB", "c_ident"]
            first = [True]

            def D(fn, eng="dve"):
                P.op(eng, fn, reads=["prep"] + (allc if first[0] else []), writes=["prep"])
                first[0] = False

            def v(name, shape=(128, 32), dt=F32):
                return sb2(name, list(shape), dt)
            dt_, xr, xi, mag, imag = v("dt"), v("xr"), v("xi"), v("mag"), v("imag")
            sn, cs, q, r, m = v("sn"), v("cs"), v("q"), v("r"), v("m")
            qi = v("qi", dt=I32)
            D(lambda e: e.activation(out=dt_[:], in_=ldt[:], func=AF.Exp), "act")
            D(lambda e: e.tensor_tensor(out=xr[:], in0=lre[:], in1=dt_[:], op=ALU.mult))
            D(lambda e: e.tensor_tensor(out=xi[:], in0=lim[:], in1=dt_[:], op=ALU.mult))
            D(lambda e: e.activation(out=mag[:], in_=xr[:], func=AF.Exp), "act")
            D(lambda e: e.activation(out=imag[:], in_=xr[:], func=AF.Exp, scale=-1.0), "act")
            for off, dst in ((0.0, sn), (PI / 2, cs)):
                D(lambda e, off=off: e.tensor_scalar(out=r[:], in0=xi[:], scalar1=off, scalar2=None, op0=ALU.add))
                D(lambda e: e.tensor_scalar(out=q[:], in0=r[:], scalar1=1.0 / (2 * PI), scalar2=0.5, op0=ALU.mult, op1=ALU.add))
                D(lambda e: e.tensor_copy(out=qi[:], in_=q[:]))
                D(lambda e: e.tensor_copy(out=q[:], in_=qi[:]))
                D(lambda e: e.scalar_tensor_tensor(out=r[:], in0=q[:], scalar=-2 * PI, in1=r[:], op0=ALU.mult, op1=ALU.add))
                D(lambda e: e.tensor_scalar(out=m[:], in0=r[:], scalar1=-PI, scalar2=2 * PI, op0=ALU.is_lt, op1=ALU.mult))
                D(lambda e: e.tensor_tensor(out=r[:], in0=r[:], in1=m[:], op=ALU.add))
                D(lambda e: e.tensor_scalar(out=m[:], in0=r[:], scalar1=PI, scalar2=-2 * PI, op0=ALU.is_gt, op1=ALU.mult))
                D(lambda e: e.tensor_tensor(out=r[:], in0=r[:], in1=m[:], op=ALU.add))
                D(lambda e: e.tensor_scalar(out=r[:], in0=r[:], scalar1=-3.1415925, scalar2=3.1415925, op0=ALU.max, op1=ALU.min))
                D(lambda e, dst=dst: e.activation(out=dst[:], in_=r[:], func=AF.Sin), "act")
            pwr, pwi, ipr, ipi = [v(n, (128, 32, 9)) for n in ("pwr", "pwi", "ipr", "ipi")]
            t1, t2 = v("t1"), v("t2")
            for (pr_, pi_, mg, sgn) in ((pwr, pwi, mag, 1.0), (ipr, ipi, imag, -1.0)):
                D(lambda e, pr_=pr_: e.memset(pr_[:, :, 0:1], 1.0))
                D(lambda e, pi_=pi_: e.memset(pi_[:, :, 0:1], 0.0))
                D(lambda e, pr_=pr_, mg=mg: e.tensor_tensor(out=pr_[:, :, 1], in0=mg[:], in1=cs[:], op=ALU.mult))
                D(lambda e, pi_=pi_, mg=mg, sgn=sgn: e.scalar_tensor_tensor(out=pi_[:, :, 1], in0=mg[:], scalar=sgn, in1=sn[:], op0=ALU.mult, op1=ALU.mult))
                for k in range(2, 9):
                    D(lambda e, pr_=pr_, k=k: e.tensor_tensor(out=t1[:], in0=pr_[:, :, k - 1], in1=pr_[:, :, 1], op=ALU.mult))
                    D(lambda e, pi_=pi_, k=k: e.tensor_tensor(out=t2[:], in0=pi_[:, :, k - 1], in1=pi_[:, :, 1], op=ALU.mult))
                    D(lambda e, pr_=pr_, k=k: e.tensor_tensor(out=pr_[:, :, k], in0=t1[:], in1=t2[:], op=ALU.subtract))
                    D(lambda e, pr_=pr_, pi_=pi_, k=k: e.tensor_tensor(out=t1[:], in0=pr_[:, :, k - 1], in1=pi_[:, :, 1], op=ALU.mult))
                    D(lambda e, pr_=pr_, pi_=pi_, k=k: e.tensor_tensor(out=t2[:], in0=pi_[:, :, k - 1], in1=pr_[:, :, 1], op=ALU.mult))
                    D(lambda e, pi_=pi_, k=k: e.tensor_tensor(out=pi_[:, :, k], in0=t1[:], in1=t2[:], op=ALU.add))
            for gh in range(2):
                for ri in range(2):
                    D(lambda e, gh=gh, ri=ri: e.tensor_copy(out=A1[:, gh, ri, :], in_=pwr[:, gh * 16:(gh + 1) * 16, 8]))
                    D(lambda e, gh=gh, ri=ri: e.tensor_scalar(out=Bm[:, gh, ri, :], in0=pwi[:, gh * 16:(gh + 1) * 16, 8], scalar1=(-1.0 if ri == 0 else 1.0),
                                                               scalar2=None, op0=ALU.mult))
            ur, ui = v("ur"), v("ui")
            D(lambda e: e.tensor_copy(out=ur[:], in_=pwr[:, :, 8]))
            D(lambda e: e.tensor_copy(out=ui[:], in_=pwi[:, :, 8]))
            g2 = lambda a, sl: a[sl].rearrange("p (a b) -> p a b", a=2)
            for k in range(LS):
                for ri in range(2):
                    P.op("pool", lambda e, k=k, ri=ri: e.tensor_copy(out=PA[:, :, ri, :, k], in_=ur[:].rearrange("p (a b) -> p a b", a=2)),
                         reads=["prep"], writes=[("PAB", k, ri, 0)])
                    P.op("pool", lambda e, k=k, ri=ri: e.tensor_scalar(out=PB[:, :, ri, :, k], in0=ui[:].rearrange("p (a b) -> p a b", a=2),
                                                                      scalar1=(-1.0 if ri == 0 else 1.0), scalar2=None, op0=ALU.mult),
                         reads=["prep"], writes=[("PAB", k, ri, 1)])
                if k == LS - 1:
                    for ri in range(2):
                        D(lambda e, ri=ri: e.tensor_copy(out=ALs[:, :, ri, :], in_=ur[:].rearrange("p (a b) -> p a b", a=2)))
                        D(lambda e, ri=ri: e.tensor_scalar(out=BLs[:, :, ri, :], in0=ui[:].rearrange("p (a b) -> p a b", a=2), scalar1=(-1.0 if ri == 0 else 1.0),
                                                           scalar2=None, op0=ALU.mult))
                else:
                    D(lambda e: e.tensor_tensor(out=t1[:], in0=ur[:], in1=pwr[:, :, 8], op=ALU.mult))
                    D(lambda e: e.tensor_tensor(out=t2[:], in0=ui[:], in1=pwi[:, :, 8], op=ALU.mult))
                    D(lambda e: e.tensor_tensor(out=t1[:], in0=t1[:], in1=t2[:], op=ALU.subtract))
                    D(lambda e: e.tensor_tensor(out=t2[:], in0=ur[:], in1=pwi[:, :, 8], op=ALU.mult))
                    D(lambda e: e.tensor_tensor(out=ui[:], in0=ui[:], in1=pwr[:, :, 8], op=ALU.mult))
                    D(lambda e: e.tensor_tensor(out=ui[:], in0=ui[:], in1=t2[:], op=ALU.add))
                    D(lambda e: e.tensor_copy(out=ur[:], in_=t1[:]))
            nr, den, c_r, c_i = v("nr"), v("den"), v("c_r"), v("c_i")
            D(lambda e: e.tensor_scalar(out=nr[:], in0=pwr[:, :, 1], scalar1=-1.0, scalar2=None, op0=ALU.add))
            D(lambda e: e.tensor_tensor(out=t1[:], in0=lre[:], in1=lre[:], op=ALU.mult))
            D(lambda e: e.tensor_tensor(out=t2[:], in0=lim[:], in1=lim[:], op=ALU.mult))
            D(lambda e: e.tensor_tensor(out=den[:], in0=t1[:], in1=t2[:], op=ALU.add))
            D(lambda e: e.reciprocal(out=den[:], in_=den[:]))
            D(lambda e: e.tensor_tensor(out=t1[:], in0=nr[:], in1=lre[:], op=ALU.mult))
            D(lambda e: e.tensor_tensor(out=t2[:], in0=pwi[:, :, 1], in1=lim[:], op=ALU.mult))
            D(lambda e: e.tensor_tensor(out=t1[:], in0=t1[:], in1=t2[:], op=ALU.add))
            D(lambda e: e.tensor_tensor(out=c_r[:], in0=t1[:], in1=den[:], op=ALU.mult))
            D(lambda e: e.tensor_tensor(out=t1[:], in0=pwi[:, :, 1], in1=lre[:], op=ALU.mult))
            D(lambda e: e.tensor_tensor(out=t2[:], in0=nr[:], in1=lim[:], op=ALU.mult))
            D(lambda e: e.tensor_tensor(out=t1[:], in0=t1[:], in1=t2[:], op=ALU.subtract))
            D(lambda e: e.tensor_tensor(out=c_i[:], in0=t1[:], in1=den[:], op=ALU.mult))
            big = lambda n: v(n, (128, 32, 128))
            Xr, Xi, Hr, Hi, tmp = big("Xr"), big("Xi"), big("Hr"), big("Hi"), big("tmpb")
            bbr, bbi = v("bbr", (128, 32, 16)), v("bbi", (128, 32, 16))
            tb16 = v("tb16", (128, 32, 16))

            def cmul(o_r, o_i, ar, ai, br, bi, tm, neg_i=False):
                D(lambda e: e.tensor_tensor(out=o_r, in0=ar, in1=br, op=ALU.mult))
                D(lambda e: e.tensor_tensor(out=tm, in0=ai, in1=bi, op=ALU.mult))
                D(lambda e: e.tensor_tensor(out=o_r, in0=o_r, in1=tm, op=ALU.subtract))
                D(lambda e: e.tensor_tensor(out=o_i, in0=ar, in1=bi, op=ALU.mult))
                D(lambda e: e.tensor_tensor(out=tm, in0=ai, in1=br, op=ALU.mult))
                D(lambda e: e.tensor_tensor(out=o_i, in0=o_i, in1=tm, op=ALU.add))
                if neg_i:
                    D(lambda e: e.tensor_scalar(out=o_i, in0=o_i, scalar1=-1.0, scalar2=None, op0=ALU.mult))
            b16 = lambda a: a[:].unsqueeze(2).broadcast_to([128, 32, 16])
            cmul(bbr[:], bbi[:], b16(c_r), b16(c_i), bre[:], bim[:], tb16[:])
            sel = {n: v(n, (128, 32, 8)) for n in ("gGr", "gGi", "gHr", "gHi", "gIr", "gIi")}
            for j in range(8):
                for (dst_r, dst_i, sr, si, kf, kb) in (("gGr", "gGi", pwr, pwi, 7 - j, j), ("gHr", "gHi", pwr, pwi, j + 1, 8 - j),
                                                       ("gIr", "gIi", ipr, ipi, j + 1, 8 - j)):
                    for dname, src in ((dst_r, sr), (dst_i, si)):
                        D(lambda e, dname=dname, src=src, kf=kf, j=j: e.tensor_copy(out=sel[dname][0:64, :, j], in_=src[0:64, :, kf]))
                        D(lambda e, dname=dname, src=src, kb=kb, j=j: e.tensor_copy(out=sel[dname][64:128, :, j], in_=src[64:128, :, kb]))
            X4 = lambda a: a[:].rearrange("p g (j h) -> p g j h", j=8)
            pj = lambda a: a[:].unsqueeze(3).broadcast_to([128, 32, 8, 16])
            ph = lambda a: a[:].unsqueeze(2).broadcast_to([128, 32, 8, 16])
            cmul(X4(Xr), X4(Xi), pj(sel["gGr"]), pj(sel["gGi"]), ph(bbr), ph(bbi), X4(tmp))
            for g in range(32):
                bank, bres = K.bank()
                for ri, X in enumerate((Xr, Xi)):
                    P.op("pe", lambda e, bank=bank, X=X, g=g, ri=ri: e.transpose(out=bank[:, ri * 128:(ri + 1) * 128], in_=X[:, g, :], identity=ident[:]),
                         reads=["prep", "c_ident"], writes=[bres])
                K.evac(g, GS[:, g, :, :], bank[:, 0:256].rearrange("p (r m) -> p r m", r=2), [bres], [("GS", g)])
            P._add("dve", None, [("GS", g) for g in range(32)], ["prep"], False)
            cmul(X4(Hr), X4(Hi), pj(sel["gHr"]), pj(sel["gHi"]), ph(cre), ph(cim), X4(tmp), neg_i=True)
            D(lambda e: e.memset(H[:], 0.0), "pool")
            for d in range(2):
                sl = slice(d * 64, (d + 1) * 64)
                D(lambda e, sl=sl, d=d: e.tensor_copy(out=H[sl, d, :, 0, :], in_=Hr[sl]))
                D(lambda e, sl=sl, d=d: e.tensor_copy(out=H[sl, d, :, 1, :], in_=Hi[sl]))
            cmul(X4(Xr), X4(Xi), pj(sel["gIr"]), pj(sel["gIi"]), ph(bbr), ph(bbi), X4(tmp))
            tt1 = v("tt1", (128, 128))
            tt2 = v("tt2", (128, 128))
            for g in range(32):
                bk = []
                for d in range(2):
                    bank, bres = K.bank()
                    sl = slice(d * 64, (d + 1) * 64)
                    P.op("pe", lambda e, bank=bank, sl=sl, g=g: e.matmul(bank[:, 0:128], lhsT=Xr[sl, g, :], rhs=Hr[sl, g, :], start=True, stop=False),
                         reads=["prep"], writes=[bres])
                    P.op("pe", lambda e, bank=bank, sl=sl, g=g: e.matmul(bank[:, 0:128], lhsT=Xi[sl, g, :], rhs=Hi[sl, g, :], start=False, stop=True),
                         reads=["prep"], writes=[bres])
                    bk.append((bank, bres))
                P.op("dve", lambda e, b=bk[0][0]: e.tensor_tensor(out=tt1[:], in0=b[:, 0:128], in1=maskF[:], op=ALU.mult), reads=[bk[0][1], "prep"], writes=["tt1"])
                P.op("dve", lambda e, b=bk[1][0]: e.tensor_tensor(out=tt2[:], in0=b[:, 0:128], in1=maskB[:], op=ALU.mult), reads=[bk[1][1], "prep"], writes=["tt2"])
                P.op("dve", lambda e: e.tensor_tensor(out=tt1[:], in0=tt1[:], in1=tt2[:], op=ALU.add), reads=["tt1", "tt2"], writes=["tt1"])
                P.op("dve", lambda e, g=g: e.scalar_tensor_tensor(out=T[:, g, :], in0=ident[:], scalar=dtl[:, g:g + 1], in1=tt1[:], op0=ALU.mult, op1=ALU.add),
                     reads=["tt1", "prep", "c_ident"], writes=[("T", g)])
        K.barrier()
        if S5_CUT == 1:
            return
        SEL, SELT = K.load_const(sb, "c_sel"), K.load_const(sb, "c_selT")
        UTs = sb("UTs", [128, 4, SEQ], BF16)
        U = sb("U", [128, 16, 512], BF16)
        Z = sb("Z", [128, 2, 16, 512], BF16)
        YG = sb("YG", [128, 8, 512], BF16)
        stt = sb("st", [128, 2, 16, NSEG], F32)
        tA = sb("tA", [128, 2, 16, LS], F32)
        tB = sb("tB", [128, 2, 16, LS], F32)
        Ec = sb("Ec", [128, 2, 16, NSEG], F32)
        Esw = sb("Esw", [128, 2, 16, NSEG], F32)
        sg = [sb("sg%d" % i, [128, 512], F32) for i in range(1)]
        bo = [sb("bo%d" % i, [128, 2, 512], BF16) for i in range(1)]
        UT, BT = K.S("UT"), K.S("BT")
        zres = [("Z", ri, gi) for ri in range(2) for gi in range(16)]
        for s in range(NSEQ):
            tb = s * SEQ
            P.dma("sp", lambda e, tb=tb: e.dma_start(out=UTs[:], in_=fm_view(UT, tb, SEQ)), writes=[("UTs", fb) for fb in range(4)])
            for gh in range(2):
                for gi in range(16):
                    g = gh * 16 + gi
                    fb, gl = g // 8, g % 8
                    bank, bres = K.bank()
                    for j in range(8):
                        P.op("pe", lambda e, bank=bank, fb=fb, gl=gl, j=j: e.matmul(bank[:], lhsT=SEL[:, gl, j, :],
                                                                                  rhs=UTs[:, fb, :].rearrange("p (c j) -> p j c", j=8)[:, j, :],
                                                                                  start=(j == 0), stop=(j == 7)),
                             reads=[("UTs", fb), "c_sel"], writes=[bres])
                    K.evac(gi, U[:, gi, :], bank[:], [bres], [("U", gi)])
                if S5_CUT == 2:
                    continue
                for gi in range(16):
                    g = gh * 16 + gi
                    for ri in range(2):
                        bank, bres = K.bank()
                        P.op("pe", lambda e, bank=bank, g=g, ri=ri, gi=gi: e.matmul(bank[0:64, :], lhsT=GS[:, g, ri, 0:64], rhs=U[:, gi, :], start=True, stop=True,
                                                                                 tile_position=(0, 0)),
                             reads=[("GS", g), ("U", gi)], writes=[bres])
                        P.op("pe", lambda e, bank=bank, g=g, ri=ri, gi=gi: e.matmul(bank[64:128, :], lhsT=GS[:, g, ri, 64:128], rhs=U[:, gi, ::-1], start=True, stop=True,
                                                                                 tile_position=(0, 64)),
                             reads=[("GS", g), ("U", gi)], writes=[bres])
                        K.evac(ri, Z[:, ri, gi, :], bank[:], [bres], [("Z", ri, gi)])
                if S5_CUT == 3:
                    continue
                for d, eng in ((0, "dve"),):
                    sl = slice(0, 128)
                    zr = ("Zrec", d)
                    P._add(eng, None, zres, [zr], False)
                    Zv = Z[sl].rearrange("p r g (s m) -> p r g s m", m=LS)
                    bc = lambda a, sl=sl, gh=gh: a[sl, gh].unsqueeze(3).broadcast_to([128, 2, 16, NSEG])
                    r1, r2 = tA[sl, :, :, 0:NSEG], tB[sl, :, :, 0:NSEG]
                    P.op(eng, lambda e, sl=sl: e.memset(stt[sl], 0.0), writes=[("st", d)])
                    for step in range(min(S5_STEPS, LS)):
                        mcol = step if d == 0 else LS - 1 - step
                        P.op(eng, lambda e, sl=sl, r1=r1, bc=bc: e.tensor_tensor(out=r1, in0=bc(A1), in1=stt[sl], op=ALU.mult),
                             reads=[("st", d)], writes=[("r1", d)])
                        P.op(eng, lambda e, sl=sl, r2=r2, bc=bc: e.tensor_tensor(out=r2, in0=bc(Bm), in1=stt[sl, ::-1], op=ALU.mult),
                             reads=[("st", d)], writes=[("r2", d)])
                        P.op(eng, lambda e, r1=r1, r2=r2: e.tensor_tensor(out=r1, in0=r1, in1=r2, op=ALU.add),
                             reads=[("r1", d), ("r2", d)], writes=[("r1", d)])
                        P.op(eng, lambda e, sl=sl, r1=r1, Zv=Zv, mcol=mcol: e.tensor_tensor(out=stt[sl], in0=r1, in1=Zv[:, :, :, :, mcol], op=ALU.add),
                             reads=[("r1", d), zr], writes=[("st", d)])
                        P.op(eng, lambda e, sl=sl, Zv=Zv, mcol=mcol: e.tensor_copy(out=Zv[:, :, :, :, mcol], in_=stt[sl]), reads=[("st", d)], writes=[zr])
                    if S5_SUB < 2:
                        continue
                    mend = LS - 1 if d == 0 else 0
                    order = list(range(NSEG)) if d == 0 else list(range(NSEG - 1, -1, -1))
                    e1, e2 = tA[sl, :, :, 0], tB[sl, :, :, 0]
                    for n_, sg_ in enumerate(order):
                        if n_ == 0:
                            P.op(eng, lambda e, sl=sl, Zv=Zv, sg_=sg_, mend=mend: e.tensor_copy(out=Ec[sl, :, :, sg_], in_=Zv[:, :, :, sg_, mend]),
                                 reads=[zr], writes=[("Ec", d)])
                            continue
                        pv = order[n_ - 1]
                        P.op(eng, lambda e, sl=sl, e1=e1, pv=pv, gh=gh: e.tensor_tensor(out=e1, in0=ALs[sl, gh], in1=Ec[sl, :, :, pv], op=ALU.mult),
                             reads=[("Ec", d)], writes=[("r1", d)])
                        P.op(eng, lambda e, sl=sl, e2=e2, pv=pv, gh=gh: e.tensor_tensor(out=e2, in0=BLs[sl, gh], in1=Ec[sl, ::-1, :, pv], op=ALU.mult),
                             reads=[("Ec", d)], writes=[("r2", d)])
                        P.op(eng, lambda e, e1=e1, e2=e2: e.tensor_tensor(out=e1, in0=e1, in1=e2, op=ALU.add), reads=[("r1", d), ("r2", d)], writes=[("r1", d)])
                        P.op(eng, lambda e, sl=sl, e1=e1, Zv=Zv, sg_=sg_, mend=mend: e.tensor_tensor(out=Ec[sl, :, :, sg_], in0=e1, in1=Zv[:, :, :, sg_, mend], op=ALU.add),
                             reads=[("r1", d), zr, ("Ec", d)], writes=[("Ec", d)])
                    if S5_SUB < 3:
                        continue
                    f1, f2 = tA[sl], tB[sl]
                    for ri in range(2):
                        P.op(eng, lambda e, sl=sl, ri=ri: e.tensor_copy(out=Esw[sl, ri], in_=Ec[sl, 1 - ri]), reads=[("Ec", d)], writes=[("Esw", d)])
                    for sg_ in range(NSEG):
                        src = sg_ - 1 if d == 0 else sg_ + 1
                        if src < 0 or src >= NSEG:
                            continue
                        eb = lambda rev, src=src, sl=sl: (Esw[sl, :, :, src] if rev else Ec[sl, :, :, src]).unsqueeze(3).broadcast_to([128, 2, 16, LS])
                        P.op(eng, lambda e, sl=sl, f1=f1, eb=eb, gh=gh: e.tensor_tensor(out=f1, in0=PA[sl, gh], in1=eb(False), op=ALU.mult), reads=[("Ec", d)], writes=[("r1", d)])
                        P.op(eng, lambda e, sl=sl, f2=f2, eb=eb, gh=gh: e.tensor_tensor(out=f2, in0=PB[sl, gh], in1=eb(True), op=ALU.mult), reads=[("Esw", d)], writes=[("r2", d)])
                        P.op(eng, lambda e, f1=f1, f2=f2: e.tensor_tensor(out=f1, in0=f1, in1=f2, op=ALU.add), reads=[("r1", d), ("r2", d)], writes=[("r1", d)])
                        P.op(eng, lambda e, f1=f1, Zv=Zv, sg_=sg_: e.tensor_tensor(out=Zv[:, :, :, sg_, :], in0=Zv[:, :, :, sg_, :], in1=f1, op=ALU.add),
                             reads=[("r1", d), zr], writes=[zr])
                P._add("pe", None, [("Zrec", 0)], zres, False)
                if S5_CUT == 4:
                    continue
                for gi in range(16):
                    g = gh * 16 + gi
                    fb, gl = g // 8, g % 8
                    bank, bres = K.bank()
                    P.op("pe", lambda e, bank=bank, g=g, gi=gi: e.matmul(bank[:], lhsT=T[:, g, :], rhs=U[:, gi, :], start=True, stop=False),
                         reads=[("T", g), ("U", gi)], writes=[bres])
                    for d in range(2):
                        sl = slice(d * 64, (d + 1) * 64)
                        for ri in range(2):
                            if d == 0:
                                o_, z_ = bank[:, 1:512], Z[:, ri, gi, 0:511]
                            else:
                                o_, z_ = bank[:, 0:511], Z[:, ri, gi, 510::-1]
                            P.op("pe", lambda e, o_=o_, z_=z_, g=g, ri=ri, d=d: e.matmul(o_, lhsT=H[:, d, g, ri, :], rhs=z_, start=False, stop=(d == 1 and ri == 1)),
                                 reads=[("Z", ri, gi), "prep"], writes=[bres])
                    P.op("act", lambda e, bank=bank, gl=gl: e.activation(out=YG[:, gl, :], in_=bank[:], func=AF.Gelu), reads=[bres], writes=[("YG", gl)])
                    if gl == 7:
                        for i in range(8):
                            bank2, b2res = K.bank()
                            for gl2 in range(8):
                                P.op("pe", lambda e, bank2=bank2, gl2=gl2, i=i: e.matmul(bank2[:], lhsT=SELT[:, gl2, i, :], rhs=YG[:, gl2, :], start=(gl2 == 0), stop=(gl2 == 7)),
                                     reads=[("YG", gl2), "c_selT"], writes=[b2res])
                            K.evac(i, UTs[:, fb, :].rearrange("p (c j) -> p j c", j=8)[:, i, :], bank2[:], [b2res], [("UTs", fb)])
            for tt_ in range(SEQ // TT):
                c0 = tt_ * TT
                o, ores = bo[0], ("bo", 0)
                for oc in range(4):
                    bank, bres = K.bank()
                    for k in range(4):
                        P.op("pe", lambda e, bank=bank, k=k, oc=oc, c0=c0: e.matmul(bank[:], lhsT=wglu[:, k, oc * 128:(oc + 1) * 128], rhs=UTs[:, k, c0:c0 + TT],
                                                                                 start=(k == 0), stop=(k == 3)),
                             reads=["w_glu", ("UTs", k)], writes=[bres])
                    sgt, sgr = sg[0], ("sg", 0)
                    P.op("act", lambda e, sgt=sgt, bank=bank, oc=oc: e.activation(out=sgt[:], in_=bank[:], func=AF.Sigmoid, bias=bglu[:, oc:oc + 1], scale=1.0),
                         reads=[bres, "b_glu"], writes=[sgr])
                    P.op("dve", lambda e, sgt=sgt, o=o, oc=oc, c0=c0: e.tensor_tensor(out=o[:, oc % 2, :], in0=UTs[:, oc, c0:c0 + TT], in1=sgt[:], op=ALU.mult),
                         reads=[sgr, ("UTs", oc)], writes=[ores + (oc % 2,)])
                    if oc % 2 == 1:
                        P.dma("pool", lambda e, o=o, c0=c0, tb=tb, oc=oc: e.dma_start(out=fm_view(BT, tb + c0, TT)[:, oc - 1:oc + 1, :], in_=o[:]),
                              reads=[ores + (q,) for q in range(2)], writes=[("BT", tb + c0, oc)])
    K.barrier()


S5_STEPS = 512
S5_CUT = 0
S5_SUB = 3
PHASES["A2"] = phase_s5


ALL_PHASES = ["A0", "A1", "A2", "A3", "A4", "A5", "B0", "B1", "B3", "B4", "B5"]
LAUNCHES = [ALL_PHASES]


def kernel(**inputs):
    inp = {k: np.asarray(v) for k, v in inputs.items()}
    lay = host_layout(inp)
    x, p = inp["x"], inp["p"]
    core_inputs = []
    for c in range(NCORES):
        ci = {k: v for k, v in inp.items() if k not in ("x", "p")}
        ci.update(lay)
        ci["x"] = np.ascontiguousarray(x[NSEQ * c:NSEQ * (c + 1)].reshape(NTOK, D))
        ci["p"] = np.ascontiguousarray(p[:, NSEQ * c:NSEQ * (c + 1)].reshape(2, NTOK, 256))
        core_inputs.append(ci)
    res = run_launch(ALL_PHASES, core_inputs, ext_out=("OUT",))
    out = np.stack([np.asarray(r["OUT"]).reshape(NSEQ, SEQ, D) for r in res], axis=0).reshape(NCORES * NSEQ, SEQ, D)
    return out.astype(np.float32)
```

```python
import numpy as np
from contextlib import ExitStack
import concourse.bass as bass
import concourse.mybir as mybir
from concourse.bass_utils import run_bass_kernel_spmd

F32 = mybir.dt.float32
BF16 = mybir.dt.bfloat16
I32 = mybir.dt.int32
ALU = mybir.AluOpType
AF = mybir.ActivationFunctionType
AX = mybir.AxisListType

COMPUTE = ("pe", "act", "dve", "pool")
NDMA_SEMS = {"sp": 20, "pool": 10, "act": 6}


class Op:
    __slots__ = ("eng", "fn", "dma", "idx", "deps", "adeps", "inc", "semval", "sem", "waits", "eidx", "cost", "fin")


class Prog:
    def __init__(self, nc):
        self.nc = nc
        self.ops = []
        self.lastw = {}
        self.rd = {}
        self.per_eng = {e: [] for e in ("pe", "act", "dve", "pool", "sp")}

    def _add(self, eng, fn, reads, writes, dma):
        op = Op()
        op.eng, op.fn, op.dma = eng, fn, dma
        op.idx = len(self.ops)
        op.inc = False
        op.semval = None
        op.sem = None
        op.cost = None
        ad = {}
        for r in reads:
            d = self.lastw.get(r)
            if d is not None:
                ad[d.idx] = (d, True)
        for r in writes:
            d = self.lastw.get(r)
            if d is not None and d.idx not in ad:
                ad[d.idx] = (d, False)
            for d in self.rd.get(r, ()):
                if d.idx not in ad:
                    ad[d.idx] = (d, False)
        ad.pop(op.idx, None)
        op.adeps = ad
        for r in reads:
            self.rd.setdefault(r, []).append(op)
        for r in writes:
            self.lastw[r] = op
            self.rd[r] = []
        self.ops.append(op)
        op.eidx = len(self.per_eng[eng])
        self.per_eng[eng].append(op)
        return op

    def finalize_deps(self):
        for e, lst in self.per_eng.items():
            for i, op in enumerate(lst):
                op.eidx = i
        for op in self.ops:
            best = {}
            deps = []
            for d, raw in op.adeps.values():
                if d.dma:
                    deps.append(d)
                    continue
                if d.eng == op.eng and not op.dma:
                    if (not raw) or op.eng == "pe":
                        continue
                b = best.get(d.eng)
                if b is None or d.eidx > b.eidx:
                    best[d.eng] = d
            op.deps = deps + list(best.values())

    def op(self, eng, fn, reads=(), writes=()):
        return self._add(eng, fn, reads, writes, False)

    def dma(self, queue, fn, reads=(), writes=()):
        return self._add(queue, fn, reads, writes, True)

    def schedule(self, window=64):
        COST = {"pe": 0.25, "act": 0.6, "dve": 0.6, "pool": 0.9, "sp": 0.05}
        import heapq
        engs = list(self.per_eng)
        src = {e: self.per_eng[e] for e in engs}
        nxt = {e: 0 for e in engs}
        buf = {e: [] for e in engs}
        new = {e: [] for e in engs}
        free = {e: 0.0 for e in engs}
        for op in self.ops:
            op.fin = None
        remaining = len(self.ops)
        cand = {e: None for e in engs}
        dirty = set(engs)

        def refill(e):
            b, sl = buf[e], src[e]
            while len(b) < window and nxt[e] < len(sl):
                b.append(sl[nxt[e]])
                nxt[e] += 1

        def find(e):
            refill(e)
            best = None
            fe = free[e]
            n = 0
            for op in buf[e]:
                n += 1
                ok = True
                rdy = fe
                for d, raw in op.adeps.values():
                    f = d.fin
                    if f is None:
                        ok = False
                        break
                    if f > rdy and (raw or d.eng != e or d.dma):
                        rdy = f
                if op.fn is None:
                    if n == 1 and ok:
                        best = (rdy, op)
                    break
                if ok and (best is None or rdy < best[0]):
                    best = (rdy, op)
                    if rdy <= fe:
                        break
            return best

        while remaining:
            for e in dirty:
                cand[e] = find(e)
            dirty = set()
            be = None
            for e in engs:
                c = cand[e]
                if c is not None and (be is None or c[0] < cand[be][0]):
                    be = e
            if be is None:
                for e in engs:
                    cand[e] = find(e)
                    if cand[e] is not None:
                        be = e
                        break
                assert be is not None, "scheduler stuck"
            rdy, op = cand[be]
            c = COST[be] if not op.dma else 0.05
            op.fin = rdy + (c if not op.dma else 4.0)
            free[be] = rdy + c
            buf[be].remove(op)
            new[be].append(op)
            remaining -= 1
            dirty = set(engs) if True else {be}
        self.per_eng = new

    def emit(self, es):
        nc = self.nc
        if getattr(self, "do_schedule", True):
            self.schedule()
        self.finalize_deps()
        for op in self.ops:
            for d in op.deps:
                if not d.dma:
                    d.inc = True
        csem = {e: es.enter_context(nc.semaphore("s_" + e)) for e in COMPUTE}
        for e in COMPUTE:
            c = 0
            for op in self.per_eng[e]:
                if op.dma:
                    continue
                if op.inc:
                    c += 1
                    op.semval = c
                    op.sem = csem[e]
        dsems = {q: [es.enter_context(nc.semaphore("d_%s%d" % (q, i))) for i in range(n)]
                 for q, n in NDMA_SEMS.items()}
        dcur = {q: [0] * n for q, n in NDMA_SEMS.items()}
        dnext = {q: 0 for q in NDMA_SEMS}
        pre_wait = {}
        for op in [o for e in self.per_eng for o in self.per_eng[e]]:
            if op.dma:
                q = op.eng
                i = dnext[q]
                dnext[q] = (i + 1) % len(dsems[q])
                pre_wait[op.idx] = (dsems[q][i], dcur[q][i])
                dcur[q][i] += 16
                op.sem = dsems[q][i]
                op.semval = dcur[q][i]
        known = {e: {} for e in self.per_eng}
        for op in [o for e in self.per_eng for o in self.per_eng[e]]:
            k = known[op.eng]
            w = {}
            cand = [(d.sem, d.semval) for d in op.deps]
            if op.dma and pre_wait[op.idx][1] > 0:
                cand.append(pre_wait[op.idx])
            for sem, val in cand:
                key = id(sem)
                if k.get(key, (None, 0))[1] >= val:
                    continue
                if key in w and w[key][1] >= val:
                    continue
                w[key] = (sem, val)
            for key, sv in w.items():
                k[key] = sv
            op.waits = list(w.values())
        self.n_waits = sum(len(o.waits) for o in self.ops)
        block = es.enter_context(nc.Block())

        def run(engobj, lst):
            for op in lst:
                for sem, val in op.waits:
                    engobj.wait_ge(sem, val)
                if op.fn is None:
                    if op.inc:
                        engobj.nop().then_inc(op.sem, 1)
                    continue
                ins = op.fn(engobj)
                if op.dma:
                    ins.then_inc(op.sem, 16)
                elif op.inc:
                    ins.then_inc(op.sem, 1)

        @block.tensor
        def _(e):
            run(e, self.per_eng["pe"])

        @block.scalar
        def _(e):
            run(e, self.per_eng["act"])

        @block.vector
        def _(e):
            run(e, self.per_eng["dve"])

        @block.gpsimd
        def _(e):
            run(e, self.per_eng["pool"])
            for q in dsems:
                for s, v in zip(dsems[q], dcur[q]):
                    if v > 0:
                        e.wait_ge(s, v)

        @block.sync
        def _(e):
            run(e, self.per_eng["sp"])


NCORES = 8
SEQ = 4096
D = 1024
NSEQ = 2
NTOK = NSEQ * SEQ
TT = 512
NT = NTOK // TT
DFF = 2816
EPS = 1e-6

IN_SHAPES = {
    "x": ([NTOK, D], F32), "p": ([2, NTOK, 256], F32),
    "ab_norm": ([1, 1024], F32), "ab_w_in": ([1, 1024, 2048], F32), "na_rpb": ([1, 8, 15, 31], F32),
    "s5_lambda_re": ([1, 2, 32, 64], F32), "s5_lambda_im": ([1, 2, 32, 64], F32), "s5_log_dt": ([1, 2, 32], F32),
    "s5_b_re": ([1, 2, 32, 64, 16], F32), "s5_b_im": ([1, 2, 32, 64, 16], F32),
    "s5_c_re": ([1, 2, 32, 16, 64], F32), "s5_c_im": ([1, 2, 32, 16, 64], F32),
    "s5_d": ([1, 512], F32), "s5_w_glu": ([1, 512, 512], F32), "s5_b_glu": ([1, 512], F32),
    "ab_w_out": ([1, 1024, 1024], F32), "ret_norm": ([1, 1024], F32), "ret_w_in": ([1, 1024, 6144], F32),
    "ret_decay": ([1, 2, 4], F32), "ret_w_out": ([1, 2048, 1024], F32), "ffn_norm": ([2, 1024], F32),
    "ffn_w_up": ([2, 1024, 5632], F32), "ffn_conv_w": ([2, 3, 5632], F32), "ffn_conv_b": ([2, 5632], F32),
    "ffn_w_down": ([2, 2816, 1024], F32), "ple_norm": ([2, 1024], F32), "ple_w_gate": ([2, 1024, 1024], F32),
    "ple_w_proj": ([2, 256, 1024], F32), "final_norm": ([1024], F32),
    "c_ident": ([128, 128], F32), "c_identb": ([128, 128], BF16),
    "c_cos": ([128, SEQ], F32), "c_sin": ([128, SEQ], F32),
    "c_D1": ([128, 128], F32), "c_M1": ([128, 128], F32), "c_D2": ([128, 128], F32), "c_M2": ([128, 128], F32),
    "c_iota1": ([128, 128], F32), "c_iota2": ([128, 128], F32), "c_pidx": ([128, 2], F32),
    "c_sel": ([128, 8, 8, 128], BF16), "c_selT": ([128, 8, 8, 128], BF16), "c_maskF": ([128, 128], F32), "c_maskB": ([128, 128], F32),
    "l_lre": ([128, 32], F32), "l_lim": ([128, 32], F32), "l_ldt": ([128, 32], F32),
    "l_bre": ([128, 32, 16], F32), "l_bim": ([128, 32, 16], F32), "l_cre": ([128, 32, 16], F32), "l_cim": ([128, 32, 16], F32),
    "l_d": ([128, 32], F32),
}

SCR_SHAPES = {
    "XT0": ([1024, NTOK], F32), "QT": ([512, NTOK], BF16), "KT": ([512, NTOK], BF16), "VN": ([NTOK, 512], BF16),
    "UT": ([512, NTOK], BF16), "AT": ([512, NTOK], BF16), "BT": ([512, NTOK], BF16),
    "XT1": ([1024, NTOK], F32), "MT0": ([DFF, NTOK], BF16), "XT3": ([1024, NTOK], F32),
    "RQ": ([1024, NTOK], BF16), "RK": ([1024, NTOK], BF16), "RKN": ([NTOK, 1024], BF16),
    "RV": ([NTOK, 2048], BF16), "RG": ([NTOK, 2048], BF16), "YB": ([NTOK, 2048], F32), "RY": ([2048, NTOK], BF16),
    "XT4": ([1024, NTOK], F32), "MT1": ([DFF, NTOK], BF16), "OUT": ([NTOK, 1024], F32),
}


class KB:
    def __init__(self, ext_in, ext_out):
        self.nc = bass.Bass("TRN2", target_bir_lowering=False)
        self.P = Prog(self.nc)
        self.es = ExitStack()
        self.ext_in, self.ext_out = set(ext_in), set(ext_out)
        self._d = {}
        self.used_inputs = []
        self.nbank = 0
        nc = self.nc
        self.banks = [self.es.enter_context(nc.psum_tensor("psf%d" % i, [128, 512], F32)) for i in range(6)]
        self.bankbs = [self.es.enter_context(nc.psum_tensor("psb%d" % i, [128, 1024], BF16)) for i in range(2)]
        self.bankb = self.bankbs[0]
        self.uid = 0

    def I(self, name):
        if name not in self._d:
            shape, dt = IN_SHAPES[name]
            self._d[name] = self.nc.dram_tensor(name, list(shape), dt, kind="ExternalInput").ap()
            self.used_inputs.append(name)
        return self._d[name]

    def S(self, name):
        if name not in self._d:
            shape, dt = SCR_SHAPES[name]
            if name in self.ext_in:
                kind = "ExternalInput"
                self.used_inputs.append(name)
            elif name in self.ext_out:
                kind = "ExternalOutput"
            else:
                kind = "Internal"
            self._d[name] = self.nc.dram_tensor(name, list(shape), dt, kind=kind).ap()
        return self._d[name]

    def bank(self, lo=0, hi=5):
        i = lo + self.nbank % (hi - lo)
        self.nbank += 1
        return self.banks[i], ("ps", i)

    def barrier(self):
        P = self.P
        deps = []
        for e in COMPUTE:
            for op in reversed(P.per_eng[e]):
                if not op.dma:
                    deps.append(op)
                    break
        start = getattr(self, "_bar_start", 0)
        deps += [op for op in P.ops[start:] if op.dma]
        self._bar_start = len(P.ops)
        for e in ("pe", "act", "dve", "pool", "sp"):
            op = P._add(e, None, (), (), False)
            op.adeps = {d.idx: (d, True) for d in deps if not ((not d.dma) and d.eng == e)}

    def sballoc(self, stack):
        def sb(name, shape, dt):
            self.uid += 1
            return stack.enter_context(self.nc.sbuf_tensor("%s_%d" % (name, self.uid), list(shape), dt))
        return sb

    def load_w(self, sb, src, K, N, name, CB=2048, stg_sb=None):
        P = self.P
        nk = K // 128
        wt = sb(name, [128, nk, N], BF16)
        if getattr(self, "_stg_owner", None) is not sb:
            self._stg_owner = sb
            self._stg = [(stg_sb or sb)("stg%d" % i, [128, CB], F32) for i in range(3)]
            self._stg_i = 0
        stg = self._stg
        i = self._stg_i
        for k in range(nk):
            for c0 in range(0, N, CB):
                cn = min(CB, N - c0)
                s = stg[i % 3]
                sr = ("stg", i % 3)
                P.dma("sp", lambda e, s=s, k=k, c0=c0, cn=cn: e.dma_start(out=s[:, :cn], in_=src[k * 128:(k + 1) * 128, c0:c0 + cn]),
                      writes=[sr])
                eng = ("pool", "dve", "act")[i % 3]
                if eng == "act":
                    fn = lambda e, s=s, k=k, c0=c0, cn=cn: e.activation(out=wt[:, k, c0:c0 + cn], in_=s[:, :cn], func=AF.Copy)
                else:
                    fn = lambda e, s=s, k=k, c0=c0, cn=cn: e.tensor_copy(out=wt[:, k, c0:c0 + cn], in_=s[:, :cn])
                P.op(eng, fn, reads=[sr], writes=[(name, k, c0)])
                i += 1
        self._stg_i = i
        P._add("pe", None, [(name, k, c0) for k in range(nk) for c0 in range(0, N, CB)], [name], False)
        return wt

    def load_vec(self, sb, src, n, name):
        t = sb(name, [128, n // 128], F32)
        self.P.dma("sp", lambda e: e.dma_start(out=t[:], in_=src.rearrange("(k p) -> p k", p=128), allow_slow_non_contiguous=True),
                   writes=[name])
        return t

    def load_const(self, sb, cname, name=None):
        shape, dt = IN_SHAPES[cname]
        name = name or cname
        t = sb(name, shape, dt)
        src = self.I(cname)
        self.P.dma("sp", lambda e: e.dma_start(out=t[:], in_=src), writes=[name])
        return t

    def rmsnorm(self, xT, xres, g, gname, h, hname, ones, sqb, rstd, nk=8, n=TT):
        P = self.P
        bank, br = self.banks[5], ("ps", 5)
        for k in range(nk):
            s = sqb[k % 2]
            sr = ("sqb", k % 2)
            P.op("act", lambda e, s=s, k=k: e.activation(out=s[:, :n], in_=xT[:, k, :n], func=AF.Square), reads=[xres(k)], writes=[sr])
            P.op("pe", lambda e, s=s, k=k: e.matmul(bank[:, :n], lhsT=ones[:], rhs=s[:, :n], start=(k == 0), stop=(k == nk - 1)),
                 reads=[sr, "ones"], writes=[br])
        P.op("act", lambda e: e.activation(out=rstd[:, :n], in_=bank[:, :n], func=AF.Sqrt, bias=EPS, scale=1.0), reads=[br], writes=["rstd"])
        P.op("dve", lambda e: e.reciprocal(out=bank[:, :n], in_=rstd[:, :n]), reads=["rstd"], writes=[br])
        for k in range(nk):
            P.op("dve", lambda e, k=k: e.scalar_tensor_tensor(out=h[:, k, :n], in0=xT[:, k, :n], scalar=g[:, k:k + 1], in1=bank[:, :n],
                                                             op0=ALU.mult, op1=ALU.mult),
                 reads=[xres(k), gname, br], writes=[(hname, k)])

    def norm_consts(self, sb):
        ones = sb("ones", [128, 128], F32)
        self.P.op("pool", lambda e: e.memset(ones[:], 1.0 / 1024), writes=["ones"])
        sqb = [sb("sqb%d" % i, [128, TT], F32) for i in range(2)]
        rstd = sb("rstd", [128, TT], F32)
        return ones, sqb, rstd

    def evac(self, i, out, in_, reads, writes):
        if i % 2 == 0:
            self.P.op("act", lambda e: e.activation(out=out, in_=in_, func=AF.Copy), reads=reads, writes=writes)
        else:
            self.P.op("dve", lambda e: e.tensor_copy(out=out, in_=in_), reads=reads, writes=writes)


def fm_view(ap, c0, n):
    return ap.rearrange("(k p) t -> p k t", p=128)[:, :, c0:c0 + n]


def phase_A0(K):
    P, nc = K.P, K.nc
    with ExitStack() as st:
        sb = K.sballoc(st)
        w = K.load_w(sb, K.I("ab_w_in")[0], 1024, 2048, "w_in")
        g = K.load_vec(sb, K.I("ab_norm")[0], 1024, "g_ab")
        ident = K.load_const(sb, "c_ident")
        ones, sqb, rstd = K.norm_consts(sb)
        xt = [sb("xt%d" % i, [128, 4, 1024], F32) for i in range(2)]
        xT = sb("xT", [128, 8, TT], F32)
        h = sb("h", [128, 8, TT], BF16)
        ofm = [sb("ofm%d" % i, [128, 4, TT], BF16) for i in range(2)]
        otm = [sb("otm%d" % i, [128, 512], BF16) for i in range(2)]
        x = K.I("x")
        XT0, QT, KT, UT, VN = K.S("XT0"), K.S("QT"), K.S("KT"), K.S("UT"), K.S("VN")
        no = 0
        nv = 0
        for t in range(NT):
            c0 = t * TT
            buf = xt[t % 2]
            br_ = ("xt", t % 2)
            P.dma("sp", lambda e, buf=buf, c0=c0: e.dma_start(out=buf[:], in_=x[c0:c0 + TT, :].rearrange("(n p) d -> p n d", p=128)),
                  writes=[br_])
            for k in range(8):
                bank, bres = K.bank()
                for n in range(4):
                    P.op("pe", lambda e, bank=bank, buf=buf, n=n, k=k: e.transpose(out=bank[:, n * 128:(n + 1) * 128],
                                                                                  in_=buf[:, n, k * 128:(k + 1) * 128], identity=ident[:]),
                         reads=[br_, "c_ident"], writes=[bres])
                K.evac(k, xT[:, k, :], bank[:], [bres], [("xT", k)])
            P.dma("pool", lambda e, c0=c0: e.dma_start(out=fm_view(XT0, c0, TT), in_=xT[:]),
                  reads=[("xT", k) for k in range(8)], writes=[("XT0", t)])
            K.rmsnorm(xT, lambda k: ("xT", k), g, "g_ab", h, "h", ones, sqb, rstd)
            for dst, dn, col0 in ((QT, "QT", 0), (KT, "KT", 512), (UT, "UT", 1536)):
                o = ofm[no % 2]
                ores = ("ofm", no % 2)
                no += 1
                for oc in range(4):
                    bank, bres = K.bank()
                    for k in range(8):
                        P.op("pe", lambda e, bank=bank, k=k, cc=col0 + oc * 128: e.matmul(bank[:], lhsT=w[:, k, cc:cc + 128], rhs=h[:, k, :],
                                                                                        start=(k == 0), stop=(k == 7)),
                             reads=["w_in", ("h", k)], writes=[bres])
                    K.evac(oc, o[:, oc, :], bank[:], [bres], [ores + (oc,)])
                P.dma("pool", lambda e, dst=dst, o=o, c0=c0: e.dma_start(out=fm_view(dst, c0, TT), in_=o[:]),
                      reads=[ores + (oc,) for oc in range(4)], writes=[(dn, t)])
            for sub in range(4):
                bank, bres = K.bank()
                for k in range(8):
                    P.op("pe", lambda e, bank=bank, k=k, sub=sub: e.matmul(bank[:], lhsT=h[:, k, sub * 128:(sub + 1) * 128], rhs=w[:, k, 1024:1536],
                                                                         start=(k == 0), stop=(k == 7)),
                         reads=["w_in", ("h", k)], writes=[bres])
                o = otm[nv % 2]
                ores = ("otm", nv % 2)
                nv += 1
                K.evac(sub, o[:], bank[:], [bres], [ores])
                r0 = c0 + sub * 128
                P.dma("pool", lambda e, o=o, r0=r0: e.dma_start(out=VN[r0:r0 + 128, :], in_=o[:]), reads=[ores], writes=[("VN", t, sub)])
    K.barrier()


PHASES = {}
PHASE_IO = {}
PHASES["A0"] = phase_A0
PHASE_IO["A0"] = ((), ("XT0", "QT", "KT", "UT", "VN"))


def host_consts():
    import ml_dtypes
    c = {}
    c["c_ident"] = np.eye(128, dtype=np.float32)
    c["c_identb"] = np.eye(128, dtype=np.float32).astype(ml_dtypes.bfloat16)
    inv = (np.float32(10000.0) ** (-np.arange(128, dtype=np.float32) / np.float32(128))).astype(np.float32)
    ang = (np.arange(SEQ, dtype=np.float32)[None, :] * inv[:, None]).astype(np.float32)
    c["c_cos"] = np.cos(ang.astype(np.float64)).astype(np.float32)
    c["c_sin"] = np.sin(ang.astype(np.float64)).astype(np.float32)
    jj = np.arange(128, dtype=np.float32)[:, None]
    ii = np.arange(128, dtype=np.float32)[None, :]
    c["c_D1"] = np.maximum(ii - jj, 0).astype(np.float32)
    c["c_M1"] = (ii >= jj).astype(np.float32)
    c["c_D2"] = np.maximum(jj - ii, 0).astype(np.float32)
    c["c_M2"] = (jj > ii).astype(np.float32)
    c["c_iota1"] = np.broadcast_to(ii + 1, (128, 128)).astype(np.float32).copy()
    c["c_iota2"] = np.broadcast_to(128 - ii, (128, 128)).astype(np.float32).copy()
    c["c_pidx"] = np.concatenate([127 - jj, jj], axis=1).astype(np.float32)
    sel = np.zeros((128, 8, 8, 128), np.float32)
    for gl in range(8):
        for j in range(8):
            for h in range(16):
                sel[gl * 16 + h, gl, j, j * 16 + h] = 1.0
    c["c_sel"] = sel.astype(ml_dtypes.bfloat16)
    c["c_selT"] = np.ascontiguousarray(sel.transpose(3, 1, 2, 0)).astype(ml_dtypes.bfloat16)
    jq = (np.arange(128) // 16)[:, None]
    iq = (np.arange(128) // 16)[None, :]
    c["c_maskF"] = (iq >= jq).astype(np.float32)
    c["c_maskB"] = (jq >= iq).astype(np.float32)
    return c


def host_layout(inp):
    o = {}
    def dp(a):
        return np.ascontiguousarray(np.asarray(a).transpose(0, 2, 1).reshape(128, 32))
    o["l_lre"] = dp(inp["s5_lambda_re"][0])
    o["l_lim"] = dp(inp["s5_lambda_im"][0])
    o["l_ldt"] = np.ascontiguousarray(np.repeat(np.asarray(inp["s5_log_dt"][0])[:, None, :], 64, axis=1).reshape(128, 32))
    o["l_bre"] = np.ascontiguousarray(np.asarray(inp["s5_b_re"][0]).transpose(0, 2, 1, 3).reshape(128, 32, 16))
    o["l_bim"] = np.ascontiguousarray(np.asarray(inp["s5_b_im"][0]).transpose(0, 2, 1, 3).reshape(128, 32, 16))
    o["l_cre"] = np.ascontiguousarray(np.asarray(inp["s5_c_re"][0]).transpose(0, 3, 1, 2).reshape(128, 32, 16))
    o["l_cim"] = np.ascontiguousarray(np.asarray(inp["s5_c_im"][0]).transpose(0, 3, 1, 2).reshape(128, 32, 16))
    dd = np.asarray(inp["s5_d"][0]).reshape(32, 16)
    o["l_d"] = np.ascontiguousarray(np.tile(dd.T[None, :, :], (8, 1, 1)).reshape(128, 32))
    return o


def build(phases, ext_in=(), ext_out=()):
    K = KB(ext_in, ext_out)
    with K.es:
        for ph in phases:
            PHASES[ph](K)
        K.P.emit(K.es)
    return K


def run_launch(phases, core_inputs, ext_in=(), ext_out=()):
    K = build(phases, ext_in, ext_out)
    consts = host_consts()
    in_maps = []
    for ci in core_inputs:
        m = {}
        for n in K.used_inputs:
            m[n] = consts[n] if n in consts else ci[n]
        in_maps.append(m)
    res = run_bass_kernel_spmd(K.nc, in_maps, core_ids=list(range(len(core_inputs))))
    return res.results


def phase_outproj(K, srcs, w_ap, kdim, xin_name, xout_name):
    P = K.P
    nk = kdim // 128
    with ExitStack() as st:
        sb = K.sballoc(st)
        w = K.load_w(sb, w_ap, kdim, 1024, "w_o")
        ain = [sb("ain%d" % i, [128, nk, TT], BF16) for i in range(2)]
        xin = [sb("xin%d" % i, [128, 8, TT], F32) for i in range(2)]
        XI, XO = K.S(xin_name), K.S(xout_name)
        for t in range(NT):
            c0 = t * TT
            a, ar = ain[t % 2], ("ain", t % 2)
            xi, xr = xin[t % 2], ("xin", t % 2)
            k0 = 0
            for sname, nch in srcs:
                S_ = K.S(sname)
                P.dma("sp", lambda e, a=a, S_=S_, k0=k0, nch=nch, c0=c0: e.dma_start(out=a[:, k0:k0 + nch, :], in_=fm_view(S_, c0, TT)),
                      writes=[ar + (k,) for k in range(k0, k0 + nch)])
                k0 += nch
            P.dma("sp", lambda e, xi=xi, c0=c0: e.dma_start(out=xi[:], in_=fm_view(XI, c0, TT)), writes=[xr + (k,) for k in range(8)])
            for oc in range(8):
                bank, bres = K.bank()
                for k in range(nk):
                    P.op("pe", lambda e, bank=bank, k=k, oc=oc, a=a: e.matmul(bank[:], lhsT=w[:, k, oc * 128:(oc + 1) * 128], rhs=a[:, k, :],
                                                                            start=(k == 0), stop=(k == nk - 1)),
                         reads=["w_o", ar + (k,)], writes=[bres])
                P.op("dve", lambda e, bank=bank, xi=xi, oc=oc: e.tensor_tensor(out=xi[:, oc, :], in0=bank[:], in1=xi[:, oc, :], op=ALU.add),
                     reads=[bres, xr + (oc,)], writes=[xr + (oc,)])
            P.dma("pool", lambda e, xi=xi, c0=c0: e.dma_start(out=fm_view(XO, c0, TT), in_=xi[:]),
                  reads=[xr + (k,) for k in range(8)], writes=[(xout_name, t)])
    K.barrier()


def phase_ffn_up(K, layer, xin_name, mout_name):
    P = K.P
    W = TT + 1
    with ExitStack() as st:
        sb = K.sballoc(st)
        w = K.load_w(sb, K.I("ffn_w_up")[layer], 1024, 2 * DFF, "w_up")
        g = K.load_vec(sb, K.I("ffn_norm")[layer], 1024, "g_ffn")
        cb = K.load_vec(sb, K.I("ffn_conv_b")[layer], 2 * DFF, "cb")
        cw = sb("cw", [128, 3, 44], F32)
        cwsrc = K.I("ffn_conv_w")[layer]
        P.dma("sp", lambda e: e.dma_start(out=cw[:], in_=cwsrc.rearrange("j (k p) -> p j k", p=128), allow_slow_non_contiguous=True), writes=["cw"])
        ones, sqb, rstd = K.norm_consts(sb)
        xin = [sb("xin%d" % i, [128, 8, TT], F32) for i in range(2)]
        hb = [sb("h%d" % i, [128, 8, TT], BF16) for i in range(2)]
        halo = sb("halo", [128, 44, 2], F32)
        acc = {wh: [sb("acc%s%d" % (wh, i), [128, W], F32) for i in range(3)] for wh in "ag"}
        gel = [sb("gel%d" % i, [128, W], F32) for i in range(2)]
        mt = [sb("mt%d" % i, [128, 22, W + 1], BF16) for i in range(1)]
        XI, MO = K.S(xin_name), K.S(mout_name)
        tiles_per_seq = SEQ // TT
        P.dma("sp", lambda e: e.dma_start(out=xin[0][:], in_=fm_view(XI, 0, TT)), writes=[("xin", 0, k) for k in range(8)])
        K.rmsnorm(xin[0], lambda k: ("xin", 0, k), g, "g_ffn", hb[0], "h0", ones, sqb, rstd)
        for t in range(NT):
            c0 = t * TT
            first = (t % tiles_per_seq == 0)
            last = (t % tiles_per_seq == tiles_per_seq - 1)
            xi, xr = xin[t % 2], ("xin", t % 2)
            if t + 1 < NT:
                P.dma("sp", lambda e, c1=c0 + TT, xn=xin[(t + 1) % 2]: e.dma_start(out=xn[:], in_=fm_view(XI, c1, TT)),
                      writes=[("xin", (t + 1) % 2, k) for k in range(8)])
            h, hn = hb[t % 2], "h%d" % (t % 2)
            if t + 1 < NT:
                K.rmsnorm(xin[(t + 1) % 2], lambda k, b=(t + 1) % 2: ("xin", b, k), g, "g_ffn", hb[(t + 1) % 2], "h%d" % ((t + 1) % 2), ones, sqb, rstd)
            m_, mr = mt[0], ("mt", 0)
            for c in range(22):
                i = c % 3
                for wh, ch in (("a", c), ("g", c + 22)):
                    bank, bres = K.bank()
                    for k in range(8):
                        P.op("pe", lambda e, bank=bank, k=k, ch=ch, h=h: e.matmul(bank[:], lhsT=w[:, k, ch * 128:(ch + 1) * 128], rhs=h[:, k, :],
                                                                          start=(k == 0), stop=(k == 7)),
                             reads=["w_up", (hn, k)], writes=[bres])
                    ac, acr = acc[wh][i], ("acc", wh, i)
                    hres = ("halo", ch)
                    P.op("act", lambda e, ac=ac, bank=bank, ch=ch: e.activation(out=ac[:, 0:TT], in_=bank[:], func=AF.Identity, bias=cb[:, ch:ch + 1],
                                                                              scale=cw[:, 2, ch:ch + 1]),
                         reads=[bres, "cb", "cw"], writes=[acr])
                    if last:
                        P.op("act", lambda e, ac=ac, bank=bank, ch=ch: e.activation(out=ac[:, TT:W], in_=bank[:, TT - 1:TT], func=AF.Identity, bias=cb[:, ch:ch + 1],
                                                                                  scale=cw[:, 1, ch:ch + 1]),
                             reads=[bres, "cb", "cw"], writes=[acr])
                    P.op("dve", lambda e, ac=ac, bank=bank, ch=ch: e.scalar_tensor_tensor(out=ac[:, 1:TT], in0=bank[:, 0:TT - 1], scalar=cw[:, 1, ch:ch + 1], in1=ac[:, 1:TT],
                                                                                        op0=ALU.mult, op1=ALU.add), reads=[bres, "cw", acr], writes=[acr])
                    hi_ = W if last else TT
                    P.op("dve", lambda e, ac=ac, bank=bank, ch=ch, hi_=hi_: e.scalar_tensor_tensor(out=ac[:, 2:hi_], in0=bank[:, 0:hi_ - 2], scalar=cw[:, 0, ch:ch + 1],
                                                                                                 in1=ac[:, 2:hi_], op0=ALU.mult, op1=ALU.add),
                         reads=[bres, "cw", acr], writes=[acr])
                    if not first:
                        P.op("dve", lambda e, ac=ac, ch=ch: e.scalar_tensor_tensor(out=ac[:, 0:2], in0=halo[:, ch, :], scalar=cw[:, 0, ch:ch + 1], in1=ac[:, 0:2],
                                                                                   op0=ALU.mult, op1=ALU.add), reads=[hres, "cw", acr], writes=[acr])
                        P.op("dve", lambda e, ac=ac, ch=ch: e.scalar_tensor_tensor(out=ac[:, 0:1], in0=halo[:, ch, 1:2], scalar=cw[:, 1, ch:ch + 1], in1=ac[:, 0:1],
                                                                                   op0=ALU.mult, op1=ALU.add), reads=[hres, "cw", acr], writes=[acr])
                    if not last:
                        P.op("act", lambda e, bank=bank, ch=ch: e.activation(out=halo[:, ch, :], in_=bank[:, TT - 2:TT], func=AF.Copy), reads=[bres], writes=[hres])
                if last:
                    ge, ger = gel[c % 2], ("gel", c % 2)
                    P.op("act", lambda e, ge=ge, ac=acc["g"][i]: e.activation(out=ge[:], in_=ac[:], func=AF.Gelu), reads=[("acc", "g", i)], writes=[ger])
                    P.op("dve", lambda e, ge=ge, ac=acc["a"][i], c=c, m_=m_: e.tensor_tensor(out=m_[:, c, 0:W], in0=ac[:], in1=ge[:], op=ALU.mult),
                         reads=[("acc", "a", i), ger], writes=[mr + (c,)])
                else:
                    gp, gpr = K.bankbs[c % 2][:].bitcast(F32), ("psb", c % 2)
                    P.op("act", lambda e, gp=gp, ac=acc["g"][i]: e.activation(out=gp, in_=ac[:, 0:TT], func=AF.Gelu), reads=[("acc", "g", i)], writes=[gpr])
                    P.op("dve", lambda e, gp=gp, ac=acc["a"][i], c=c, m_=m_: e.tensor_tensor(out=m_[:, c, 0:TT], in0=ac[:, 0:TT], in1=gp, op=ALU.mult),
                         reads=[("acc", "a", i), gpr], writes=[mr + (c,)])
            lo = 1 if first else 0
            hi = W if last else TT
            P.dma("pool", lambda e, lo=lo, hi=hi, c0=c0, m_=m_: e.dma_start(out=MO.rearrange("(k p) t -> p k t", p=128)[:, :, c0 - 1 + lo:c0 - 1 + hi],
                                                                   in_=m_[:, :, lo:hi]),
                  reads=[mr + (c,) for c in range(22)], writes=[(mout_name, t)])
    K.barrier()


def phase_ffn_down(K, layer, min_name, xin_name, xout_name, final):
    P = K.P
    with ExitStack() as st:
        sb = K.sballoc(st)
        w = K.load_w(sb, K.I("ffn_w_down")[layer], DFF, 1024, "w_dn")
        wg = K.load_w(sb, K.I("ple_w_gate")[layer], 1024, 1024, "w_pg")
        wp = K.load_w(sb, K.I("ple_w_proj")[layer], 256, 1024, "w_pp")
        gp = K.load_vec(sb, K.I("ple_norm")[layer], 1024, "g_ple")
        if final:
            gf = K.load_vec(sb, K.I("final_norm"), 1024, "g_fin")
            hf = sb("hf", [128, 8, TT], F32)
            otok = [sb("otok%d" % i, [128, 1024], F32) for i in range(2)]
        ident = K.load_const(sb, "c_ident")
        ones, sqb, rstd = K.norm_consts(sb)
        min_ = [sb("min%d" % i, [128, 22, TT], BF16) for i in range(1)]
        xin = [sb("xin%d" % i, [128, 8, TT], F32) for i in range(1)]
        pin = [sb("pin%d" % i, [128, 4, 256], F32) for i in range(2)]
        pT = sb("pT", [128, 2, TT], BF16)
        h = sb("h", [128, 8, TT], BF16)
        sig = [sb("sig%d" % i, [128, TT], F32) for i in range(2)]
        MI, XI = K.S(min_name), K.S(xin_name)
        XO = K.S(xout_name)
        pd = K.I("p")[layer]
        no = 0
        for t in range(NT):
            c0 = t * TT
            m, mr = min_[0], ("min", 0)
            xi, xr = xin[0], ("xin", 0)
            pi, pr = pin[t % 2], ("pin", t % 2)
            P.dma("sp", lambda e, m=m, c0=c0: e.dma_start(out=m[:], in_=fm_view(MI, c0, TT)), writes=[mr])
            P.dma("sp", lambda e, xi=xi, c0=c0: e.dma_start(out=xi[:], in_=fm_view(XI, c0, TT)), writes=[xr + (k,) for k in range(8)])
            P.dma("sp", lambda e, pi=pi, c0=c0: e.dma_start(out=pi[:], in_=pd[c0:c0 + TT, :].rearrange("(n p) d -> p n d", p=128)), writes=[pr])
            for kc in range(2):
                bank, bres = K.bank()
                for n in range(4):
                    P.op("pe", lambda e, bank=bank, pi=pi, n=n, kc=kc: e.transpose(out=bank[:, n * 128:(n + 1) * 128],
                                                                                  in_=pi[:, n, kc * 128:(kc + 1) * 128], identity=ident[:]),
                         reads=[pr, "c_ident"], writes=[bres])
                K.evac(kc, pT[:, kc, :], bank[:], [bres], [("pT", kc)])
            for oc in range(8):
                bank, bres = K.bank()
                for k in range(22):
                    P.op("pe", lambda e, bank=bank, k=k, oc=oc, m=m: e.matmul(bank[:], lhsT=w[:, k, oc * 128:(oc + 1) * 128], rhs=m[:, k, :],
                                                                            start=(k == 0), stop=(k == 21)),
                         reads=["w_dn", mr], writes=[bres])
                P.op("dve", lambda e, bank=bank, xi=xi, oc=oc: e.tensor_tensor(out=xi[:, oc, :], in0=bank[:], in1=xi[:, oc, :], op=ALU.add),
                     reads=[bres, xr + (oc,)], writes=[xr + (oc,)])
            K.rmsnorm(xi, lambda k: xr + (k,), gp, "g_ple", h, "h", ones, sqb, rstd)
            for oc in range(8):
                bankg, bgres = K.bank()
                for k in range(8):
                    P.op("pe", lambda e, bankg=bankg, k=k, oc=oc: e.matmul(bankg[:], lhsT=wg[:, k, oc * 128:(oc + 1) * 128], rhs=h[:, k, :],
                                                                         start=(k == 0), stop=(k == 7)),
                         reads=["w_pg", ("h", k)], writes=[bgres])
                bankp, bpres = K.bank()
                for k in range(2):
                    P.op("pe", lambda e, bankp=bankp, k=k, oc=oc: e.matmul(bankp[:], lhsT=wp[:, k, oc * 128:(oc + 1) * 128], rhs=pT[:, k, :],
                                                                         start=(k == 0), stop=(k == 1)),
                         reads=["w_pp", ("pT", k)], writes=[bpres])
                sg, sgr = sig[oc % 2], ("sig", oc % 2)
                P.op("act", lambda e, sg=sg, bankg=bankg: e.activation(out=sg[:], in_=bankg[:], func=AF.Sigmoid), reads=[bgres], writes=[sgr])
                P.op("dve", lambda e, sg=sg, bankp=bankp: e.tensor_tensor(out=sg[:], in0=bankp[:], in1=sg[:], op=ALU.mult), reads=[bpres, sgr], writes=[sgr])
                P.op("dve", lambda e, sg=sg, xi=xi, oc=oc: e.tensor_tensor(out=xi[:, oc, :], in0=xi[:, oc, :], in1=sg[:], op=ALU.add),
                     reads=[sgr, xr + (oc,)], writes=[xr + (oc,)])
            if not final:
                P.dma("pool", lambda e, xi=xi, c0=c0: e.dma_start(out=fm_view(XO, c0, TT), in_=xi[:]),
                      reads=[xr + (k,) for k in range(8)], writes=[(xout_name, t)])
            else:
                K.rmsnorm(xi, lambda k: xr + (k,), gf, "g_fin", hf, "hf", ones, sqb, rstd)
                for n in range(4):
                    o, ores = otok[no % 2], ("otok", no % 2)
                    no += 1
                    for hb in range(2):
                        bank, bres = K.bank()
                        for kk in range(4):
                            k = hb * 4 + kk
                            P.op("pe", lambda e, bank=bank, k=k, kk=kk, n=n: e.transpose(out=bank[:, kk * 128:(kk + 1) * 128],
                                                                                          in_=hf[:, k, n * 128:(n + 1) * 128], identity=ident[:]),
                                 reads=[("hf", k), "c_ident"], writes=[bres])
                        K.evac(hb, o[:, hb * 512:(hb + 1) * 512], bank[:], [bres], [ores + (hb,)])
                    r0 = c0 + n * 128
                    P.dma("pool", lambda e, o=o, r0=r0: e.dma_start(out=XO[r0:r0 + 128, :], in_=o[:]),
                          reads=[ores + (0,), ores + (1,)], writes=[(xout_name, t, n)])
    K.barrier()


PHASES["A3"] = lambda K: phase_outproj(K, [("AT", 4), ("BT", 4)], K.I("ab_w_out")[0], 1024, "XT0", "XT1")
PHASES["A4"] = lambda K: phase_ffn_up(K, 0, "XT1", "MT0")
PHASES["A5"] = lambda K: phase_ffn_down(K, 0, "MT0", "XT1", "XT3", False)
PHASES["B3"] = lambda K: phase_outproj(K, [("RY", 16)], K.I("ret_w_out")[0], 2048, "XT3", "XT4")
PHASES["B4"] = lambda K: phase_ffn_up(K, 1, "XT4", "MT1")
PHASES["B5"] = lambda K: phase_ffn_down(K, 1, "MT1", "XT4", "OUT", True)


def phase_na(K):
    P = K.P
    NEG = -30000.0
    NB = 4
    with ExitStack() as st:
        sb = K.sballoc(st)
        ident = K.load_const(sb, "c_ident")
        identb = K.load_const(sb, "c_identb")
        KTs = sb("KTs", [64, 8, SEQ], BF16)
        Vs = sb("Vs", [64, 64, 512], BF16)
        Qb = [sb("Qb%d" % i, [64, 8, 512], BF16) for i in range(2)]
        Bf = sb("Bf", [128, 4 * 960], F32)
        sc = [sb("sc%d" % i, [128, 512], F32) for i in range(NB)]
        pr = [sb("pr%d" % i, [128, 512], BF16) for i in range(NB)]
        prT = [sb("prT%d" % i, [64, 8, 128], BF16) for i in range(3)]
        stt = sb("stt", [128, NB, 4], F32)
        aout = [sb("aout%d" % i, [128, 256], F32) for i in range(2)]
        aT = [sb("aT%d" % i, [64, 8, 512], BF16) for i in range(1)]
        QT, KT, VN, AT = K.S("QT"), K.S("KT"), K.S("VN"), K.S("AT")
        hv = lambda ap, c0, n: ap.rearrange("(h p) t -> p h t", p=64)[:, :, c0:c0 + n]
        rpb = K.I("na_rpb")[0]
        P.op("pool", lambda e: e.memset(Bf[:], NEG), writes=["Bf"])
        Bf4 = Bf[:].rearrange("p (h r c) -> p h r c", h=4, r=15)
        for hp in range(2):
            for c in range(64):
                cs = min(max(c - 8, 0), 48)
                dcs = cs - c + 15
                P.dma("sp", lambda e, c=c, cs=cs, dcs=dcs, hp=hp: e.dma_start(out=Bf4[hp * 64 + c:hp * 64 + c + 1, :, :, cs:cs + 16],
                                                                             in_=rpb[hp::2, :, dcs:dcs + 16][None]),
                      reads=[], writes=["Bf"])
        cnt = 0
        nT = 0
        for s_ in range(NSEQ):
            tb = s_ * SEQ
            P.dma("sp", lambda e, tb=tb: e.dma_start(out=KTs[:], in_=hv(KT, tb, SEQ)), writes=["KTs"])
            for rq in range(4):
                P.dma("sp", lambda e, tb=tb, rq=rq: e.dma_start(out=Vs[:, rq * 16:(rq + 1) * 16, :],
                                                              in_=VN[tb + rq * 1024:tb + (rq + 1) * 1024, :].rearrange("(r c) f -> c r f", c=64)),
                      writes=[("Vs", rq)])
            for r in range(64):
                rr = r % 8
                qb, qbr = Qb[(r // 8) % 2], ("Qb", (r // 8) % 2)
                if rr == 0:
                    P.dma("sp", lambda e, qb=qb, c0=tb + r * 64: e.dma_start(out=qb[:], in_=hv(QT, c0, 512)), writes=[qbr])
                rs = min(max(r - 4, 0), 56)
                dr0 = rs - r + 7
                bankO, bOres = K.bank()
                ao, aor = aout[r % 2], ("aout", r % 2)
                for pp in range(4):
                    i2 = cnt % NB
                    i3 = cnt % 3
                    ib = cnt % 2
                    cnt += 1
                    bankS, bSres = K.bank()
                    for hp in range(2):
                        hd = 2 * pp + hp
                        P.op("pe", lambda e, bankS=bankS, hd=hd, hp=hp, rr=rr, rs=rs, qb=qb: e.matmul(bankS[hp * 64:(hp + 1) * 64, :], lhsT=qb[:, hd, rr * 64:(rr + 1) * 64],
                                                                                                 rhs=KTs[:, hd, rs * 64:rs * 64 + 512], start=True, stop=True,
                                                                                                 tile_position=(0, hp * 64)),
                             reads=[qbr, "KTs"], writes=[bSres])
                    b0 = pp * 960 + dr0 * 64
                    P.op("dve", lambda e, bankS=bankS, i2=i2, b0=b0: e.scalar_tensor_tensor(out=sc[i2][:], in0=bankS[:], scalar=0.125, in1=Bf[:, b0:b0 + 512],
                                                                                          op0=ALU.mult, op1=ALU.add),
                         reads=[bSres, "Bf"], writes=[("sc", i2)])
                    P.op("dve", lambda e, i2=i2: e.tensor_reduce(out=stt[:, i2, 0:1], in_=sc[i2][:], axis=AX.X, op=ALU.max, negate=True),
                         reads=[("sc", i2)], writes=[("nmx", i2)])
                    P.op("act", lambda e, i2=i2: e.activation(out=pr[i2][:], in_=sc[i2][:], func=AF.Exp, bias=stt[:, i2, 0:1], scale=1.0,
                                                             accum_out=stt[:, i2, 1:2]),
                         reads=[("sc", i2), ("nmx", i2)], writes=[("pr", i2), ("rsum", i2)])
                    bb = K.bankbs[ib]
                    for i in range(8):
                        P.op("pe", lambda e, i=i, i2=i2, bb=bb: e.transpose(out=bb[0:64, i * 128:(i + 1) * 128], in_=pr[i2][:, i * 64:(i + 1) * 64], identity=identb[:]),
                             reads=[("pr", i2), "c_identb"], writes=[("psb", ib)])
                    K.evac(cnt, prT[i3][:], bb[0:64, :].rearrange("p (i q) -> p i q", i=8), [("psb", ib)], [("prT", i3)])
                    for hp in range(2):
                        hd = 2 * pp + hp
                        for i in range(8):
                            P.op("pe", lambda e, bankO=bankO, i=i, i3=i3, hd=hd, hp=hp, pp=pp, rs=rs: e.matmul(bankO[hp * 64:(hp + 1) * 64, pp * 64:(pp + 1) * 64],
                                                                                                          lhsT=prT[i3][:, i, hp * 64:(hp + 1) * 64],
                                                                                                          rhs=Vs[:, rs + i, hd * 64:(hd + 1) * 64], start=(i == 0), stop=(i == 7),
                                                                                                          tile_position=(0, hp * 64)),
                                 reads=[("prT", i3), ("Vs", (rs + i) // 16)], writes=[bOres])
                    P.op("dve", lambda e, i2=i2: e.reciprocal(out=stt[:, i2, 2:3], in_=stt[:, i2, 1:2]), reads=[("rsum", i2)], writes=[("rinv", i2)])
                    P.op("act", lambda e, bankO=bankO, ao=ao, pp=pp, i2=i2: e.activation(out=ao[:, pp * 64:(pp + 1) * 64], in_=bankO[:, pp * 64:(pp + 1) * 64],
                                                                                        func=AF.Copy, scale=stt[:, i2, 2:3]),
                         reads=[bOres, ("rinv", i2)], writes=[aor])
                a, ar = aT[0], ("aT", 0)
                bankT, bTres = K.bank()
                for pp in range(4):
                    P.op("pe", lambda e, bankT=bankT, ao=ao, pp=pp: e.transpose(out=bankT[0:64, pp * 128:(pp + 1) * 128], in_=ao[:, pp * 64:(pp + 1) * 64], identity=ident[:]),
                         reads=[aor, "c_ident"], writes=[bTres])
                K.evac(r, a[:, :, rr * 64:(rr + 1) * 64], bankT[0:64, :].rearrange("p (h q) -> p h q", h=8), [bTres], [ar + (rr,)])
                if rr == 7:
                    c0 = tb + (r - 7) * 64
                    P.dma("pool", lambda e, a=a, c0=c0: e.dma_start(out=hv(AT, c0, 512), in_=a[:]),
                          reads=[ar + (q,) for q in range(8)], writes=[("AT", c0)])
    K.barrier()


PHASES["A1"] = phase_na


def phase_ret_in(K):
    P = K.P
    with ExitStack() as st:
        sb = K.sballoc(st)
        w = K.load_w(sb, K.I("ret_w_in")[0], 1024, 6144, "w_ri")
        g = K.load_vec(sb, K.I("ret_norm")[0], 1024, "g_ret")
        identb = K.load_const(sb, "c_identb")
        ones, sqb, rstd = K.norm_consts(sb)
        xin = sb("xin", [128, 8, TT], F32)
        h = sb("h", [128, 8, TT], BF16)
        cs_ = [sb("cos%d" % i, [128, TT], F32) for i in range(2)]
        sn_ = [sb("sin%d" % i, [128, TT], F32) for i in range(2)]
        t1 = [sb("t1_%d" % i, [128, TT], F32) for i in range(2)]
        t2 = [sb("t2_%d" % i, [128, TT], F32) for i in range(2)]
        qk = [sb("qk%d" % i, [128, 8, TT], BF16) for i in range(2)]
        ktok = [sb("ktok%d" % i, [128, 1024], BF16) for i in range(2)]
        vtok = [sb("vtok%d" % i, [128, 2048], BF16) for i in range(2)]
        XI = K.S("XT3")
        RQ, RK, RKN, RV, RG = K.S("RQ"), K.S("RK"), K.S("RKN"), K.S("RV"), K.S("RG")
        ccos, csin = K.I("c_cos"), K.I("c_sin")
        nrot = 0
        nvt = 0
        nkt = 0
        for t in range(NT):
            c0 = t * TT
            pos0 = c0 % SEQ
            cs, sn = cs_[t % 2], sn_[t % 2]
            P.dma("sp", lambda e, c0=c0: e.dma_start(out=xin[:], in_=fm_view(XI, c0, TT)), writes=[("xin", k) for k in range(8)])
            P.dma("sp", lambda e, cs=cs, pos0=pos0: e.dma_start(out=cs[:], in_=ccos[:, pos0:pos0 + TT]), writes=[("cos", t % 2)])
            P.dma("sp", lambda e, sn=sn, pos0=pos0: e.dma_start(out=sn[:], in_=csin[:, pos0:pos0 + TT]), writes=[("sin", t % 2)])
            K.rmsnorm(xin, lambda k: ("xin", k), g, "g_ret", h, "h", ones, sqb, rstd)
            for qi, (dst, dn, scl) in enumerate(((RQ, "RQ", 1.0), (RK, "RK", 0.0625))):
                o, ores = qk[qi], ("qk", qi)
                for hh in range(4):
                    bk = []
                    for half in range(2):
                        bank, bres = K.bank()
                        cc = qi * 1024 + hh * 256 + half * 128
                        for k in range(8):
                            P.op("pe", lambda e, bank=bank, k=k, cc=cc: e.matmul(bank[:], lhsT=w[:, k, cc:cc + 128], rhs=h[:, k, :], start=(k == 0), stop=(k == 7)),
                                 reads=["w_ri", ("h", k)], writes=[bres])
                        bk.append((bank, bres))
                    (b1, b1r), (b2, b2r) = bk
                    for half, (A, B, op) in enumerate((((b1, b1r, cs, ("cos", t % 2)), (b2, b2r, sn, ("sin", t % 2)), ALU.subtract),
                                                       ((b1, b1r, sn, ("sin", t % 2)), (b2, b2r, cs, ("cos", t % 2)), ALU.add))):
                        i2 = nrot % 2
                        nrot += 1
                        P.op("dve", lambda e, A=A, i2=i2, scl=scl: e.scalar_tensor_tensor(out=t1[i2][:], in0=A[0][:], scalar=scl, in1=A[2][:], op0=ALU.mult, op1=ALU.mult),
                             reads=[A[1], A[3]], writes=[("t1", i2)])
                        P.op("dve", lambda e, B=B, i2=i2, scl=scl: e.scalar_tensor_tensor(out=t2[i2][:], in0=B[0][:], scalar=scl, in1=B[2][:], op0=ALU.mult, op1=ALU.mult),
                             reads=[B[1], B[3]], writes=[("t2", i2)])
                        P.op("pool", lambda e, i2=i2, o=o, hh=hh, half=half, op=op: e.tensor_tensor(out=o[:, hh * 2 + half, :], in0=t1[i2][:], in1=t2[i2][:], op=op),
                             reads=[("t1", i2), ("t2", i2)], writes=[ores + (hh * 2 + half,)])
                P.dma("pool", lambda e, dst=dst, o=o, c0=c0: e.dma_start(out=fm_view(dst, c0, TT), in_=o[:]),
                      reads=[ores + (k,) for k in range(8)], writes=[(dn, t)])
            for sub in range(4):
                for k in range(8):
                    P.op("pe", lambda e, k=k, sub=sub: e.transpose(out=K.bankb[:, k * 128:(k + 1) * 128], in_=qk[1][:, k, sub * 128:(sub + 1) * 128], identity=identb[:]),
                         reads=[("qk", 1, k), "c_identb"], writes=[("psb", 0)])
                kt, ktr = ktok[nkt % 2], ("ktok", nkt % 2)
                nkt += 1
                K.evac(sub, kt[:], K.bankb[:], [("psb", 0)], [ktr])
                r0 = c0 + sub * 128
                P.dma("pool", lambda e, kt=kt, r0=r0: e.dma_start(out=RKN[r0:r0 + 128, :], in_=kt[:]), reads=[ktr], writes=[("RKN", r0)])
            for which, (dst, dn, colb) in enumerate(((RV, "RV", 2048), (RG, "RG", 4096))):
                for sub in range(4):
                    vt, vtr = vtok[nvt % 2], ("vtok", nvt % 2)
                    nvt += 1
                    for cbk in range(4):
                        bank, bres = K.bank()
                        cc = colb + cbk * 512
                        for k in range(8):
                            P.op("pe", lambda e, bank=bank, k=k, sub=sub, cc=cc: e.matmul(bank[:], lhsT=h[:, k, sub * 128:(sub + 1) * 128], rhs=w[:, k, cc:cc + 512],
                                                                                     start=(k == 0), stop=(k == 7)),
                                 reads=["w_ri", ("h", k)], writes=[bres])
                        if which == 0:
                            K.evac(cbk, vt[:, cbk * 512:(cbk + 1) * 512], bank[:], [bres], [vtr + (cbk,)])
                        else:
                            P.op("act", lambda e, vt=vt, bank=bank, cbk=cbk: e.activation(out=vt[:, cbk * 512:(cbk + 1) * 512], in_=bank[:], func=AF.Silu),
                                 reads=[bres], writes=[vtr + (cbk,)])
                    r0 = c0 + sub * 128
                    P.dma("pool", lambda e, dst=dst, vt=vt, r0=r0: e.dma_start(out=dst[r0:r0 + 128, :], in_=vt[:]),
                          reads=[vtr + (q,) for q in range(4)], writes=[(dn, r0)])
    K.barrier()


def phase_ret(K):
    P = K.P
    L = 128
    NCH = SEQ // L
    with ExitStack() as st:
        sb = K.sballoc(st)
        ident_b = K.load_const(sb, "c_identb")
        cD1, cM1, cD2, cM2 = [K.load_const(sb, n) for n in ("c_D1", "c_M1", "c_D2", "c_M2")]
        io1, io2, pidx = [K.load_const(sb, n) for n in ("c_iota1", "c_iota2", "c_pidx")]
        lg = sb("lg", [128, 8], F32)
        dsrc = K.I("ret_decay")[0].rearrange("a b -> (a b)").partition_broadcast(128)
        P.dma("sp", lambda e: e.dma_start(out=lg[:], in_=dsrc), writes=["lg"])
        P.op("act", lambda e: e.activation(out=lg[:], in_=lg[:], func=AF.Exp), reads=["lg"], writes=["lg"])
        P.op("dve", lambda e: e.tensor_scalar(out=lg[:], in0=lg[:], scalar1=-1.0, scalar2=None, op0=ALU.mult), reads=["lg"], writes=["lg"])
        decT = sb("decT", [128, 4, 128], F32)
        tmpd = sb("tmpd", [128, 128], F32)
        qd = sb("qd", [128, 2, 4, 128], F32)
        kd = sb("kd", [128, 2, 4], F32)
        cd = sb("cd", [128, 2, 4], F32)
        for hh in range(4):
            lf, lb = lg[:, hh:hh + 1], lg[:, 4 + hh:5 + hh]
            P.op("act", lambda e, hh=hh, lf=lf: e.activation(out=decT[:, hh, :], in_=cD1[:], func=AF.Exp, scale=lf), reads=["lg", "c_D1"], writes=[("decT", hh)])
            P.op("dve", lambda e, hh=hh: e.tensor_tensor(out=decT[:, hh, :], in0=decT[:, hh, :], in1=cM1[:], op=ALU.mult), reads=[("decT", hh), "c_M1"], writes=[("decT", hh)])
            P.op("act", lambda e, lb=lb: e.activation(out=tmpd[:], in_=cD2[:], func=AF.Exp, scale=lb), reads=["lg", "c_D2"], writes=["tmpd"])
            P.op("dve", lambda e: e.tensor_tensor(out=tmpd[:], in0=tmpd[:], in1=cM2[:], op=ALU.mult), reads=["tmpd", "c_M2"], writes=["tmpd"])
            P.op("dve", lambda e, hh=hh: e.tensor_tensor(out=decT[:, hh, :], in0=decT[:, hh, :], in1=tmpd[:], op=ALU.add), reads=[("decT", hh), "tmpd"], writes=[("decT", hh)])
            P.op("act", lambda e, hh=hh, lf=lf: e.activation(out=qd[:, 0, hh, :], in_=io1[:], func=AF.Exp, scale=lf), reads=["lg", "c_iota1"], writes=["qd"])
            P.op("act", lambda e, hh=hh, lb=lb: e.activation(out=qd[:, 1, hh, :], in_=io2[:], func=AF.Exp, scale=lb), reads=["lg", "c_iota2"], writes=["qd"])
            P.op("act", lambda e, hh=hh, lf=lf: e.activation(out=kd[:, 0, hh:hh + 1], in_=pidx[:, 0:1], func=AF.Exp, scale=lf), reads=["lg", "c_pidx"], writes=["kd"])
            P.op("act", lambda e, hh=hh, lb=lb: e.activation(out=kd[:, 1, hh:hh + 1], in_=pidx[:, 1:2], func=AF.Exp, scale=lb), reads=["lg", "c_pidx"], writes=["kd"])
            P.op("act", lambda e, hh=hh: e.activation(out=cd[:, 0, hh:hh + 1], in_=lg[:, hh:hh + 1], func=AF.Exp, scale=float(L)), reads=["lg"], writes=["cd"])
            P.op("act", lambda e, hh=hh: e.activation(out=cd[:, 1, hh:hh + 1], in_=lg[:, 4 + hh:5 + hh], func=AF.Exp, scale=float(L)), reads=["lg"], writes=["cd"])
        qcol = sb("qcol", [128, 2, 4], F32)
        for hh in range(4):
            P.op("act", lambda e, hh=hh: e.activation(out=qcol[:, 0, hh:hh + 1], in_=pidx[:, 1:2], func=AF.Exp, scale=lg[:, hh:hh + 1], bias=lg[:, hh:hh + 1]),
                 reads=["lg", "c_pidx"], writes=["qcol"])
            P.op("act", lambda e, hh=hh: e.activation(out=qcol[:, 1, hh:hh + 1], in_=pidx[:, 0:1], func=AF.Exp, scale=lg[:, 4 + hh:5 + hh], bias=lg[:, 4 + hh:5 + hh]),
                 reads=["lg", "c_pidx"], writes=["qcol"])
        S32 = sb("S32", [128, 8, 512], F32)
        S16 = sb("S16", [128, 8, 512], BF16)
        Qc = [sb("Qc%d" % i, [128, 8, L], BF16) for i in range(2)]
        Kc = [sb("Kc%d" % i, [128, 8, L], BF16) for i in range(2)]
        Kn = [sb("Kn%d" % i, [128, 1024], BF16) for i in range(2)]
        Vn = [sb("Vn%d" % i, [128, 2048], BF16) for i in range(2)]
        Gn = [sb("Gn%d" % i, [128, 2048], BF16) for i in range(2)]
        Yn = [sb("Yn%d" % i, [128, 2048], F32) for i in range(2)]
        Qd = [sb("Qd%d" % i, [128, 2, L], BF16) for i in range(2)]
        Kd = [sb("Kd%d" % i, [128, 256], BF16) for i in range(2)]
        STt = [sb("ST%d" % i, [128, L], BF16) for i in range(2)]
        o32 = [sb("o32_%d" % i, [128, 512], F32) for i in range(2)]
        junk = sb("junk", [128, 512], F32)
        hst = sb("hst", [128, 2, 2], F32)
        ytok = [sb("ytok%d" % i, [128, 2048], BF16) for i in range(2)]
        yT = [sb("yT%d" % i, [128, 16, 512], BF16) for i in range(2)]
        RQ, RK, RKN, RV, RG, YB, RY = [K.S(n) for n in ("RQ", "RK", "RKN", "RV", "RG", "YB", "RY")]
        it = 0
        nh = 0
        for s in range(NSEQ):
            tb = s * SEQ
            for sweep in (1, 0):
                d = sweep
                P.op("pool", lambda e: e.memset(S32[:], 0.0), writes=[("S32", q) for q in range(8)])
                P.op("pool", lambda e: e.memset(S16[:], 0.0), writes=[("S16", q) for q in range(8)])
                order = range(NCH - 1, -1, -1) if sweep == 1 else range(NCH)
                for n in order:
                    tc = tb + n * L
                    b2 = it % 2
                    it += 1
                    qc, qcr = Qc[b2], ("Qc", b2)
                    kn, knr = Kn[b2], ("Kn", b2)
                    vn, vnr = Vn[b2], ("Vn", b2)
                    P.dma("sp", lambda e, qc=qc, tc=tc: e.dma_start(out=qc[:], in_=fm_view(RQ, tc, L)), writes=[qcr])
                    P.dma("sp", lambda e, kn=kn, tc=tc: e.dma_start(out=kn[:], in_=RKN[tc:tc + L, :]), writes=[knr])
                    P.dma("sp", lambda e, vn=vn, tc=tc: e.dma_start(out=vn[:], in_=RV[tc:tc + L, :]), writes=[vnr])
                    if sweep == 0:
                        kc, kcr = Kc[b2], ("Kc", b2)
                        gn, gnr = Gn[b2], ("Gn", b2)
                        yn, ynr = Yn[b2], ("Yn", b2)
                        P.dma("sp", lambda e, kc=kc, tc=tc: e.dma_start(out=kc[:], in_=fm_view(RK, tc, L)), writes=[kcr])
                        P.dma("sp", lambda e, gn=gn, tc=tc: e.dma_start(out=gn[:], in_=RG[tc:tc + L, :]), writes=[gnr])
                        P.dma("sp", lambda e, yn=yn, tc=tc: e.dma_start(out=yn[:], in_=YB[tc:tc + L, :]), reads=[("YB", s, n)], writes=[ynr + (q,) for q in range(4)])
                        yt, ytr = ytok[b2], ("ytok", b2)
                    else:
                        yn, ynr = Yn[b2], ("Yn", b2)
                    for hh in range(4):
                        h2 = nh % 2
                        nh += 1
                        qdt, qdr = Qd[h2], ("Qd", h2)
                        kdt, kdr = Kd[h2], ("Kd", h2)
                        P.op("act", lambda e, kdt=kdt, kn=kn, hh=hh, d=d: e.activation(out=kdt[:], in_=kn[:, hh * 256:(hh + 1) * 256], func=AF.Copy, scale=kd[:, d, hh:hh + 1]),
                             reads=[knr, "kd"], writes=[kdr])
                        bankC, bCres = K.bank()
                        for dc in range(2):
                            P.op("pe", lambda e, bankC=bankC, qc=qc, dc=dc, hh=hh: e.matmul(bankC[:], lhsT=qc[:, 2 * hh + dc, :], rhs=S16[:, hh * 2 + dc, :],
                                                                                       start=(dc == 0), stop=(dc == 1)),
                                 reads=[qcr, ("S16", hh * 2 + dc)], writes=[bCres])
                        if sweep == 1:
                            P.op("act", lambda e, bankC=bankC, yn=yn, hh=hh: e.activation(out=yn[:, hh * 512:(hh + 1) * 512], in_=bankC[:], func=AF.Copy, scale=qcol[:, 1, hh:hh + 1]),
                                 reads=[bCres, "qcol"], writes=[ynr + (hh,)])
                        else:
                            bankS, bSres = K.bank()
                            for dc in range(2):
                                P.op("pe", lambda e, bankS=bankS, kc=kc, qc=qc, dc=dc, hh=hh: e.matmul(bankS[:, 0:L], lhsT=kc[:, 2 * hh + dc, :], rhs=qc[:, 2 * hh + dc, :],
                                                                                                  start=(dc == 0), stop=(dc == 1)),
                                     reads=[kcr, qcr], writes=[bSres])
                            stt_, strr = STt[h2], ("ST", h2)
                            P.op("dve", lambda e, stt_=stt_, bankS=bankS, hh=hh: e.tensor_tensor(out=stt_[:], in0=bankS[:, 0:L], in1=decT[:, hh, :], op=ALU.mult),
                                 reads=[bSres, ("decT", hh)], writes=[strr])
                            bankO, bOres = K.bank()
                            P.op("pe", lambda e, bankO=bankO, stt_=stt_, vn=vn, hh=hh: e.matmul(bankO[:], lhsT=stt_[:], rhs=vn[:, hh * 512:(hh + 1) * 512], start=True, stop=True),
                                 reads=[strr, vnr], writes=[bOres])
                            ot, otr = o32[h2], ("o32", h2)
                            P.op("dve", lambda e, ot=ot, bankC=bankC, yn=yn, hh=hh: e.scalar_tensor_tensor(out=ot[:], in0=bankC[:], scalar=qcol[:, 0, hh:hh + 1],
                                                                                                          in1=yn[:, hh * 512:(hh + 1) * 512], op0=ALU.mult, op1=ALU.add),
                                 reads=[bCres, ynr + (hh,), "qcol"], writes=[otr])
                            P.op("dve", lambda e, ot=ot, bankO=bankO: e.tensor_tensor(out=ot[:], in0=bankO[:], in1=ot[:], op=ALU.add),
                                 reads=[bOres, otr], writes=[otr])
                            P.op("act", lambda e, ot=ot, h2=h2: e.activation(out=junk[:], in_=ot[:], func=AF.Square, accum_out=hst[:, h2, 0:1]),
                                 reads=[otr], writes=["junk", ("hss", h2)])
                            P.op("act", lambda e, h2=h2: e.activation(out=hst[:, h2, 1:2], in_=hst[:, h2, 0:1], func=AF.Sqrt, bias=EPS, scale=1.0 / 512),
                                 reads=[("hss", h2)], writes=[("hrs", h2)])
                            P.op("dve", lambda e, h2=h2: e.reciprocal(out=hst[:, h2, 1:2], in_=hst[:, h2, 1:2]), reads=[("hrs", h2)], writes=[("hrs", h2)])
                            P.op("dve", lambda e, ot=ot, yt=yt, gn=gn, hh=hh, h2=h2: e.scalar_tensor_tensor(out=yt[:, hh * 512:(hh + 1) * 512], in0=ot[:], scalar=hst[:, h2, 1:2],
                                                                                                           in1=gn[:, hh * 512:(hh + 1) * 512], op0=ALU.mult, op1=ALU.mult),
                                 reads=[otr, ("hrs", h2), gnr], writes=[ytr + (hh,)])
                        for dc in range(2):
                            bankK, bKres = K.bank()
                            q = hh * 2 + dc
                            P.op("pe", lambda e, bankK=bankK, kdt=kdt, vn=vn, dc=dc, hh=hh: e.matmul(bankK[:], lhsT=kdt[:, dc * 128:(dc + 1) * 128], rhs=vn[:, hh * 512:(hh + 1) * 512],
                                                                                                 start=True, stop=True),
                                 reads=[kdr, vnr], writes=[bKres])
                            P.op("dve", lambda e, bankK=bankK, q=q, hh=hh, d=d: e.scalar_tensor_tensor(out=S32[:, q, :], in0=S32[:, q, :], scalar=cd[:, d, hh:hh + 1], in1=bankK[:],
                                                                                                      op0=ALU.mult, op1=ALU.add),
                                 reads=[bKres, ("S32", q), "cd"], writes=[("S32", q)])
                            P.op("act", lambda e, q=q: e.activation(out=S16[:, q, :], in_=S32[:, q, :], func=AF.Copy), reads=[("S32", q)], writes=[("S16", q)])
                    if sweep == 1:
                        P.dma("pool", lambda e, yn=yn, tc=tc: e.dma_start(out=YB[tc:tc + L, :], in_=yn[:]),
                              reads=[ynr + (q,) for q in range(4)], writes=[("YB", s, n)])
                    else:
                        ytile, ytiler = yT[(n // 4) % 2], ("yT", (n // 4) % 2)
                        for half in range(2):
                            for f8 in range(8):
                                f = half * 8 + f8
                                P.op("pe", lambda e, yt=yt, f=f, f8=f8: e.transpose(out=K.bankb[:, f8 * 128:(f8 + 1) * 128], in_=yt[:, f * 128:(f + 1) * 128], identity=ident_b[:]),
                                     reads=[ytr + (f // 4,), "c_identb"], writes=[("psb", 0)])
                            K.evac(half, ytile[:, half * 8:(half + 1) * 8, (n % 4) * L:(n % 4 + 1) * L], K.bankb[:].rearrange("p (f t) -> p f t", f=8),
                                   [("psb", 0)], [ytiler + (n % 4, half)])
                        if n % 4 == 3:
                            c0 = tb + (n - 3) * L
                            P.dma("pool", lambda e, ytile=ytile, c0=c0: e.dma_start(out=fm_view(RY, c0, 512), in_=ytile[:]),
                                  reads=[ytiler + (q, hf) for q in range(4) for hf in range(2)], writes=[("RY", c0)])
    K.barrier()


PHASES["B0"] = phase_ret_in
PHASES["B1"] = phase_ret


def phase_s5(K):
    P = K.P
    PI = float(np.pi)
    with ExitStack() as st:
        sb = K.sballoc(st)
        T = sb("T", [128, 32, 128], BF16)
        GS = sb("GS", [128, 32, 2, 128], BF16)
        H = sb("H", [128, 2, 32, 2, 128], BF16)
        A1 = sb("A1", [128, 2, 2, 16], F32)
        Bm = sb("Bm", [128, 2, 2, 16], F32)
        NSEG, LS = 16, 32
        ALs = sb("ALs", [128, 2, 2, 16], F32)
        BLs = sb("BLs", [128, 2, 2, 16], F32)
        PA = sb("PA", [128, 2, 2, 16, LS], BF16)
        PB = sb("PB", [128, 2, 2, 16, LS], BF16)
        ident = K.load_const(sb, "c_ident")
        bglu = K.load_vec(sb, K.I("s5_b_glu")[0], 512, "b_glu")
        with ExitStack() as st2:
            sb2 = K.sballoc(st2)
            wglu = K.load_w(sb, K.I("s5_w_glu")[0], 512, 512, "w_glu", CB=512, stg_sb=sb2)
            ld = lambda n: K.load_const(sb2, n)
            lre, lim, ldt, bre, bim, cre, cim, dtl = [ld(n) for n in ("l_lre", "l_lim", "l_ldt", "l_bre", "l_bim", "l_cre", "l_cim", "l_d")]
            maskF, maskB = ld("c_maskF"), ld("c_maskB")
            allc = ["l_lre", "l_lim", "l_ldt", "l_bre", "l_bim", "l_cre", "l_cim", "l_d", "c_maskF", "c_maskB", "c_ident"]
            first = [True]

            def D(fn, eng="dve"):
                P.op(eng, fn, reads=["prep"] + (allc if first[0] else []), writes=["prep"])
                first[0] = False

            def v(name, shape=(128, 32), dt=F32):
                return sb2(name, list(shape), dt)
            dt_, xr, xi, mag, imag = v("dt"), v("xr"), v("xi"), v("mag"), v("imag")
            sn, cs, q, r, m = v("sn"), v("cs"), v("q"), v("r"), v("m")
            qi = v("qi", dt=I32)
            D(lambda e: e.activation(out=dt_[:], in_=ldt[:], func=AF.Exp), "act")
            D(lambda e: e.tensor_tensor(out=xr[:], in0=lre[:], in1=dt_[:], op=ALU.mult))
            D(lambda e: e.tensor_tensor(out=xi[:], in0=lim[:], in1=dt_[:], op=ALU.mult))
            D(lambda e: e.activation(out=mag[:], in_=xr[:], func=AF.Exp), "act")
            D(lambda e: e.activation(out=imag[:], in_=xr[:], func=AF.Exp, scale=-1.0), "act")
            for off, dst in ((0.0, sn), (PI / 2, cs)):
                D(lambda e, off=off: e.tensor_scalar(out=r[:], in0=xi[:], scalar1=off, scalar2=None, op0=ALU.add))
                D(lambda e: e.tensor_scalar(out=q[:], in0=r[:], scalar1=1.0 / (2 * PI), scalar2=0.5, op0=ALU.mult, op1=ALU.add))
                D(lambda e: e.tensor_copy(out=qi[:], in_=q[:]))
                D(lambda e: e.tensor_copy(out=q[:], in_=qi[:]))
                D(lambda e: e.scalar_tensor_tensor(out=r[:], in0=q[:], scalar=-2 * PI, in1=r[:], op0=ALU.mult, op1=ALU.add))
                D(lambda e: e.tensor_scalar(out=m[:], in0=r[:], scalar1=-PI, scalar2=2 * PI, op0=ALU.is_lt, op1=ALU.mult))
                D(lambda e: e.tensor_tensor(out=r[:], in0=r[:], in1=m[:], op=ALU.add))
                D(lambda e: e.tensor_scalar(out=m[:], in0=r[:], scalar1=PI, scalar2=-2 * PI, op0=ALU.is_gt, op1=ALU.mult))
                D(lambda e: e.tensor_tensor(out=r[:], in0=r[:], in1=m[:], op=ALU.add))
                D(lambda e: e.tensor_scalar(out=r[:], in0=r[:], scalar1=-3.1415925, scalar2=3.1415925, op0=ALU.max, op1=ALU.min))
                D(lambda e, dst=dst: e.activation(out=dst[:], in_=r[:], func=AF.Sin), "act")
            pwr, pwi, ipr, ipi = [v(n, (128, 32, 9)) for n in ("pwr", "pwi", "ipr", "ipi")]
            t1, t2 = v("t1"), v("t2")
            for (pr_, pi_, mg, sgn) in ((pwr, pwi, mag, 1.0), (ipr, ipi, imag, -1.0)):
                D(lambda e, pr_=pr_: e.memset(pr_[:, :, 0:1], 1.0))
                D(lambda e, pi_=pi_: e.memset(pi_[:, :, 0:1], 0.0))
                D(lambda e, pr_=pr_, mg=mg: e.tensor_tensor(out=pr_[:, :, 1], in0=mg[:], in1=cs[:], op=ALU.mult))
                D(lambda e, pi_=pi_, mg=mg, sgn=sgn: e.scalar_tensor_tensor(out=pi_[:, :, 1], in0=mg[:], scalar=sgn, in1=sn[:], op0=ALU.mult, op1=ALU.mult))
                for k in range(2, 9):
                    D(lambda e, pr_=pr_, k=k: e.tensor_tensor(out=t1[:], in0=pr_[:, :, k - 1], in1=pr_[:, :, 1], op=ALU.mult))
                    D(lambda e, pi_=pi_, k=k: e.tensor_tensor(out=t2[:], in0=pi_[:, :, k - 1], in1=pi_[:, :, 1], op=ALU.mult))
                    D(lambda e, pr_=pr_, k=k: e.tensor_tensor(out=pr_[:, :, k], in0=t1[:], in1=t2[:], op=ALU.subtract))
                    D(lambda e, pr_=pr_, pi_=pi_, k=k: e.tensor_tensor(out=t1[:], in0=pr_[:, :, k - 1], in1=pi_[:, :, 1], op=ALU.mult))
                    D(lambda e, pr_=pr_, pi_=pi_, k=k: e.tensor_tensor(out=t2[:], in0=pi_[:, :, k - 1], in1=pr_[:, :, 1], op=ALU.mult))
                    D(lambda e, pi_=pi_, k=k: e.tensor_tensor(out=pi_[:, :, k], in0=t1[:], in1=t2[:], op=ALU.add))
            for gh in range(2):
                for ri in range(2):
                    D(lambda e, gh=gh, ri=ri: e.tensor_copy(out=A1[:, gh, ri, :], in_=pwr[:, gh * 16:(gh + 1) * 16, 8]))
                    D(lambda e, gh=gh, ri=ri: e.tensor_scalar(out=Bm[:, gh, ri, :], in0=pwi[:, gh * 16:(gh + 1) * 16, 8], scalar1=(-1.0 if ri == 0 else 1.0),
                                                               scalar2=None, op0=ALU.mult))
            ur, ui = v("ur"), v("ui")
            D(lambda e: e.tensor_copy(out=ur[:], in_=pwr[:, :, 8]))
            D(lambda e: e.tensor_copy(out=ui[:], in_=pwi[:, :, 8]))
            g2 = lambda a, sl: a[sl].rearrange("p (a b) -> p a b", a=2)
            for k in range(LS):
                for d in range(2):
                    sl = slice(d * 64, (d + 1) * 64)
                    mi = k
                    for ri in range(2):
                        D(lambda e, sl=sl, mi=mi, ri=ri: e.tensor_copy(out=PA[sl, :, ri, :, mi], in_=g2(ur, sl)))
                        D(lambda e, sl=sl, mi=mi, ri=ri: e.tensor_scalar(out=PB[sl, :, ri, :, mi], in0=g2(ui, sl), scalar1=(-1.0 if ri == 0 else 1.0),
                                                                          scalar2=None, op0=ALU.mult))
                if k == LS - 1:
                    for ri in range(2):
                        D(lambda e, ri=ri: e.tensor_copy(out=ALs[:, :, ri, :], in_=ur[:].rearrange("p (a b) -> p a b", a=2)))
                        D(lambda e, ri=ri: e.tensor_scalar(out=BLs[:, :, ri, :], in0=ui[:].rearrange("p (a b) -> p a b", a=2), scalar1=(-1.0 if ri == 0 else 1.0),
                                                           scalar2=None, op0=ALU.mult))
                else:
                    D(lambda e: e.tensor_tensor(out=t1[:], in0=ur[:], in1=pwr[:, :, 8], op=ALU.mult))
                    D(lambda e: e.tensor_tensor(out=t2[:], in0=ui[:], in1=pwi[:, :, 8], op=ALU.mult))
                    D(lambda e: e.tensor_tensor(out=t1[:], in0=t1[:], in1=t2[:], op=ALU.subtract))
                    D(lambda e: e.tensor_tensor(out=t2[:], in0=ur[:], in1=pwi[:, :, 8], op=ALU.mult))
                    D(lambda e: e.tensor_tensor(out=ui[:], in0=ui[:], in1=pwr[:, :, 8], op=ALU.mult))
                    D(lambda e: e.tensor_tensor(out=ui[:], in0=ui[:], in1=t2[:], op=ALU.add))
                    D(lambda e: e.tensor_copy(out=ur[:], in_=t1[:]))
            nr, den, c_r, c_i = v("nr"), v("den"), v("c_r"), v("c_i")
            D(lambda e: e.tensor_scalar(out=nr[:], in0=pwr[:, :, 1], scalar1=-1.0, scalar2=None, op0=ALU.add))
            D(lambda e: e.tensor_tensor(out=t1[:], in0=lre[:], in1=lre[:], op=ALU.mult))
            D(lambda e: e.tensor_tensor(out=t2[:], in0=lim[:], in1=lim[:], op=ALU.mult))
            D(lambda e: e.tensor_tensor(out=den[:], in0=t1[:], in1=t2[:], op=ALU.add))
            D(lambda e: e.reciprocal(out=den[:], in_=den[:]))
            D(lambda e: e.tensor_tensor(out=t1[:], in0=nr[:], in1=lre[:], op=ALU.mult))
            D(lambda e: e.tensor_tensor(out=t2[:], in0=pwi[:, :, 1], in1=lim[:], op=ALU.mult))
            D(lambda e: e.tensor_tensor(out=t1[:], in0=t1[:], in1=t2[:], op=ALU.add))
            D(lambda e: e.tensor_tensor(out=c_r[:], in0=t1[:], in1=den[:], op=ALU.mult))
            D(lambda e: e.tensor_tensor(out=t1[:], in0=pwi[:, :, 1], in1=lre[:], op=ALU.mult))
            D(lambda e: e.tensor_tensor(out=t2[:], in0=nr[:], in1=lim[:], op=ALU.mult))
            D(lambda e: e.tensor_tensor(out=t1[:], in0=t1[:], in1=t2[:], op=ALU.subtract))
            D(lambda e: e.tensor_tensor(out=c_i[:], in0=t1[:], in1=den[:], op=ALU.mult))
            big = lambda n: v(n, (128, 32, 128))
            Xr, Xi, Hr, Hi, tmp = big("Xr"), big("Xi"), big("Hr"), big("Hi"), big("tmpb")
            bbr, bbi = v("bbr", (128, 32, 16)), v("bbi", (128, 32, 16))
            tb16 = v("tb16", (128, 32, 16))

            def cmul(o_r, o_i, ar, ai, br, bi, tm, neg_i=False):
                D(lambda e: e.tensor_tensor(out=o_r, in0=ar, in1=br, op=ALU.mult))
                D(lambda e: e.tensor_tensor(out=tm, in0=ai, in1=bi, op=ALU.mult))
                D(lambda e: e.tensor_tensor(out=o_r, in0=o_r, in1=tm, op=ALU.subtract))
                D(lambda e: e.tensor_tensor(out=o_i, in0=ar, in1=bi, op=ALU.mult))
                D(lambda e: e.tensor_tensor(out=tm, in0=ai, in1=br, op=ALU.mult))
                D(lambda e: e.tensor_tensor(out=o_i, in0=o_i, in1=tm, op=ALU.add))
                if neg_i:
                    D(lambda e: e.tensor_scalar(out=o_i, in0=o_i, scalar1=-1.0, scalar2=None, op0=ALU.mult))
            b16 = lambda a: a[:].unsqueeze(2).broadcast_to([128, 32, 16])
            cmul(bbr[:], bbi[:], b16(c_r), b16(c_i), bre[:], bim[:], tb16[:])
            sel = {n: v(n, (128, 32, 8)) for n in ("gGr", "gGi", "gHr", "gHi", "gIr", "gIi")}
            for j in range(8):
                for (dst_r, dst_i, sr, si, kf, kb) in (("gGr", "gGi", pwr, pwi, 7 - j, j), ("gHr", "gHi", pwr, pwi, j + 1, 8 - j),
                                                       ("gIr", "gIi", ipr, ipi, j + 1, 8 - j)):
                    for dname, src in ((dst_r, sr), (dst_i, si)):
                        D(lambda e, dname=dname, src=src, kf=kf, j=j: e.tensor_copy(out=sel[dname][0:64, :, j], in_=src[0:64, :, kf]))
                        D(lambda e, dname=dname, src=src, kb=kb, j=j: e.tensor_copy(out=sel[dname][64:128, :, j], in_=src[64:128, :, kb]))
            X4 = lambda a: a[:].rearrange("p g (j h) -> p g j h", j=8)
            pj = lambda a: a[:].unsqueeze(3).broadcast_to([128, 32, 8, 16])
            ph = lambda a: a[:].unsqueeze(2).broadcast_to([128, 32, 8, 16])
            cmul(X4(Xr), X4(Xi), pj(sel["gGr"]), pj(sel["gGi"]), ph(bbr), ph(bbi), X4(tmp))
            for g in range(32):
                bank, bres = K.bank()
                for ri, X in enumerate((Xr, Xi)):
                    P.op("pe", lambda e, bank=bank, X=X, g=g, ri=ri: e.transpose(out=bank[:, ri * 128:(ri + 1) * 128], in_=X[:, g, :], identity=ident[:]),
                         reads=["prep", "c_ident"], writes=[bres])
                K.evac(g, GS[:, g, :, :], bank[:, 0:256].rearrange("p (r m) -> p r m", r=2), [bres], [("GS", g)])
            P._add("dve", None, [("GS", g) for g in range(32)], ["prep"], False)
            cmul(X4(Hr), X4(Hi), pj(sel["gHr"]), pj(sel["gHi"]), ph(cre), ph(cim), X4(tmp), neg_i=True)
            D(lambda e: e.memset(H[:], 0.0), "pool")
            for d in range(2):
                sl = slice(d * 64, (d + 1) * 64)
                D(lambda e, sl=sl, d=d: e.tensor_copy(out=H[sl, d, :, 0, :], in_=Hr[sl]))
                D(lambda e, sl=sl, d=d: e.tensor_copy(out=H[sl, d, :, 1, :], in_=Hi[sl]))
            cmul(X4(Xr), X4(Xi), pj(sel["gIr"]), pj(sel["gIi"]), ph(bbr), ph(bbi), X4(tmp))
            tt1 = v("tt1", (128, 128))
            tt2 = v("tt2", (128, 128))
            for g in range(32):
                bk = []
                for d in range(2):
                    bank, bres = K.bank()
                    sl = slice(d * 64, (d + 1) * 64)
                    P.op("pe", lambda e, bank=bank, sl=sl, g=g: e.matmul(bank[:, 0:128], lhsT=Xr[sl, g, :], rhs=Hr[sl, g, :], start=True, stop=False),
                         reads=["prep"], writes=[bres])
                    P.op("pe", lambda e, bank=bank, sl=sl, g=g: e.matmul(bank[:, 0:128], lhsT=Xi[sl, g, :], rhs=Hi[sl, g, :], start=False, stop=True),
                         reads=["prep"], writes=[bres])
                    bk.append((bank, bres))
                P.op("dve", lambda e, b=bk[0][0]: e.tensor_tensor(out=tt1[:], in0=b[:, 0:128], in1=maskF[:], op=ALU.mult), reads=[bk[0][1], "prep"], writes=["tt1"])
                P.op("dve", lambda e, b=bk[1][0]: e.tensor_tensor(out=tt2[:], in0=b[:, 0:128], in1=maskB[:], op=ALU.mult), reads=[bk[1][1], "prep"], writes=["tt2"])
                P.op("dve", lambda e: e.tensor_tensor(out=tt1[:], in0=tt1[:], in1=tt2[:], op=ALU.add), reads=["tt1", "tt2"], writes=["tt1"])
                P.op("dve", lambda e, g=g: e.scalar_tensor_tensor(out=T[:, g, :], in0=ident[:], scalar=dtl[:, g:g + 1], in1=tt1[:], op0=ALU.mult, op1=ALU.add),
                     reads=["tt1", "prep", "c_ident"], writes=[("T", g)])
        K.barrier()
        if S5_CUT == 1:
            return
        SEL, SELT = K.load_const(sb, "c_sel"), K.load_const(sb, "c_selT")
        UTs = sb("UTs", [128, 4, SEQ], BF16)
        U = sb("U", [128, 16, 512], BF16)
        Z = sb("Z", [128, 2, 16, 512], BF16)
        YG = sb("YG", [128, 8, 512], BF16)
        stt = sb("st", [128, 2, 16, NSEG], F32)
        tA = sb("tA", [128, 2, 16, LS], F32)
        tB = sb("tB", [128, 2, 16, LS], F32)
        Ec = sb("Ec", [128, 2, 16, NSEG], F32)
        Esw = sb("Esw", [128, 2, 16, NSEG], F32)
        sg = [sb("sg%d" % i, [128, 512], F32) for i in range(1)]
        bo = [sb("bo%d" % i, [128, 2, 512], BF16) for i in range(1)]
        UT, BT = K.S("UT"), K.S("BT")
        zres = [("Z", ri, gi) for ri in range(2) for gi in range(16)]
        for s in range(NSEQ):
            tb = s * SEQ
            P.dma("sp", lambda e, tb=tb: e.dma_start(out=UTs[:], in_=fm_view(UT, tb, SEQ)), writes=[("UTs", fb) for fb in range(4)])
            for gh in range(2):
                for gi in range(16):
                    g = gh * 16 + gi
                    fb, gl = g // 8, g % 8
                    bank, bres = K.bank()
                    for j in range(8):
                        P.op("pe", lambda e, bank=bank, fb=fb, gl=gl, j=j: e.matmul(bank[:], lhsT=SEL[:, gl, j, :],
                                                                                  rhs=UTs[:, fb, :].rearrange("p (c j) -> p j c", j=8)[:, j, :],
                                                                                  start=(j == 0), stop=(j == 7)),
                             reads=[("UTs", fb), "c_sel"], writes=[bres])
                    K.evac(gi, U[:, gi, :], bank[:], [bres], [("U", gi)])
                if S5_CUT == 2:
                    continue
                for gi in range(16):
                    g = gh * 16 + gi
                    for ri in range(2):
                        bank, bres = K.bank()
                        P.op("pe", lambda e, bank=bank, g=g, ri=ri, gi=gi: e.matmul(bank[0:64, :], lhsT=GS[:, g, ri, 0:64], rhs=U[:, gi, :], start=True, stop=True,
                                                                                 tile_position=(0, 0)),
                             reads=[("GS", g), ("U", gi)], writes=[bres])
                        P.op("pe", lambda e, bank=bank, g=g, ri=ri, gi=gi: e.matmul(bank[64:128, :], lhsT=GS[:, g, ri, 64:128], rhs=U[:, gi, ::-1], start=True, stop=True,
                                                                                 tile_position=(0, 64)),
                             reads=[("GS", g), ("U", gi)], writes=[bres])
                        K.evac(ri, Z[:, ri, gi, :], bank[:], [bres], [("Z", ri, gi)])
                if S5_CUT == 3:
                    continue
                for d, eng in ((0, "dve"),):
                    sl = slice(0, 128)
                    zr = ("Zrec", d)
                    P._add(eng, None, zres, [zr], False)
                    Zv = Z[sl].rearrange("p r g (s m) -> p r g s m", m=LS)
                    bc = lambda a, sl=sl, gh=gh: a[sl, gh].unsqueeze(3).broadcast_to([128, 2, 16, NSEG])
                    r1, r2 = tA[sl, :, :, 0:NSEG], tB[sl, :, :, 0:NSEG]
                    P.op(eng, lambda e, sl=sl: e.memset(stt[sl], 0.0), writes=[("st", d)])
                    for step in range(min(S5_STEPS, LS)):
                        mcol = step if d == 0 else LS - 1 - step
                        P.op(eng, lambda e, sl=sl, r1=r1, bc=bc: e.tensor_tensor(out=r1, in0=bc(A1), in1=stt[sl], op=ALU.mult),
                             reads=[("st", d)], writes=[("r1", d)])
                        P.op(eng, lambda e, sl=sl, r2=r2, bc=bc: e.tensor_tensor(out=r2, in0=bc(Bm), in1=stt[sl, ::-1], op=ALU.mult),
                             reads=[("st", d)], writes=[("r2", d)])
                        P.op(eng, lambda e, r1=r1, r2=r2: e.tensor_tensor(out=r1, in0=r1, in1=r2, op=ALU.add),
                             reads=[("r1", d), ("r2", d)], writes=[("r1", d)])
                        P.op(eng, lambda e, sl=sl, r1=r1, Zv=Zv, mcol=mcol: e.tensor_tensor(out=stt[sl], in0=r1, in1=Zv[:, :, :, :, mcol], op=ALU.add),
                             reads=[("r1", d), zr], writes=[("st", d)])
                        P.op(eng, lambda e, sl=sl, Zv=Zv, mcol=mcol: e.tensor_copy(out=Zv[:, :, :, :, mcol], in_=stt[sl]), reads=[("st", d)], writes=[zr])
                    if S5_SUB < 2:
                        continue
                    mend = LS - 1 if d == 0 else 0
                    order = list(range(NSEG)) if d == 0 else list(range(NSEG - 1, -1, -1))
                    e1, e2 = tA[sl, :, :, 0], tB[sl, :, :, 0]
                    for n_, sg_ in enumerate(order):
                        if n_ == 0:
                            P.op(eng, lambda e, sl=sl, Zv=Zv, sg_=sg_, mend=mend: e.tensor_copy(out=Ec[sl, :, :, sg_], in_=Zv[:, :, :, sg_, mend]),
                                 reads=[zr], writes=[("Ec", d)])
                            continue
                        pv = order[n_ - 1]
                        P.op(eng, lambda e, sl=sl, e1=e1, pv=pv, gh=gh: e.tensor_tensor(out=e1, in0=ALs[sl, gh], in1=Ec[sl, :, :, pv], op=ALU.mult),
                             reads=[("Ec", d)], writes=[("r1", d)])
                        P.op(eng, lambda e, sl=sl, e2=e2, pv=pv, gh=gh: e.tensor_tensor(out=e2, in0=BLs[sl, gh], in1=Ec[sl, ::-1, :, pv], op=ALU.mult),
                             reads=[("Ec", d)], writes=[("r2", d)])
                        P.op(eng, lambda e, e1=e1, e2=e2: e.tensor_tensor(out=e1, in0=e1, in1=e2, op=ALU.add), reads=[("r1", d), ("r2", d)], writes=[("r1", d)])
                        P.op(eng, lambda e, sl=sl, e1=e1, Zv=Zv, sg_=sg_, mend=mend: e.tensor_tensor(out=Ec[sl, :, :, sg_], in0=e1, in1=Zv[:, :, :, sg_, mend], op=ALU.add),
                             reads=[("r1", d), zr, ("Ec", d)], writes=[("Ec", d)])
                    if S5_SUB < 3:
                        continue
                    f1, f2 = tA[sl], tB[sl]
                    for ri in range(2):
                        P.op(eng, lambda e, sl=sl, ri=ri: e.tensor_copy(out=Esw[sl, ri], in_=Ec[sl, 1 - ri]), reads=[("Ec", d)], writes=[("Esw", d)])
                    for sg_ in range(NSEG):
                        src = sg_ - 1 if d == 0 else sg_ + 1
                        if src < 0 or src >= NSEG:
                            continue
                        eb = lambda rev, src=src, sl=sl: (Esw[sl, :, :, src] if rev else Ec[sl, :, :, src]).unsqueeze(3).broadcast_to([128, 2, 16, LS])
                        P.op(eng, lambda e, sl=sl, f1=f1, eb=eb, gh=gh: e.tensor_tensor(out=f1, in0=PA[sl, gh], in1=eb(False), op=ALU.mult), reads=[("Ec", d)], writes=[("r1", d)])
                        P.op(eng, lambda e, sl=sl, f2=f2, eb=eb, gh=gh: e.tensor_tensor(out=f2, in0=PB[sl, gh], in1=eb(True), op=ALU.mult), reads=[("Esw", d)], writes=[("r2", d)])
                        P.op(eng, lambda e, f1=f1, f2=f2: e.tensor_tensor(out=f1, in0=f1, in1=f2, op=ALU.add), reads=[("r1", d), ("r2", d)], writes=[("r1", d)])
                        P.op(eng, lambda e, f1=f1, Zv=Zv, sg_=sg_: e.tensor_tensor(out=Zv[:, :, :, sg_, :], in0=Zv[:, :, :, sg_, :], in1=f1, op=ALU.add),
                             reads=[("r1", d), zr], writes=[zr])
                P._add("pe", None, [("Zrec", 0)], zres, False)
                if S5_CUT == 4:
                    continue
                for gi in range(16):
                    g = gh * 16 + gi
                    fb, gl = g // 8, g % 8
                    bank, bres = K.bank()
                    P.op("pe", lambda e, bank=bank, g=g, gi=gi: e.matmul(bank[:], lhsT=T[:, g, :], rhs=U[:, gi, :], start=True, stop=False),
                         reads=[("T", g), ("U", gi)], writes=[bres])
                    for d in range(2):
                        sl = slice(d * 64, (d + 1) * 64)
                        for ri in range(2):
                            if d == 0:
                                o_, z_ = bank[:, 1:512], Z[:, ri, gi, 0:511]
                            else:
                                o_, z_ = bank[:, 0:511], Z[:, ri, gi, 510::-1]
                            P.op("pe", lambda e, o_=o_, z_=z_, g=g, ri=ri, d=d: e.matmul(o_, lhsT=H[:, d, g, ri, :], rhs=z_, start=False, stop=(d == 1 and ri == 1)),
                                 reads=[("Z", ri, gi), "prep"], writes=[bres])
                    P.op("act", lambda e, bank=bank, gl=gl: e.activation(out=YG[:, gl, :], in_=bank[:], func=AF.Gelu), reads=[bres], writes=[("YG", gl)])
                    if gl == 7:
                        for i in range(8):
                            bank2, b2res = K.bank()
                            for gl2 in range(8):
                                P.op("pe", lambda e, bank2=bank2, gl2=gl2, i=i: e.matmul(bank2[:], lhsT=SELT[:, gl2, i, :], rhs=YG[:, gl2, :], start=(gl2 == 0), stop=(gl2 == 7)),
                                     reads=[("YG", gl2), "c_selT"], writes=[b2res])
                            K.evac(i, UTs[:, fb, :].rearrange("p (c j) -> p j c", j=8)[:, i, :], bank2[:], [b2res], [("UTs", fb)])
            for tt_ in range(SEQ // TT):
                c0 = tt_ * TT
                o, ores = bo[0], ("bo", 0)
                for oc in range(4):
                    bank, bres = K.bank()
                    for k in range(4):
                        P.op("pe", lambda e, bank=bank, k=k, oc=oc, c0=c0: e.matmul(bank[:], lhsT=wglu[:, k, oc * 128:(oc + 1) * 128], rhs=UTs[:, k, c0:c0 + TT],
                                                                                 start=(k == 0), stop=(k == 3)),
                             reads=["w_glu", ("UTs", k)], writes=[bres])
                    sgt, sgr = sg[0], ("sg", 0)
                    P.op("act", lambda e, sgt=sgt, bank=bank, oc=oc: e.activation(out=sgt[:], in_=bank[:], func=AF.Sigmoid, bias=bglu[:, oc:oc + 1], scale=1.0),
                         reads=[bres, "b_glu"], writes=[sgr])
                    P.op("dve", lambda e, sgt=sgt, o=o, oc=oc, c0=c0: e.tensor_tensor(out=o[:, oc % 2, :], in0=UTs[:, oc, c0:c0 + TT], in1=sgt[:], op=ALU.mult),
                         reads=[sgr, ("UTs", oc)], writes=[ores + (oc % 2,)])
                    if oc % 2 == 1:
                        P.dma("pool", lambda e, o=o, c0=c0, tb=tb, oc=oc: e.dma_start(out=fm_view(BT, tb + c0, TT)[:, oc - 1:oc + 1, :], in_=o[:]),
                              reads=[ores + (q,) for q in range(2)], writes=[("BT", tb + c0, oc)])
    K.barrier()


S5_STEPS = 512
S5_CUT = 0
S5_SUB = 3
PHASES["A2"] = phase_s5


ALL_PHASES = ["A0", "A1", "A2", "A3", "A4", "A5", "B0", "B1", "B3", "B4", "B5"]
LAUNCHES = [ALL_PHASES]


def kernel(**inputs):
    inp = {k: np.asarray(v) for k, v in inputs.items()}
    lay = host_layout(inp)
    x, p = inp["x"], inp["p"]
    core_inputs = []
    for c in range(NCORES):
        ci = {k: v for k, v in inp.items() if k not in ("x", "p")}
        ci.update(lay)
        ci["x"] = np.ascontiguousarray(x[NSEQ * c:NSEQ * (c + 1)].reshape(NTOK, D))
        ci["p"] = np.ascontiguousarray(p[:, NSEQ * c:NSEQ * (c + 1)].reshape(2, NTOK, 256))
        core_inputs.append(ci)
    res = run_launch(ALL_PHASES, core_inputs, ext_out=("OUT",))
    out = np.stack([np.asarray(r["OUT"]).reshape(NSEQ, SEQ, D) for r in res], axis=0).reshape(NCORES * NSEQ, SEQ, D)
    return out.astype(np.float32)
```

```python
import numpy as np
from contextlib import ExitStack
import concourse.bass as bass
import concourse.mybir as mybir
from concourse.bass_utils import run_bass_kernel_spmd

F32 = mybir.dt.float32
BF16 = mybir.dt.bfloat16
I32 = mybir.dt.int32
ALU = mybir.AluOpType
AF = mybir.ActivationFunctionType
AX = mybir.AxisListType

COMPUTE = ("pe", "act", "dve", "pool")
NDMA_SEMS = {"sp": 20, "pool": 10, "act": 6}


class Op:
    __slots__ = ("eng", "fn", "dma", "idx", "deps", "adeps", "inc", "semval", "sem", "waits", "eidx", "cost", "fin")


class Prog:
    def __init__(self, nc):
        self.nc = nc
        self.ops = []
        self.lastw = {}
        self.rd = {}
        self.per_eng = {e: [] for e in ("pe", "act", "dve", "pool", "sp")}

    def _add(self, eng, fn, reads, writes, dma):
        op = Op()
        op.eng, op.fn, op.dma = eng, fn, dma
        op.idx = len(self.ops)
        op.inc = False
        op.semval = None
        op.sem = None
        op.cost = None
        ad = {}
        for r in reads:
            d = self.lastw.get(r)
            if d is not None:
                ad[d.idx] = (d, True)
        for r in writes:
            d = self.lastw.get(r)
            if d is not None and d.idx not in ad:
                ad[d.idx] = (d, False)
            for d in self.rd.get(r, ()):
                if d.idx not in ad:
                    ad[d.idx] = (d, False)
        ad.pop(op.idx, None)
        op.adeps = ad
        for r in reads:
            self.rd.setdefault(r, []).append(op)
        for r in writes:
            self.lastw[r] = op
            self.rd[r] = []
        self.ops.append(op)
        op.eidx = len(self.per_eng[eng])
        self.per_eng[eng].append(op)
        return op

    def finalize_deps(self):
        for e, lst in self.per_eng.items():
            for i, op in enumerate(lst):
                op.eidx = i
        for op in self.ops:
            best = {}
            deps = []
            for d, raw in op.adeps.values():
                if d.dma:
                    deps.append(d)
                    continue
                if d.eng == op.eng and not op.dma:
                    if (not raw) or op.eng == "pe":
                        continue
                b = best.get(d.eng)
                if b is None or d.eidx > b.eidx:
                    best[d.eng] = d
            op.deps = deps + list(best.values())

    def op(self, eng, fn, reads=(), writes=()):
        return self._add(eng, fn, reads, writes, False)

    def dma(self, queue, fn, reads=(), writes=()):
        return self._add(queue, fn, reads, writes, True)

    def schedule(self, window=64):
        COST = {"pe": 0.25, "act": 0.6, "dve": 0.6, "pool": 0.9, "sp": 0.05}
        import heapq
        engs = list(self.per_eng)
        src = {e: self.per_eng[e] for e in engs}
        nxt = {e: 0 for e in engs}
        buf = {e: [] for e in engs}
        new = {e: [] for e in engs}
        free = {e: 0.0 for e in engs}
        for op in self.ops:
            op.fin = None
        remaining = len(self.ops)
        cand = {e: None for e in engs}
        dirty = set(engs)

        def refill(e):
            b, sl = buf[e], src[e]
            while len(b) < window and nxt[e] < len(sl):
                b.append(sl[nxt[e]])
                nxt[e] += 1

        def find(e):
            refill(e)
            best = None
            fe = free[e]
            n = 0
            for op in buf[e]:
                n += 1
                ok = True
                rdy = fe
                for d, raw in op.adeps.values():
                    f = d.fin
                    if f is None:
                        ok = False
                        break
                    if f > rdy and (raw or d.eng != e or d.dma):
                        rdy = f
                if op.fn is None:
                    if n == 1 and ok:
                        best = (rdy, op)
                    break
                if ok and (best is None or rdy < best[0]):
                    best = (rdy, op)
                    if rdy <= fe:
                        break
            return best

        while remaining:
            for e in dirty:
                cand[e] = find(e)
            dirty = set()
            be = None
            for e in engs:
                c = cand[e]
                if c is not None and (be is None or c[0] < cand[be][0]):
                    be = e
            if be is None:
                for e in engs:
                    cand[e] = find(e)
                    if cand[e] is not None:
                        be = e
                        break
                assert be is not None, "scheduler stuck"
            rdy, op = cand[be]
            c = COST[be] if not op.dma else 0.05
            op.fin = rdy + (c if not op.dma else 4.0)
            free[be] = rdy + c
            buf[be].remove(op)
            new[be].append(op)
            remaining -= 1
            dirty = set(engs) if True else {be}
        self.per_eng = new

    def emit(self, es):
        nc = self.nc
        if getattr(self, "do_schedule", True):
            self.schedule()
        self.finalize_deps()
        for op in self.ops:
            for d in op.deps:
                if not d.dma:
                    d.inc = True
        csem = {e: es.enter_context(nc.semaphore("s_" + e)) for e in COMPUTE}
        for e in COMPUTE:
            c = 0
            for op in self.per_eng[e]:
                if op.dma:
                    continue
                if op.inc:
                    c += 1
                    op.semval = c
                    op.sem = csem[e]
        dsems = {q: [es.enter_context(nc.semaphore("d_%s%d" % (q, i))) for i in range(n)]
                 for q, n in NDMA_SEMS.items()}
        dcur = {q: [0] * n for q, n in NDMA_SEMS.items()}
        dnext = {q: 0 for q in NDMA_SEMS}
        pre_wait = {}
        for op in [o for e in self.per_eng for o in self.per_eng[e]]:
            if op.dma:
                q = op.eng
                i = dnext[q]
                dnext[q] = (i + 1) % len(dsems[q])
                pre_wait[op.idx] = (dsems[q][i], dcur[q][i])
                dcur[q][i] += 16
                op.sem = dsems[q][i]
                op.semval = dcur[q][i]
        known = {e: {} for e in self.per_eng}
        for op in [o for e in self.per_eng for o in self.per_eng[e]]:
            k = known[op.eng]
            w = {}
            cand = [(d.sem, d.semval) for d in op.deps]
            if op.dma and pre_wait[op.idx][1] > 0:
                cand.append(pre_wait[op.idx])
            for sem, val in cand:
                key = id(sem)
                if k.get(key, (None, 0))[1] >= val:
                    continue
                if key in w and w[key][1] >= val:
                    continue
                w[key] = (sem, val)
            for key, sv in w.items():
                k[key] = sv
            op.waits = list(w.values())
        self.n_waits = sum(len(o.waits) for o in self.ops)
        block = es.enter_context(nc.Block())

        def run(engobj, lst):
            for op in lst:
                for sem, val in op.waits:
                    engobj.wait_ge(sem, val)
                if op.fn is None:
                    if op.inc:
                        engobj.nop().then_inc(op.sem, 1)
                    continue
                ins = op.fn(engobj)
                if op.dma:
                    ins.then_inc(op.sem, 16)
                elif op.inc:
                    ins.then_inc(op.sem, 1)

        @block.tensor
        def _(e):
            run(e, self.per_eng["pe"])

        @block.scalar
        def _(e):
            run(e, self.per_eng["act"])

        @block.vector
        def _(e):
            run(e, self.per_eng["dve"])

        @block.gpsimd
        def _(e):
            run(e, self.per_eng["pool"])
            for q in dsems:
                for s, v in zip(dsems[q], dcur[q]):
                    if v > 0:
                        e.wait_ge(s, v)

        @block.sync
        def _(e):
            run(e, self.per_eng["sp"])


NCORES = 8
SEQ = 4096
D = 1024
NSEQ = 2
NTOK = NSEQ * SEQ
TT = 512
NT = NTOK // TT
DFF = 2816
EPS = 1e-6

IN_SHAPES = {
    "x": ([NTOK, D], F32), "p": ([2, NTOK, 256], F32),
    "ab_norm": ([1, 1024], F32), "ab_w_in": ([1, 1024, 2048], F32), "na_rpb": ([1, 8, 15, 31], F32),
    "s5_lambda_re": ([1, 2, 32, 64], F32), "s5_lambda_im": ([1, 2, 32, 64], F32), "s5_log_dt": ([1, 2, 32], F32),
    "s5_b_re": ([1, 2, 32, 64, 16], F32), "s5_b_im": ([1, 2, 32, 64, 16], F32),
    "s5_c_re": ([1, 2, 32, 16, 64], F32), "s5_c_im": ([1, 2, 32, 16, 64], F32),
    "s5_d": ([1, 512], F32), "s5_w_glu": ([1, 512, 512], F32), "s5_b_glu": ([1, 512], F32),
    "ab_w_out": ([1, 1024, 1024], F32), "ret_norm": ([1, 1024], F32), "ret_w_in": ([1, 1024, 6144], F32),
    "ret_decay": ([1, 2, 4], F32), "ret_w_out": ([1, 2048, 1024], F32), "ffn_norm": ([2, 1024], F32),
    "ffn_w_up": ([2, 1024, 5632], F32), "ffn_conv_w": ([2, 3, 5632], F32), "ffn_conv_b": ([2, 5632], F32),
    "ffn_w_down": ([2, 2816, 1024], F32), "ple_norm": ([2, 1024], F32), "ple_w_gate": ([2, 1024, 1024], F32),
    "ple_w_proj": ([2, 256, 1024], F32), "final_norm": ([1024], F32),
    "c_ident": ([128, 128], F32), "c_identb": ([128, 128], BF16),
    "c_cos": ([128, SEQ], F32), "c_sin": ([128, SEQ], F32),
    "c_D1": ([128, 128], F32), "c_M1": ([128, 128], F32), "c_D2": ([128, 128], F32), "c_M2": ([128, 128], F32),
    "c_iota1": ([128, 128], F32), "c_iota2": ([128, 128], F32), "c_pidx": ([128, 2], F32),
    "c_sel": ([128, 8, 8, 128], BF16), "c_selT": ([128, 8, 8, 128], BF16), "c_maskF": ([128, 128], F32), "c_maskB": ([128, 128], F32),
    "l_lre": ([128, 32], F32), "l_lim": ([128, 32], F32), "l_ldt": ([128, 32], F32),
    "l_bre": ([128, 32, 16], F32), "l_bim": ([128, 32, 16], F32), "l_cre": ([128, 32, 16], F32), "l_cim": ([128, 32, 16], F32),
    "l_d": ([128, 32], F32),
}

SCR_SHAPES = {
    "XT0": ([1024, NTOK], F32), "QT": ([512, NTOK], BF16), "KT": ([512, NTOK], BF16), "VN": ([NTOK, 512], BF16),
    "UT": ([512, NTOK], BF16), "AT": ([512, NTOK], BF16), "BT": ([512, NTOK], BF16),
    "XT1": ([1024, NTOK], F32), "MT0": ([DFF, NTOK], BF16), "XT3": ([1024, NTOK], F32),
    "RQ": ([1024, NTOK], BF16), "RK": ([1024, NTOK], BF16), "RKN": ([NTOK, 1024], BF16),
    "RV": ([NTOK, 2048], BF16), "RG": ([NTOK, 2048], BF16), "YB": ([NTOK, 2048], F32), "RY": ([2048, NTOK], BF16),
    "XT4": ([1024, NTOK], F32), "MT1": ([DFF, NTOK], BF16), "OUT": ([NTOK, 1024], F32),
}


class KB:
    def __init__(self, ext_in, ext_out):
        self.nc = bass.Bass("TRN2", target_bir_lowering=False)
        self.P = Prog(self.nc)
        self.es = ExitStack()
        self.ext_in, self.ext_out = set(ext_in), set(ext_out)
        self._d = {}
        self.used_inputs = []
        self.nbank = 0
        nc = self.nc
        self.banks = [self.es.enter_context(nc.psum_tensor("psf%d" % i, [128, 512], F32)) for i in range(6)]
        self.bankbs = [self.es.enter_context(nc.psum_tensor("psb%d" % i, [128, 1024], BF16)) for i in range(2)]
        self.bankb = self.bankbs[0]
        self.uid = 0

    def I(self, name):
        if name not in self._d:
            shape, dt = IN_SHAPES[name]
            self._d[name] = self.nc.dram_tensor(name, list(shape), dt, kind="ExternalInput").ap()
            self.used_inputs.append(name)
        return self._d[name]

    def S(self, name):
        if name not in self._d:
            shape, dt = SCR_SHAPES[name]
            if name in self.ext_in:
                kind = "ExternalInput"
                self.used_inputs.append(name)
            elif name in self.ext_out:
                kind = "ExternalOutput"
            else:
                kind = "Internal"
            self._d[name] = self.nc.dram_tensor(name, list(shape), dt, kind=kind).ap()
        return self._d[name]

    def bank(self, lo=0, hi=5):
        i = lo + self.nbank % (hi - lo)
        self.nbank += 1
        return self.banks[i], ("ps", i)

    def barrier(self):
        P = self.P
        deps = []
        for e in COMPUTE:
            for op in reversed(P.per_eng[e]):
                if not op.dma:
                    deps.append(op)
                    break
        start = getattr(self, "_bar_start", 0)
        deps += [op for op in P.ops[start:] if op.dma]
        self._bar_start = len(P.ops)
        for e in ("pe", "act", "dve", "pool", "sp"):
            op = P._add(e, None, (), (), False)
            op.adeps = {d.idx: (d, True) for d in deps if not ((not d.dma) and d.eng == e)}

    def sballoc(self, stack):
        def sb(name, shape, dt):
            self.uid += 1
            return stack.enter_context(self.nc.sbuf_tensor("%s_%d" % (name, self.uid), list(shape), dt))
        return sb

    def load_w(self, sb, src, K, N, name, CB=2048, stg_sb=None):
        P = self.P
        nk = K // 128
        wt = sb(name, [128, nk, N], BF16)
        if getattr(self, "_stg_owner", None) is not sb:
            self._stg_owner = sb
            self._stg = [(stg_sb or sb)("stg%d" % i, [128, CB], F32) for i in range(3)]
            self._stg_i = 0
        stg = self._stg
        i = self._stg_i
        for k in range(nk):
            for c0 in range(0, N, CB):
                cn = min(CB, N - c0)
                s = stg[i % 3]
                sr = ("stg", i % 3)
                P.dma("sp", lambda e, s=s, k=k, c0=c0, cn=cn: e.dma_start(out=s[:, :cn], in_=src[k * 128:(k + 1) * 128, c0:c0 + cn]),
                      writes=[sr])
                eng = ("pool", "dve", "act")[i % 3]
                if eng == "act":
                    fn = lambda e, s=s, k=k, c0=c0, cn=cn: e.activation(out=wt[:, k, c0:c0 + cn], in_=s[:, :cn], func=AF.Copy)
                else:
                    fn = lambda e, s=s, k=k, c0=c0, cn=cn: e.tensor_copy(out=wt[:, k, c0:c0 + cn], in_=s[:, :cn])
                P.op(eng, fn, reads=[sr], writes=[(name, k, c0)])
                i += 1
        self._stg_i = i
        P._add("pe", None, [(name, k, c0) for k in range(nk) for c0 in range(0, N, CB)], [name], False)
        return wt

    def load_vec(self, sb, src, n, name):
        t = sb(name, [128, n // 128], F32)
        self.P.dma("sp", lambda e: e.dma_start(out=t[:], in_=src.rearrange("(k p) -> p k", p=128), allow_slow_non_contiguous=True),
                   writes=[name])
        return t

    def load_const(self, sb, cname, name=None):
        shape, dt = IN_SHAPES[cname]
        name = name or cname
        t = sb(name, shape, dt)
        src = self.I(cname)
        self.P.dma("sp", lambda e: e.dma_start(out=t[:], in_=src), writes=[name])
        return t

    def rmsnorm(self, xT, xres, g, gname, h, hname, ones, sqb, rstd, nk=8, n=TT):
        P = self.P
        bank, br = self.banks[5], ("ps", 5)
        for k in range(nk):
            s = sqb[k % 2]
            sr = ("sqb", k % 2)
            P.op("act", lambda e, s=s, k=k: e.activation(out=s[:, :n], in_=xT[:, k, :n], func=AF.Square), reads=[xres(k)], writes=[sr])
            P.op("pe", lambda e, s=s, k=k: e.matmul(bank[:, :n], lhsT=ones[:], rhs=s[:, :n], start=(k == 0), stop=(k == nk - 1)),
                 reads=[sr, "ones"], writes=[br])
        P.op("act", lambda e: e.activation(out=rstd[:, :n], in_=bank[:, :n], func=AF.Sqrt, bias=EPS, scale=1.0), reads=[br], writes=["rstd"])
        P.op("dve", lambda e: e.reciprocal(out=bank[:, :n], in_=rstd[:, :n]), reads=["rstd"], writes=[br])
        for k in range(nk):
            P.op("dve", lambda e, k=k: e.scalar_tensor_tensor(out=h[:, k, :n], in0=xT[:, k, :n], scalar=g[:, k:k + 1], in1=bank[:, :n],
                                                             op0=ALU.mult, op1=ALU.mult),
                 reads=[xres(k), gname, br], writes=[(hname, k)])

    def norm_consts(self, sb):
        ones = sb("ones", [128, 128], F32)
        self.P.op("pool", lambda e: e.memset(ones[:], 1.0 / 1024), writes=["ones"])
        sqb = [sb("sqb%d" % i, [128, TT], F32) for i in range(2)]
        rstd = sb("rstd", [128, TT], F32)
        return ones, sqb, rstd

    def evac(self, i, out, in_, reads, writes):
        if i % 2 == 0:
            self.P.op("act", lambda e: e.activation(out=out, in_=in_, func=AF.Copy), reads=reads, writes=writes)
        else:
            self.P.op("dve", lambda e: e.tensor_copy(out=out, in_=in_), reads=reads, writes=writes)


def fm_view(ap, c0, n):
    return ap.rearrange("(k p) t -> p k t", p=128)[:, :, c0:c0 + n]


def phase_A0(K):
    P, nc = K.P, K.nc
    with ExitStack() as st:
        sb = K.sballoc(st)
        w = K.load_w(sb, K.I("ab_w_in")[0], 1024, 2048, "w_in")
        g = K.load_vec(sb, K.I("ab_norm")[0], 1024, "g_ab")
        ident = K.load_const(sb, "c_ident")
        ones, sqb, rstd = K.norm_consts(sb)
        xt = [sb("xt%d" % i, [128, 4, 1024], F32) for i in range(2)]
        xT = sb("xT", [128, 8, TT], F32)
        h = sb("h", [128, 8, TT], BF16)
        ofm = [sb("ofm%d" % i, [128, 4, TT], BF16) for i in range(2)]
        otm = [sb("otm%d" % i, [128, 512], BF16) for i in range(2)]
        x = K.I("x")
        XT0, QT, KT, UT, VN = K.S("XT0"), K.S("QT"), K.S("KT"), K.S("UT"), K.S("VN")
        no = 0
        nv = 0
        for t in range(NT):
            c0 = t * TT
            buf = xt[t % 2]
            br_ = ("xt", t % 2)
            P.dma("sp", lambda e, buf=buf, c0=c0: e.dma_start(out=buf[:], in_=x[c0:c0 + TT, :].rearrange("(n p) d -> p n d", p=128)),
                  writes=[br_])
            for k in range(8):
                bank, bres = K.bank()
                for n in range(4):
                    P.op("pe", lambda e, bank=bank, buf=buf, n=n, k=k: e.transpose(out=bank[:, n * 128:(n + 1) * 128],
                                                                                  in_=buf[:, n, k * 128:(k + 1) * 128], identity=ident[:]),
                         reads=[br_, "c_ident"], writes=[bres])
                K.evac(k, xT[:, k, :], bank[:], [bres], [("xT", k)])
            P.dma("pool", lambda e, c0=c0: e.dma_start(out=fm_view(XT0, c0, TT), in_=xT[:]),
                  reads=[("xT", k) for k in range(8)], writes=[("XT0", t)])
            K.rmsnorm(xT, lambda k: ("xT", k), g, "g_ab", h, "h", ones, sqb, rstd)
            for dst, dn, col0 in ((QT, "QT", 0), (KT, "KT", 512), (UT, "UT", 1536)):
                o = ofm[no % 2]
                ores = ("ofm", no % 2)
                no += 1
                for oc in range(4):
                    bank, bres = K.bank()
                    for k in range(8):
                        P.op("pe", lambda e, bank=bank, k=k, cc=col0 + oc * 128: e.matmul(bank[:], lhsT=w[:, k, cc:cc + 128], rhs=h[:, k, :],
                                                                                        start=(k == 0), stop=(k == 7)),
                             reads=["w_in", ("h", k)], writes=[bres])
                    K.evac(oc, o[:, oc, :], bank[:], [bres], [ores + (oc,)])
                P.dma("pool", lambda e, dst=dst, o=o, c0=c0: e.dma_start(out=fm_view(dst, c0, TT), in_=o[:]),
                      reads=[ores + (oc,) for oc in range(4)], writes=[(dn, t)])
            for sub in range(4):
                bank, bres = K.bank()
                for k in range(8):
                    P.op("pe", lambda e, bank=bank, k=k, sub=sub: e.matmul(bank[:], lhsT=h[:, k, sub * 128:(sub + 1) * 128], rhs=w[:, k, 1024:1536],
                                                                         start=(k == 0), stop=(k == 7)),
                         reads=["w_in", ("h", k)], writes=[bres])
                o = otm[nv % 2]
                ores = ("otm", nv % 2)
                nv += 1
                K.evac(sub, o[:], bank[:], [bres], [ores])
                r0 = c0 + sub * 128
                P.dma("pool", lambda e, o=o, r0=r0: e.dma_start(out=VN[r0:r0 + 128, :], in_=o[:]), reads=[ores], writes=[("VN", t, sub)])
    K.barrier()


PHASES = {}
PHASE_IO = {}
PHASES["A0"] = phase_A0
PHASE_IO["A0"] = ((), ("XT0", "QT", "KT", "UT", "VN"))


def host_consts():
    import ml_dtypes
    c = {}
    c["c_ident"] = np.eye(128, dtype=np.float32)
    c["c_identb"] = np.eye(128, dtype=np.float32).astype(ml_dtypes.bfloat16)
    inv = (np.float32(10000.0) ** (-np.arange(128, dtype=np.float32) / np.float32(128))).astype(np.float32)
    ang = (np.arange(SEQ, dtype=np.float32)[None, :] * inv[:, None]).astype(np.float32)
    c["c_cos"] = np.cos(ang.astype(np.float64)).astype(np.float32)
    c["c_sin"] = np.sin(ang.astype(np.float64)).astype(np.float32)
    jj = np.arange(128, dtype=np.float32)[:, None]
    ii = np.arange(128, dtype=np.float32)[None, :]
    c["c_D1"] = np.maximum(ii - jj, 0).astype(np.float32)
    c["c_M1"] = (ii >= jj).astype(np.float32)
    c["c_D2"] = np.maximum(jj - ii, 0).astype(np.float32)
    c["c_M2"] = (jj > ii).astype(np.float32)
    c["c_iota1"] = np.broadcast_to(ii + 1, (128, 128)).astype(np.float32).copy()
    c["c_iota2"] = np.broadcast_to(128 - ii, (128, 128)).astype(np.float32).copy()
    c["c_pidx"] = np.concatenate([127 - jj, jj], axis=1).astype(np.float32)
    sel = np.zeros((128, 8, 8, 128), np.float32)
    for gl in range(8):
        for j in range(8):
            for h in range(16):
                sel[gl * 16 + h, gl, j, j * 16 + h] = 1.0
    c["c_sel"] = sel.astype(ml_dtypes.bfloat16)
    c["c_selT"] = np.ascontiguousarray(sel.transpose(3, 1, 2, 0)).astype(ml_dtypes.bfloat16)
    jq = (np.arange(128) // 16)[:, None]
    iq = (np.arange(128) // 16)[None, :]
    c["c_maskF"] = (iq >= jq).astype(np.float32)
    c["c_maskB"] = (jq >= iq).astype(np.float32)
    return c


def host_layout(inp):
    o = {}
    def dp(a):
        return np.ascontiguousarray(np.asarray(a).transpose(0, 2, 1).reshape(128, 32))
    o["l_lre"] = dp(inp["s5_lambda_re"][0])
    o["l_lim"] = dp(inp["s5_lambda_im"][0])
    o["l_ldt"] = np.ascontiguousarray(np.repeat(np.asarray(inp["s5_log_dt"][0])[:, None, :], 64, axis=1).reshape(128, 32))
    o["l_bre"] = np.ascontiguousarray(np.asarray(inp["s5_b_re"][0]).transpose(0, 2, 1, 3).reshape(128, 32, 16))
    o["l_bim"] = np.ascontiguousarray(np.asarray(inp["s5_b_im"][0]).transpose(0, 2, 1, 3).reshape(128, 32, 16))
    o["l_cre"] = np.ascontiguousarray(np.asarray(inp["s5_c_re"][0]).transpose(0, 3, 1, 2).reshape(128, 32, 16))
    o["l_cim"] = np.ascontiguousarray(np.asarray(inp["s5_c_im"][0]).transpose(0, 3, 1, 2).reshape(128, 32, 16))
    dd = np.asarray(inp["s5_d"][0]).reshape(32, 16)
    o["l_d"] = np.ascontiguousarray(np.tile(dd.T[None, :, :], (8, 1, 1)).reshape(128, 32))
    return o


def build(phases, ext_in=(), ext_out=()):
    K = KB(ext_in, ext_out)
    with K.es:
        for ph in phases:
            PHASES[ph](K)
        K.P.emit(K.es)
    return K


def run_launch(phases, core_inputs, ext_in=(), ext_out=()):
    K = build(phases, ext_in, ext_out)
    consts = host_consts()
    in_maps = []
    for ci in core_inputs:
        m = {}
        for n in K.used_inputs:
            m[n] = consts[n] if n in consts else ci[n]
        in_maps.append(m)
    res = run_bass_kernel_spmd(K.nc, in_maps, core_ids=list(range(len(core_inputs))))
    return res.results


def phase_outproj(K, srcs, w_ap, kdim, xin_name, xout_name):
    P = K.P
    nk = kdim // 128
    with ExitStack() as st:
        sb = K.sballoc(st)
        w = K.load_w(sb, w_ap, kdim, 1024, "w_o")
        ain = [sb("ain%d" % i, [128, nk, TT], BF16) for i in range(2)]
        xin = [sb("xin%d" % i, [128, 8, TT], F32) for i in range(2)]
        XI, XO = K.S(xin_name), K.S(xout_name)
        for t in range(NT):
            c0 = t * TT
            a, ar = ain[t % 2], ("ain", t % 2)
            xi, xr = xin[t % 2], ("xin", t % 2)
            k0 = 0
            for sname, nch in srcs:
                S_ = K.S(sname)
                P.dma("sp", lambda e, a=a, S_=S_, k0=k0, nch=nch, c0=c0: e.dma_start(out=a[:, k0:k0 + nch, :], in_=fm_view(S_, c0, TT)),
                      writes=[ar + (k,) for k in range(k0, k0 + nch)])
                k0 += nch
            P.dma("sp", lambda e, xi=xi, c0=c0: e.dma_start(out=xi[:], in_=fm_view(XI, c0, TT)), writes=[xr + (k,) for k in range(8)])
            for oc in range(8):
                bank, bres = K.bank()
                for k in range(nk):
                    P.op("pe", lambda e, bank=bank, k=k, oc=oc, a=a: e.matmul(bank[:], lhsT=w[:, k, oc * 128:(oc + 1) * 128], rhs=a[:, k, :],
                                                                            start=(k == 0), stop=(k == nk - 1)),
                         reads=["w_o", ar + (k,)], writes=[bres])
                P.op("dve", lambda e, bank=bank, xi=xi, oc=oc: e.tensor_tensor(out=xi[:, oc, :], in0=bank[:], in1=xi[:, oc, :], op=ALU.add),
                     reads=[bres, xr + (oc,)], writes=[xr + (oc,)])
            P.dma("pool", lambda e, xi=xi, c0=c0: e.dma_start(out=fm_view(XO, c0, TT), in_=xi[:]),
                  reads=[xr + (k,) for k in range(8)], writes=[(xout_name, t)])
    K.barrier()


def phase_ffn_up(K, layer, xin_name, mout_name):
    P = K.P
    W = TT + 1
    with ExitStack() as st:
        sb = K.sballoc(st)
        w = K.load_w(sb, K.I("ffn_w_up")[layer], 1024, 2 * DFF, "w_up")
        g = K.load_vec(sb, K.I("ffn_norm")[layer], 1024, "g_ffn")
        cb = K.load_vec(sb, K.I("ffn_conv_b")[layer], 2 * DFF, "cb")
        cw = sb("cw", [128, 3, 44], F32)
        cwsrc = K.I("ffn_conv_w")[layer]
        P.dma("sp", lambda e: e.dma_start(out=cw[:], in_=cwsrc.rearrange("j (k p) -> p j k", p=128), allow_slow_non_contiguous=True), writes=["cw"])
        ones, sqb, rstd = K.norm_consts(sb)
        xin = [sb("xin%d" % i, [128, 8, TT], F32) for i in range(2)]
        hb = [sb("h%d" % i, [128, 8, TT], BF16) for i in range(2)]
        halo = sb("halo", [128, 44, 2], F32)
        hbias = [sb("hbias%d" % i, [128, 2, 44], F32) for i in range(2)]
        htmp = sb("htmp", [128, 44], F32)
        acc = {wh: [sb("acc%s%d" % (wh, i), [128, W], F32) for i in range(3)] for wh in "ag"}
        gel = [sb("gel%d" % i, [128, W], F32) for i in range(2)]
        mt = [sb("mt%d" % i, [128, 22, W + 1], BF16) for i in range(1)]
        XI, MO = K.S(xin_name), K.S(mout_name)
        tiles_per_seq = SEQ // TT
        P.dma("sp", lambda e: e.dma_start(out=xin[0][:], in_=fm_view(XI, 0, TT)), writes=[("xin", 0, k) for k in range(8)])
        K.rmsnorm(xin[0], lambda k: ("xin", 0, k), g, "g_ffn", hb[0], "h0", ones, sqb, rstd)
        for t in range(NT):
            c0 = t * TT
            first = (t % tiles_per_seq == 0)
            last = (t % tiles_per_seq == tiles_per_seq - 1)
            xi, xr = xin[t % 2], ("xin", t % 2)
            if t + 1 < NT:
                P.dma("sp", lambda e, c1=c0 + TT, xn=xin[(t + 1) % 2]: e.dma_start(out=xn[:], in_=fm_view(XI, c1, TT)),
                      writes=[("xin", (t + 1) % 2, k) for k in range(8)])
            h, hn = hb[t % 2], "h%d" % (t % 2)
            if t + 1 < NT:
                K.rmsnorm(xin[(t + 1) % 2], lambda k, b=(t + 1) % 2: ("xin", b, k), g, "g_ffn", hb[(t + 1) % 2], "h%d" % ((t + 1) % 2), ones, sqb, rstd)
            m_, mr = mt[0], ("mt", 0)
            hbt, hbr = hbias[t % 2], ("hbias", t % 2)
            if not first:
                hall = [("halo", ch) for ch in range(44)]
                P.op("dve", lambda e, hbt=hbt: e.tensor_tensor(out=hbt[:, 0, :], in0=halo[:, :, 1], in1=cw[:, 1, :], op=ALU.mult), reads=hall + ["cw"], writes=[hbr])
                P.op("dve", lambda e: e.tensor_tensor(out=htmp[:], in0=halo[:, :, 0], in1=cw[:, 0, :], op=ALU.mult), reads=hall + ["cw"], writes=["htmp"])
                P.op("dve", lambda e, hbt=hbt: e.tensor_tensor(out=hbt[:, 0, :], in0=hbt[:, 0, :], in1=htmp[:], op=ALU.add), reads=[hbr, "htmp"], writes=[hbr])
                P.op("dve", lambda e, hbt=hbt: e.tensor_tensor(out=hbt[:, 0, :], in0=hbt[:, 0, :], in1=cb[:], op=ALU.add), reads=[hbr, "cb"], writes=[hbr])
                P.op("dve", lambda e, hbt=hbt: e.tensor_tensor(out=hbt[:, 1, :], in0=halo[:, :, 1], in1=cw[:, 0, :], op=ALU.mult), reads=hall + ["cw"], writes=[hbr])
                P.op("dve", lambda e, hbt=hbt: e.tensor_tensor(out=hbt[:, 1, :], in0=hbt[:, 1, :], in1=cb[:], op=ALU.add), reads=[hbr, "cb"], writes=[hbr])
            for c in range(22):
                i = c % 3
                for wh, ch in (("a", c), ("g", c + 22)):
                    bank, bres = K.bank()
                    for k in range(8):
                        P.op("pe", lambda e, bank=bank, k=k, ch=ch, h=h: e.matmul(bank[:], lhsT=w[:, k, ch * 128:(ch + 1) * 128], rhs=h[:, k, :],
                                                                          start=(k == 0), stop=(k == 7)),
                             reads=["w_up", (hn, k)], writes=[bres])
                    ac, acr = acc[wh][i], ("acc", wh, i)
                    hres = ("halo", ch)
                    P.op("act", lambda e, ac=ac, bank=bank, ch=ch: e.activation(out=ac[:, 0:TT], in_=bank[:], func=AF.Identity, bias=cb[:, ch:ch + 1],
                                                                              scale=cw[:, 2, ch:ch + 1]),
                         reads=[bres, "cb", "cw"], writes=[acr])
                    if not first:
                        for col in range(2):
                            P.op("act", lambda e, ac=ac, bank=bank, ch=ch, col=col, hbt=hbt: e.activation(out=ac[:, col:col + 1], in_=bank[:, col:col + 1], func=AF.Identity,
                                                                                                   bias=hbt[:, col, ch:ch + 1], scale=cw[:, 2, ch:ch + 1]),
                                 reads=[bres, hbr, "cw"], writes=[acr])
                    if last:
                        P.op("act", lambda e, ac=ac, bank=bank, ch=ch: e.activation(out=ac[:, TT:W], in_=bank[:, TT - 1:TT], func=AF.Identity, bias=cb[:, ch:ch + 1],
                                                                                  scale=cw[:, 1, ch:ch + 1]),
                             reads=[bres, "cb", "cw"], writes=[acr])
                    P.op("dve", lambda e, ac=ac, bank=bank, ch=ch: e.scalar_tensor_tensor(out=ac[:, 1:TT], in0=bank[:, 0:TT - 1], scalar=cw[:, 1, ch:ch + 1], in1=ac[:, 1:TT],
                                                                                        op0=ALU.mult, op1=ALU.add), reads=[bres, "cw", acr], writes=[acr])
                    hi_ = W if last else TT
                    P.op("dve", lambda e, ac=ac, bank=bank, ch=ch, hi_=hi_: e.scalar_tensor_tensor(out=ac[:, 2:hi_], in0=bank[:, 0:hi_ - 2], scalar=cw[:, 0, ch:ch + 1],
                                                                                                 in1=ac[:, 2:hi_], op0=ALU.mult, op1=ALU.add),
                         reads=[bres, "cw", acr], writes=[acr])
                    if not last:
                        P.op("act", lambda e, bank=bank, ch=ch: e.activation(out=halo[:, ch, :], in_=bank[:, TT - 2:TT], func=AF.Copy), reads=[bres], writes=[hres])
                if last:
                    ge, ger = gel[c % 2], ("gel", c % 2)
                    P.op("act", lambda e, ge=ge, ac=acc["g"][i]: e.activation(out=ge[:], in_=ac[:], func=AF.Gelu), reads=[("acc", "g", i)], writes=[ger])
                    P.op("dve", lambda e, ge=ge, ac=acc["a"][i], c=c, m_=m_: e.tensor_tensor(out=m_[:, c, 0:W], in0=ac[:], in1=ge[:], op=ALU.mult),
                         reads=[("acc", "a", i), ger], writes=[mr + (c,)])
                else:
                    gp, gpr = K.bankbs[c % 2][:].bitcast(F32), ("psb", c % 2)
                    P.op("act", lambda e, gp=gp, ac=acc["g"][i]: e.activation(out=gp, in_=ac[:, 0:TT], func=AF.Gelu), reads=[("acc", "g", i)], writes=[gpr])
                    P.op("dve", lambda e, gp=gp, ac=acc["a"][i], c=c, m_=m_: e.tensor_tensor(out=m_[:, c, 0:TT], in0=ac[:, 0:TT], in1=gp, op=ALU.mult),
                         reads=[("acc", "a", i), gpr], writes=[mr + (c,)])
            lo = 1 if first else 0
            hi = W if last else TT
            P.dma("pool", lambda e, lo=lo, hi=hi, c0=c0, m_=m_: e.dma_start(out=MO.rearrange("(k p) t -> p k t", p=128)[:, :, c0 - 1 + lo:c0 - 1 + hi],
                                                                   in_=m_[:, :, lo:hi]),
                  reads=[mr + (c,) for c in range(22)], writes=[(mout_name, t)])
    K.barrier()


def phase_ffn_down(K, layer, min_name, xin_name, xout_name, final):
    P = K.P
    with ExitStack() as st:
        sb = K.sballoc(st)
        w = K.load_w(sb, K.I("ffn_w_down")[layer], DFF, 1024, "w_dn")
        wg = K.load_w(sb, K.I("ple_w_gate")[layer], 1024, 1024, "w_pg")
        wp = K.load_w(sb, K.I("ple_w_proj")[layer], 256, 1024, "w_pp")
        gp = K.load_vec(sb, K.I("ple_norm")[layer], 1024, "g_ple")
        if final:
            gf = K.load_vec(sb, K.I("final_norm"), 1024, "g_fin")
            hf = sb("hf", [128, 8, TT], F32)
            otok = [sb("otok%d" % i, [128, 1024], F32) for i in range(2)]
        ident = K.load_const(sb, "c_ident")
        ones, sqb, rstd = K.norm_consts(sb)
        min_ = [sb("min%d" % i, [128, 22, TT], BF16) for i in range(1)]
        xin = [sb("xin%d" % i, [128, 8, TT], F32) for i in range(1)]
        pin = [sb("pin%d" % i, [128, 4, 256], F32) for i in range(2)]
        pT = sb("pT", [128, 2, TT], BF16)
        h = sb("h", [128, 8, TT], BF16)
        sig = [sb("sig%d" % i, [128, TT], F32) for i in range(2)]
        MI, XI = K.S(min_name), K.S(xin_name)
        XO = K.S(xout_name)
        pd = K.I("p")[layer]
        no = 0
        for t in range(NT):
            c0 = t * TT
            m, mr = min_[0], ("min", 0)
            xi, xr = xin[0], ("xin", 0)
            pi, pr = pin[t % 2], ("pin", t % 2)
            P.dma("sp", lambda e, m=m, c0=c0: e.dma_start(out=m[:], in_=fm_view(MI, c0, TT)), writes=[mr])
            P.dma("sp", lambda e, xi=xi, c0=c0: e.dma_start(out=xi[:], in_=fm_view(XI, c0, TT)), writes=[xr + (k,) for k in range(8)])
            P.dma("sp", lambda e, pi=pi, c0=c0: e.dma_start(out=pi[:], in_=pd[c0:c0 + TT, :].rearrange("(n p) d -> p n d", p=128)), writes=[pr])
            for kc in range(2):
                bank, bres = K.bank()
                for n in range(4):
                    P.op("pe", lambda e, bank=bank, pi=pi, n=n, kc=kc: e.transpose(out=bank[:, n * 128:(n + 1) * 128],
                                                                                  in_=pi[:, n, kc * 128:(kc + 1) * 128], identity=ident[:]),
                         reads=[pr, "c_ident"], writes=[bres])
                K.evac(kc, pT[:, kc, :], bank[:], [bres], [("pT", kc)])
            for oc in range(8):
                bank, bres = K.bank()
                for k in range(22):
                    P.op("pe", lambda e, bank=bank, k=k, oc=oc, m=m: e.matmul(bank[:], lhsT=w[:, k, oc * 128:(oc + 1) * 128], rhs=m[:, k, :],
                                                                            start=(k == 0), stop=(k == 21)),
                         reads=["w_dn", mr], writes=[bres])
                P.op("dve", lambda e, bank=bank, xi=xi, oc=oc: e.tensor_tensor(out=xi[:, oc, :], in0=bank[:], in1=xi[:, oc, :], op=ALU.add),
                     reads=[bres, xr + (oc,)], writes=[xr + (oc,)])
            K.rmsnorm(xi, lambda k: xr + (k,), gp, "g_ple", h, "h", ones, sqb, rstd)
            for oc in range(8):
                bankg, bgres = K.bank()
                for k in range(8):
                    P.op("pe", lambda e, bankg=bankg, k=k, oc=oc: e.matmul(bankg[:], lhsT=wg[:, k, oc * 128:(oc + 1) * 128], rhs=h[:, k, :],
                                                                         start=(k == 0), stop=(k == 7)),
                         reads=["w_pg", ("h", k)], writes=[bgres])
                bankp, bpres = K.bank()
                for k in range(2):
                    P.op("pe", lambda e, bankp=bankp, k=k, oc=oc: e.matmul(bankp[:], lhsT=wp[:, k, oc * 128:(oc + 1) * 128], rhs=pT[:, k, :],
                                                                         start=(k == 0), stop=(k == 1)),
                         reads=["w_pp", ("pT", k)], writes=[bpres])
                sg, sgr = sig[oc % 2], ("sig", oc % 2)
                P.op("act", lambda e, sg=sg, bankg=bankg: e.activation(out=sg[:], in_=bankg[:], func=AF.Sigmoid), reads=[bgres], writes=[sgr])
                P.op("dve", lambda e, sg=sg, bankp=bankp: e.tensor_tensor(out=sg[:], in0=bankp[:], in1=sg[:], op=ALU.mult), reads=[bpres, sgr], writes=[sgr])
                P.op("dve", lambda e, sg=sg, xi=xi, oc=oc: e.tensor_tensor(out=xi[:, oc, :], in0=xi[:, oc, :], in1=sg[:], op=ALU.add),
                     reads=[sgr, xr + (oc,)], writes=[xr + (oc,)])
            if not final:
                P.dma("pool", lambda e, xi=xi, c0=c0: e.dma_start(out=fm_view(XO, c0, TT), in_=xi[:]),
                      reads=[xr + (k,) for k in range(8)], writes=[(xout_name, t)])
            else:
                K.rmsnorm(xi, lambda k: xr + (k,), gf, "g_fin", hf, "hf", ones, sqb, rstd)
                for n in range(4):
                    o, ores = otok[no % 2], ("otok", no % 2)
                    no += 1
                    for hb in range(2):
                        bank, bres = K.bank()
                        for kk in range(4):
                            k = hb * 4 + kk
                            P.op("pe", lambda e, bank=bank, k=k, kk=kk, n=n: e.transpose(out=bank[:, kk * 128:(kk + 1) * 128],
                                                                                          in_=hf[:, k, n * 128:(n + 1) * 128], identity=ident[:]),
                                 reads=[("hf", k), "c_ident"], writes=[bres])
                        K.evac(hb, o[:, hb * 512:(hb + 1) * 512], bank[:], [bres], [ores + (hb,)])
                    r0 = c0 + n * 128
                    P.dma("pool", lambda e, o=o, r0=r0: e.dma_start(out=XO[r0:r0 + 128, :], in_=o[:]),
                          reads=[ores + (0,), ores + (1,)], writes=[(xout_name, t, n)])
    K.barrier()


PHASES["A3"] = lambda K: phase_outproj(K, [("AT", 4), ("BT", 4)], K.I("ab_w_out")[0], 1024, "XT0", "XT1")
PHASES["A4"] = lambda K: phase_ffn_up(K, 0, "XT1", "MT0")
PHASES["A5"] = lambda K: phase_ffn_down(K, 0, "MT0", "XT1", "XT3", False)
PHASES["B3"] = lambda K: phase_outproj(K, [("RY", 16)], K.I("ret_w_out")[0], 2048, "XT3", "XT4")
PHASES["B4"] = lambda K: phase_ffn_up(K, 1, "XT4", "MT1")
PHASES["B5"] = lambda K: phase_ffn_down(K, 1, "MT1", "XT4", "OUT", True)


def phase_na(K):
    P = K.P
    NEG = -30000.0
    NB = 4
    with ExitStack() as st:
        sb = K.sballoc(st)
        ident = K.load_const(sb, "c_ident")
        identb = K.load_const(sb, "c_identb")
        KTs = sb("KTs", [64, 8, SEQ], BF16)
        Vs = sb("Vs", [64, 64, 512], BF16)
        Qb = [sb("Qb%d" % i, [64, 8, 512], BF16) for i in range(2)]
        Bf = sb("Bf", [128, 4 * 960], F32)
        sc = [sb("sc%d" % i, [128, 512], F32) for i in range(NB)]
        pr = [sb("pr%d" % i, [128, 512], BF16) for i in range(NB)]
        prT = [sb("prT%d" % i, [64, 8, 128], BF16) for i in range(3)]
        stt = sb("stt", [128, NB, 4], F32)
        aout = [sb("aout%d" % i, [128, 256], F32) for i in range(2)]
        aT = [sb("aT%d" % i, [64, 8, 512], BF16) for i in range(1)]
        QT, KT, VN, AT = K.S("QT"), K.S("KT"), K.S("VN"), K.S("AT")
        hv = lambda ap, c0, n: ap.rearrange("(h p) t -> p h t", p=64)[:, :, c0:c0 + n]
        rpb = K.I("na_rpb")[0]
        P.op("pool", lambda e: e.memset(Bf[:], NEG), writes=["Bf"])
        Bf4 = Bf[:].rearrange("p (h r c) -> p h r c", h=4, r=15)
        for hp in range(2):
            for c in range(64):
                cs = min(max(c - 8, 0), 48)
                dcs = cs - c + 15
                P.dma("sp", lambda e, c=c, cs=cs, dcs=dcs, hp=hp: e.dma_start(out=Bf4[hp * 64 + c:hp * 64 + c + 1, :, :, cs:cs + 16],
                                                                             in_=rpb[hp::2, :, dcs:dcs + 16][None]),
                      reads=[], writes=["Bf"])
        cnt = 0
        nT = 0
        for s_ in range(NSEQ):
            tb = s_ * SEQ
            P.dma("sp", lambda e, tb=tb: e.dma_start(out=KTs[:], in_=hv(KT, tb, SEQ)), writes=["KTs"])
            for rq in range(4):
                P.dma("sp", lambda e, tb=tb, rq=rq: e.dma_start(out=Vs[:, rq * 16:(rq + 1) * 16, :],
                                                              in_=VN[tb + rq * 1024:tb + (rq + 1) * 1024, :].rearrange("(r c) f -> c r f", c=64)),
                      writes=[("Vs", rq)])
            for r in range(64):
                rr = r % 8
                qb, qbr = Qb[(r // 8) % 2], ("Qb", (r // 8) % 2)
                if rr == 0:
                    P.dma("sp", lambda e, qb=qb, c0=tb + r * 64: e.dma_start(out=qb[:], in_=hv(QT, c0, 512)), writes=[qbr])
                rs = min(max(r - 4, 0), 56)
                dr0 = rs - r + 7
                bankO, bOres = K.bank()
                ao, aor = aout[r % 2], ("aout", r % 2)
                for pp in range(4):
                    i2 = cnt % NB
                    i3 = cnt % 3
                    ib = cnt % 2
                    cnt += 1
                    bankS, bSres = K.bank()
                    for hp in range(2):
                        hd = 2 * pp + hp
                        P.op("pe", lambda e, bankS=bankS, hd=hd, hp=hp, rr=rr, rs=rs, qb=qb: e.matmul(bankS[hp * 64:(hp + 1) * 64, :], lhsT=qb[:, hd, rr * 64:(rr + 1) * 64],
                                                                                                 rhs=KTs[:, hd, rs * 64:rs * 64 + 512], start=True, stop=True,
                                                                                                 tile_position=(0, hp * 64)),
                             reads=[qbr, "KTs"], writes=[bSres])
                    b0 = pp * 960 + dr0 * 64
                    P.op("dve", lambda e, bankS=bankS, i2=i2, b0=b0: e.scalar_tensor_tensor(out=sc[i2][:], in0=bankS[:], scalar=0.125, in1=Bf[:, b0:b0 + 512],
                                                                                          op0=ALU.mult, op1=ALU.add),
                         reads=[bSres, "Bf"], writes=[("sc", i2)])
                    P.op("dve", lambda e, i2=i2: e.tensor_reduce(out=stt[:, i2, 0:1], in_=sc[i2][:], axis=AX.X, op=ALU.max, negate=True),
                         reads=[("sc", i2)], writes=[("nmx", i2)])
                    P.op("act", lambda e, i2=i2: e.activation(out=pr[i2][:], in_=sc[i2][:], func=AF.Exp, bias=stt[:, i2, 0:1], scale=1.0,
                                                             accum_out=stt[:, i2, 1:2]),
                         reads=[("sc", i2), ("nmx", i2)], writes=[("pr", i2), ("rsum", i2)])
                    bb = K.bankbs[ib]
                    for i in range(8):
                        P.op("pe", lambda e, i=i, i2=i2, bb=bb: e.transpose(out=bb[0:64, i * 128:(i + 1) * 128], in_=pr[i2][:, i * 64:(i + 1) * 64], identity=identb[:]),
                             reads=[("pr", i2), "c_identb"], writes=[("psb", ib)])
                    K.evac(cnt, prT[i3][:], bb[0:64, :].rearrange("p (i q) -> p i q", i=8), [("psb", ib)], [("prT", i3)])
                    for hp in range(2):
                        hd = 2 * pp + hp
                        for i in range(8):
                            P.op("pe", lambda e, bankO=bankO, i=i, i3=i3, hd=hd, hp=hp, pp=pp, rs=rs: e.matmul(bankO[hp * 64:(hp + 1) * 64, pp * 64:(pp + 1) * 64],
                                                                                                          lhsT=prT[i3][:, i, hp * 64:(hp + 1) * 64],
                                                                                                          rhs=Vs[:, rs + i, hd * 64:(hd + 1) * 64], start=(i == 0), stop=(i == 7),
                                                                                                          tile_position=(0, hp * 64)),
                                 reads=[("prT", i3), ("Vs", (rs + i) // 16)], writes=[bOres])
                    P.op("dve", lambda e, i2=i2: e.reciprocal(out=stt[:, i2, 2:3], in_=stt[:, i2, 1:2]), reads=[("rsum", i2)], writes=[("rinv", i2)])
                    P.op("act", lambda e, bankO=bankO, ao=ao, pp=pp, i2=i2: e.activation(out=ao[:, pp * 64:(pp + 1) * 64], in_=bankO[:, pp * 64:(pp + 1) * 64],
                                                                                        func=AF.Copy, scale=stt[:, i2, 2:3]),
                         reads=[bOres, ("rinv", i2)], writes=[aor])
                a, ar = aT[0], ("aT", 0)
                bankT, bTres = K.bank()
                for pp in range(4):
                    P.op("pe", lambda e, bankT=bankT, ao=ao, pp=pp: e.transpose(out=bankT[0:64, pp * 128:(pp + 1) * 128], in_=ao[:, pp * 64:(pp + 1) * 64], identity=ident[:]),
                         reads=[aor, "c_ident"], writes=[bTres])
                K.evac(r, a[:, :, rr * 64:(rr + 1) * 64], bankT[0:64, :].rearrange("p (h q) -> p h q", h=8), [bTres], [ar + (rr,)])
                if rr == 7:
                    c0 = tb + (r - 7) * 64
                    P.dma("pool", lambda e, a=a, c0=c0: e.dma_start(out=hv(AT, c0, 512), in_=a[:]),
                          reads=[ar + (q,) for q in range(8)], writes=[("AT", c0)])
    K.barrier()


PHASES["A1"] = phase_na


def phase_ret_in(K):
    P = K.P
    with ExitStack() as st:
        sb = K.sballoc(st)
        w = K.load_w(sb, K.I("ret_w_in")[0], 1024, 6144, "w_ri")
        g = K.load_vec(sb, K.I("ret_norm")[0], 1024, "g_ret")
        identb = K.load_const(sb, "c_identb")
        ones, sqb, rstd = K.norm_consts(sb)
        xin = sb("xin", [128, 8, TT], F32)
        h = sb("h", [128, 8, TT], BF16)
        cs_ = [sb("cos%d" % i, [128, TT], F32) for i in range(2)]
        sn_ = [sb("sin%d" % i, [128, TT], F32) for i in range(2)]
        t1 = [sb("t1_%d" % i, [128, TT], F32) for i in range(2)]
        t2 = [sb("t2_%d" % i, [128, TT], F32) for i in range(2)]
        qk = [sb("qk%d" % i, [128, 8, TT], BF16) for i in range(2)]
        ktok = [sb("ktok%d" % i, [128, 1024], BF16) for i in range(2)]
        vtok = [sb("vtok%d" % i, [128, 2048], BF16) for i in range(2)]
        XI = K.S("XT3")
        RQ, RK, RKN, RV, RG = K.S("RQ"), K.S("RK"), K.S("RKN"), K.S("RV"), K.S("RG")
        ccos, csin = K.I("c_cos"), K.I("c_sin")
        nrot = 0
        nvt = 0
        nkt = 0
        for t in range(NT):
            c0 = t * TT
            pos0 = c0 % SEQ
            cs, sn = cs_[t % 2], sn_[t % 2]
            P.dma("sp", lambda e, c0=c0: e.dma_start(out=xin[:], in_=fm_view(XI, c0, TT)), writes=[("xin", k) for k in range(8)])
            P.dma("sp", lambda e, cs=cs, pos0=pos0: e.dma_start(out=cs[:], in_=ccos[:, pos0:pos0 + TT]), writes=[("cos", t % 2)])
            P.dma("sp", lambda e, sn=sn, pos0=pos0: e.dma_start(out=sn[:], in_=csin[:, pos0:pos0 + TT]), writes=[("sin", t % 2)])
            K.rmsnorm(xin, lambda k: ("xin", k), g, "g_ret", h, "h", ones, sqb, rstd)
            for qi, (dst, dn, scl) in enumerate(((RQ, "RQ", 1.0), (RK, "RK", 0.0625))):
                o, ores = qk[qi], ("qk", qi)
                for hh in range(4):
                    bk = []
                    for half in range(2):
                        bank, bres = K.bank()
                        cc = qi * 1024 + hh * 256 + half * 128
                        for k in range(8):
                            P.op("pe", lambda e, bank=bank, k=k, cc=cc: e.matmul(bank[:], lhsT=w[:, k, cc:cc + 128], rhs=h[:, k, :], start=(k == 0), stop=(k == 7)),
                                 reads=["w_ri", ("h", k)], writes=[bres])
                        bk.append((bank, bres))
                    (b1, b1r), (b2, b2r) = bk
                    for half, (A, B, op) in enumerate((((b1, b1r, cs, ("cos", t % 2)), (b2, b2r, sn, ("sin", t % 2)), ALU.subtract),
                                                       ((b1, b1r, sn, ("sin", t % 2)), (b2, b2r, cs, ("cos", t % 2)), ALU.add))):
                        i2 = nrot % 2
                        nrot += 1
                        P.op("dve", lambda e, A=A, i2=i2, scl=scl: e.scalar_tensor_tensor(out=t1[i2][:], in0=A[0][:], scalar=scl, in1=A[2][:], op0=ALU.mult, op1=ALU.mult),
                             reads=[A[1], A[3]], writes=[("t1", i2)])
                        P.op("dve", lambda e, B=B, i2=i2, scl=scl: e.scalar_tensor_tensor(out=t2[i2][:], in0=B[0][:], scalar=scl, in1=B[2][:], op0=ALU.mult, op1=ALU.mult),
                             reads=[B[1], B[3]], writes=[("t2", i2)])
                        P.op("pool", lambda e, i2=i2, o=o, hh=hh, half=half, op=op: e.tensor_tensor(out=o[:, hh * 2 + half, :], in0=t1[i2][:], in1=t2[i2][:], op=op),
                             reads=[("t1", i2), ("t2", i2)], writes=[ores + (hh * 2 + half,)])
                P.dma("pool", lambda e, dst=dst, o=o, c0=c0: e.dma_start(out=fm_view(dst, c0, TT), in_=o[:]),
                      reads=[ores + (k,) for k in range(8)], writes=[(dn, t)])
            for sub in range(4):
                for k in range(8):
                    P.op("pe", lambda e, k=k, sub=sub: e.transpose(out=K.bankb[:, k * 128:(k + 1) * 128], in_=qk[1][:, k, sub * 128:(sub + 1) * 128], identity=identb[:]),
                         reads=[("qk", 1, k), "c_identb"], writes=[("psb", 0)])
                kt, ktr = ktok[nkt % 2], ("ktok", nkt % 2)
                nkt += 1
                K.evac(sub, kt[:], K.bankb[:], [("psb", 0)], [ktr])
                r0 = c0 + sub * 128
                P.dma("pool", lambda e, kt=kt, r0=r0: e.dma_start(out=RKN[r0:r0 + 128, :], in_=kt[:]), reads=[ktr], writes=[("RKN", r0)])
            for which, (dst, dn, colb) in enumerate(((RV, "RV", 2048), (RG, "RG", 4096))):
                for sub in range(4):
                    vt, vtr = vtok[nvt % 2], ("vtok", nvt % 2)
                    nvt += 1
                    for cbk in range(4):
                        bank, bres = K.bank()
                        cc = colb + cbk * 512
                        for k in range(8):
                            P.op("pe", lambda e, bank=bank, k=k, sub=sub, cc=cc: e.matmul(bank[:], lhsT=h[:, k, sub * 128:(sub + 1) * 128], rhs=w[:, k, cc:cc + 512],
                                                                                     start=(k == 0), stop=(k == 7)),
                                 reads=["w_ri", ("h", k)], writes=[bres])
                        if which == 0:
                            K.evac(cbk, vt[:, cbk * 512:(cbk + 1) * 512], bank[:], [bres], [vtr + (cbk,)])
                        else:
                            P.op("act", lambda e, vt=vt, bank=bank, cbk=cbk: e.activation(out=vt[:, cbk * 512:(cbk + 1) * 512], in_=bank[:], func=AF.Silu),
                                 reads=[bres], writes=[vtr + (cbk,)])
                    r0 = c0 + sub * 128
                    P.dma("pool", lambda e, dst=dst, vt=vt, r0=r0: e.dma_start(out=dst[r0:r0 + 128, :], in_=vt[:]),
                          reads=[vtr + (q,) for q in range(4)], writes=[(dn, r0)])
    K.barrier()


def phase_ret(K):
    P = K.P
    L = 128
    NCH = SEQ // L
    with ExitStack() as st:
        sb = K.sballoc(st)
        ident_b = K.load_const(sb, "c_identb")
        cD1, cM1, cD2, cM2 = [K.load_const(sb, n) for n in ("c_D1", "c_M1", "c_D2", "c_M2")]
        io1, io2, pidx = [K.load_const(sb, n) for n in ("c_iota1", "c_iota2", "c_pidx")]
        lg = sb("lg", [128, 8], F32)
        dsrc = K.I("ret_decay")[0].rearrange("a b -> (a b)").partition_broadcast(128)
        P.dma("sp", lambda e: e.dma_start(out=lg[:], in_=dsrc), writes=["lg"])
        P.op("act", lambda e: e.activation(out=lg[:], in_=lg[:], func=AF.Exp), reads=["lg"], writes=["lg"])
        P.op("dve", lambda e: e.tensor_scalar(out=lg[:], in0=lg[:], scalar1=-1.0, scalar2=None, op0=ALU.mult), reads=["lg"], writes=["lg"])
        decT = sb("decT", [128, 4, 128], F32)
        tmpd = sb("tmpd", [128, 128], F32)
        qd = sb("qd", [128, 2, 4, 128], F32)
        kd = sb("kd", [128, 2, 4], F32)
        cd = sb("cd", [128, 2, 4], F32)
        for hh in range(4):
            lf, lb = lg[:, hh:hh + 1], lg[:, 4 + hh:5 + hh]
            P.op("act", lambda e, hh=hh, lf=lf: e.activation(out=decT[:, hh, :], in_=cD1[:], func=AF.Exp, scale=lf), reads=["lg", "c_D1"], writes=[("decT", hh)])
            P.op("dve", lambda e, hh=hh: e.tensor_tensor(out=decT[:, hh, :], in0=decT[:, hh, :], in1=cM1[:], op=ALU.mult), reads=[("decT", hh), "c_M1"], writes=[("decT", hh)])
            P.op("act", lambda e, lb=lb: e.activation(out=tmpd[:], in_=cD2[:], func=AF.Exp, scale=lb), reads=["lg", "c_D2"], writes=["tmpd"])
            P.op("dve", lambda e: e.tensor_tensor(out=tmpd[:], in0=tmpd[:], in1=cM2[:], op=ALU.mult), reads=["tmpd", "c_M2"], writes=["tmpd"])
            P.op("dve", lambda e, hh=hh: e.tensor_tensor(out=decT[:, hh, :], in0=decT[:, hh, :], in1=tmpd[:], op=ALU.add), reads=[("decT", hh), "tmpd"], writes=[("decT", hh)])
            P.op("act", lambda e, hh=hh, lf=lf: e.activation(out=qd[:, 0, hh, :], in_=io1[:], func=AF.Exp, scale=lf), reads=["lg", "c_iota1"], writes=["qd"])
            P.op("act", lambda e, hh=hh, lb=lb: e.activation(out=qd[:, 1, hh, :], in_=io2[:], func=AF.Exp, scale=lb), reads=["lg", "c_iota2"], writes=["qd"])
            P.op("act", lambda e, hh=hh, lf=lf: e.activation(out=kd[:, 0, hh:hh + 1], in_=pidx[:, 0:1], func=AF.Exp, scale=lf), reads=["lg", "c_pidx"], writes=["kd"])
            P.op("act", lambda e, hh=hh, lb=lb: e.activation(out=kd[:, 1, hh:hh + 1], in_=pidx[:, 1:2], func=AF.Exp, scale=lb), reads=["lg", "c_pidx"], writes=["kd"])
            P.op("act", lambda e, hh=hh: e.activation(out=cd[:, 0, hh:hh + 1], in_=lg[:, hh:hh + 1], func=AF.Exp, scale=float(L)), reads=["lg"], writes=["cd"])
            P.op("act", lambda e, hh=hh: e.activation(out=cd[:, 1, hh:hh + 1], in_=lg[:, 4 + hh:5 + hh], func=AF.Exp, scale=float(L)), reads=["lg"], writes=["cd"])
        qcol = sb("qcol", [128, 2, 4], F32)
        for hh in range(4):
            P.op("act", lambda e, hh=hh: e.activation(out=qcol[:, 0, hh:hh + 1], in_=pidx[:, 1:2], func=AF.Exp, scale=lg[:, hh:hh + 1], bias=lg[:, hh:hh + 1]),
                 reads=["lg", "c_pidx"], writes=["qcol"])
            P.op("act", lambda e, hh=hh: e.activation(out=qcol[:, 1, hh:hh + 1], in_=pidx[:, 0:1], func=AF.Exp, scale=lg[:, 4 + hh:5 + hh], bias=lg[:, 4 + hh:5 + hh]),
                 reads=["lg", "c_pidx"], writes=["qcol"])
        S32 = sb("S32", [128, 8, 512], F32)
        S16 = sb("S16", [128, 8, 512], BF16)
        Qc = [sb("Qc%d" % i, [128, 8, L], BF16) for i in range(2)]
        Kc = [sb("Kc%d" % i, [128, 8, L], BF16) for i in range(2)]
        Kn = [sb("Kn%d" % i, [128, 1024], BF16) for i in range(2)]
        Vn = [sb("Vn%d" % i, [128, 2048], BF16) for i in range(2)]
        Gn = [sb("Gn%d" % i, [128, 2048], BF16) for i in range(2)]
        Yn = [sb("Yn%d" % i, [128, 2048], F32) for i in range(2)]
        Qd = [sb("Qd%d" % i, [128, 2, L], BF16) for i in range(2)]
        Kd = [sb("Kd%d" % i, [128, 256], BF16) for i in range(2)]
        STt = [sb("ST%d" % i, [128, L], BF16) for i in range(2)]
        o32 = [sb("o32_%d" % i, [128, 512], F32) for i in range(2)]
        junk = sb("junk", [128, 512], F32)
        hst = sb("hst", [128, 2, 2], F32)
        ytok = [sb("ytok%d" % i, [128, 2048], BF16) for i in range(2)]
        yT = [sb("yT%d" % i, [128, 16, 512], BF16) for i in range(2)]
        RQ, RK, RKN, RV, RG, YB, RY = [K.S(n) for n in ("RQ", "RK", "RKN", "RV", "RG", "YB", "RY")]
        it = 0
        nh = 0
        for s in range(NSEQ):
            tb = s * SEQ
            for sweep in (1, 0):
                d = sweep
                P.op("pool", lambda e: e.memset(S32[:], 0.0), writes=[("S32", q) for q in range(8)])
                P.op("pool", lambda e: e.memset(S16[:], 0.0), writes=[("S16", q) for q in range(8)])
                order = range(NCH - 1, -1, -1) if sweep == 1 else range(NCH)
                for n in order:
                    tc = tb + n * L
                    b2 = it % 2
                    it += 1
                    qc, qcr = Qc[b2], ("Qc", b2)
                    kn, knr = Kn[b2], ("Kn", b2)
                    vn, vnr = Vn[b2], ("Vn", b2)
                    P.dma("sp", lambda e, qc=qc, tc=tc: e.dma_start(out=qc[:], in_=fm_view(RQ, tc, L)), writes=[qcr])
                    P.dma("sp", lambda e, kn=kn, tc=tc: e.dma_start(out=kn[:], in_=RKN[tc:tc + L, :]), writes=[knr])
                    P.dma("sp", lambda e, vn=vn, tc=tc: e.dma_start(out=vn[:], in_=RV[tc:tc + L, :]), writes=[vnr])
                    if sweep == 0:
                        kc, kcr = Kc[b2], ("Kc", b2)
                        gn, gnr = Gn[b2], ("Gn", b2)
                        yn, ynr = Yn[b2], ("Yn", b2)
                        P.dma("sp", lambda e, kc=kc, tc=tc: e.dma_start(out=kc[:], in_=fm_view(RK, tc, L)), writes=[kcr])
                        P.dma("sp", lambda e, gn=gn, tc=tc: e.dma_start(out=gn[:], in_=RG[tc:tc + L, :]), writes=[gnr])
                        P.dma("sp", lambda e, yn=yn, tc=tc: e.dma_start(out=yn[:], in_=YB[tc:tc + L, :]), reads=[("YB", s, n)], writes=[ynr + (q,) for q in range(4)])
                        yt, ytr = ytok[b2], ("ytok", b2)
                    else:
                        yn, ynr = Yn[b2], ("Yn", b2)
                    for hh in range(4):
                        h2 = nh % 2
                        nh += 1
                        qdt, qdr = Qd[h2], ("Qd", h2)
                        kdt, kdr = Kd[h2], ("Kd", h2)
                        P.op("act", lambda e, kdt=kdt, kn=kn, hh=hh, d=d: e.activation(out=kdt[:], in_=kn[:, hh * 256:(hh + 1) * 256], func=AF.Copy, scale=kd[:, d, hh:hh + 1]),
                             reads=[knr, "kd"], writes=[kdr])
                        bankC, bCres = K.bank()
                        for dc in range(2):
                            P.op("pe", lambda e, bankC=bankC, qc=qc, dc=dc, hh=hh: e.matmul(bankC[:], lhsT=qc[:, 2 * hh + dc, :], rhs=S16[:, hh * 2 + dc, :],
                                                                                       start=(dc == 0), stop=(dc == 1)),
                                 reads=[qcr, ("S16", hh * 2 + dc)], writes=[bCres])
                        if sweep == 1:
                            P.op("act", lambda e, bankC=bankC, yn=yn, hh=hh: e.activation(out=yn[:, hh * 512:(hh + 1) * 512], in_=bankC[:], func=AF.Copy, scale=qcol[:, 1, hh:hh + 1]),
                                 reads=[bCres, "qcol"], writes=[ynr + (hh,)])
                        else:
                            bankS, bSres = K.bank()
                            for dc in range(2):
                                P.op("pe", lambda e, bankS=bankS, kc=kc, qc=qc, dc=dc, hh=hh: e.matmul(bankS[:, 0:L], lhsT=kc[:, 2 * hh + dc, :], rhs=qc[:, 2 * hh + dc, :],
                                                                                                  start=(dc == 0), stop=(dc == 1)),
                                     reads=[kcr, qcr], writes=[bSres])
                            stt_, strr = STt[h2], ("ST", h2)
                            P.op("dve", lambda e, stt_=stt_, bankS=bankS, hh=hh: e.tensor_tensor(out=stt_[:], in0=bankS[:, 0:L], in1=decT[:, hh, :], op=ALU.mult),
                                 reads=[bSres, ("decT", hh)], writes=[strr])
                            bankO, bOres = K.bank()
                            P.op("pe", lambda e, bankO=bankO, stt_=stt_, vn=vn, hh=hh: e.matmul(bankO[:], lhsT=stt_[:], rhs=vn[:, hh * 512:(hh + 1) * 512], start=True, stop=True),
                                 reads=[strr, vnr], writes=[bOres])
                            ot, otr = o32[h2], ("o32", h2)
                            P.op("dve", lambda e, ot=ot, bankC=bankC, yn=yn, hh=hh: e.scalar_tensor_tensor(out=ot[:], in0=bankC[:], scalar=qcol[:, 0, hh:hh + 1],
                                                                                                          in1=yn[:, hh * 512:(hh + 1) * 512], op0=ALU.mult, op1=ALU.add),
                                 reads=[bCres, ynr + (hh,), "qcol"], writes=[otr])
                            P.op("dve", lambda e, ot=ot, bankO=bankO: e.tensor_tensor(out=ot[:], in0=bankO[:], in1=ot[:], op=ALU.add),
                                 reads=[bOres, otr], writes=[otr])
                            P.op("act", lambda e, ot=ot, h2=h2: e.activation(out=junk[:], in_=ot[:], func=AF.Square, accum_out=hst[:, h2, 0:1]),
                                 reads=[otr], writes=["junk", ("hss", h2)])
                            P.op("act", lambda e, h2=h2: e.activation(out=hst[:, h2, 1:2], in_=hst[:, h2, 0:1], func=AF.Sqrt, bias=EPS, scale=1.0 / 512),
                                 reads=[("hss", h2)], writes=[("hrs", h2)])
                            P.op("dve", lambda e, h2=h2: e.reciprocal(out=hst[:, h2, 1:2], in_=hst[:, h2, 1:2]), reads=[("hrs", h2)], writes=[("hrs", h2)])
                            P.op("dve", lambda e, ot=ot, yt=yt, gn=gn, hh=hh, h2=h2: e.scalar_tensor_tensor(out=yt[:, hh * 512:(hh + 1) * 512], in0=ot[:], scalar=hst[:, h2, 1:2],
                                                                                                           in1=gn[:, hh * 512:(hh + 1) * 512], op0=ALU.mult, op1=ALU.mult),
                                 reads=[otr, ("hrs", h2), gnr], writes=[ytr + (hh,)])
                        for dc in range(2):
                            bankK, bKres = K.bank()
                            q = hh * 2 + dc
                            P.op("pe", lambda e, bankK=bankK, kdt=kdt, vn=vn, dc=dc, hh=hh: e.matmul(bankK[:], lhsT=kdt[:, dc * 128:(dc + 1) * 128], rhs=vn[:, hh * 512:(hh + 1) * 512],
                                                                                                 start=True, stop=True),
                                 reads=[kdr, vnr], writes=[bKres])
                            P.op("dve", lambda e, bankK=bankK, q=q, hh=hh, d=d: e.scalar_tensor_tensor(out=S32[:, q, :], in0=S32[:, q, :], scalar=cd[:, d, hh:hh + 1], in1=bankK[:],
                                                                                                      op0=ALU.mult, op1=ALU.add),
                                 reads=[bKres, ("S32", q), "cd"], writes=[("S32", q)])
                            P.op("act", lambda e, q=q: e.activation(out=S16[:, q, :], in_=S32[:, q, :], func=AF.Copy), reads=[("S32", q)], writes=[("S16", q)])
                    if sweep == 1:
                        P.dma("pool", lambda e, yn=yn, tc=tc: e.dma_start(out=YB[tc:tc + L, :], in_=yn[:]),
                              reads=[ynr + (q,) for q in range(4)], writes=[("YB", s, n)])
                    else:
                        ytile, ytiler = yT[(n // 4) % 2], ("yT", (n // 4) % 2)
                        for half in range(2):
                            for f8 in range(8):
                                f = half * 8 + f8
                                P.op("pe", lambda e, yt=yt, f=f, f8=f8: e.transpose(out=K.bankb[:, f8 * 128:(f8 + 1) * 128], in_=yt[:, f * 128:(f + 1) * 128], identity=ident_b[:]),
                                     reads=[ytr + (f // 4,), "c_identb"], writes=[("psb", 0)])
                            K.evac(half, ytile[:, half * 8:(half + 1) * 8, (n % 4) * L:(n % 4 + 1) * L], K.bankb[:].rearrange("p (f t) -> p f t", f=8),
                                   [("psb", 0)], [ytiler + (n % 4, half)])
                        if n % 4 == 3:
                            c0 = tb + (n - 3) * L
                            P.dma("pool", lambda e, ytile=ytile, c0=c0: e.dma_start(out=fm_view(RY, c0, 512), in_=ytile[:]),
                                  reads=[ytiler + (q, hf) for q in range(4) for hf in range(2)], writes=[("RY", c0)])
    K.barrier()


PHASES["B0"] = phase_ret_in
PHASES["B1"] = phase_ret


def phase_s5(K):
    P = K.P
    PI = float(np.pi)
    with ExitStack() as st:
        sb = K.sballoc(st)
        T = sb("T", [128, 32, 128], BF16)
        GS = sb("GS", [128, 32, 2, 128], BF16)
        H = sb("H", [128, 2, 32, 2, 128], BF16)
        A1 = sb("A1", [128, 2, 2, 16], F32)
        Bm = sb("Bm", [128, 2, 2, 16], F32)
        NSEG, LS = 16, 32
        ALs = sb("ALs", [128, 2, 2, 16], F32)
        BLs = sb("BLs", [128, 2, 2, 16], F32)
        PA = sb("PA", [128, 2, 2, 16, LS], BF16)
        PB = sb("PB", [128, 2, 2, 16, LS], BF16)
        ident = K.load_const(sb, "c_ident")
        bglu = K.load_vec(sb, K.I("s5_b_glu")[0], 512, "b_glu")
        with ExitStack() as st2:
            sb2 = K.sballoc(st2)
            wglu = K.load_w(sb, K.I("s5_w_glu")[0], 512, 512, "w_glu", CB=512, stg_sb=sb2)
            ld = lambda n: K.load_const(sb2, n)
            lre, lim, ldt, bre, bim, cre, cim, dtl = [ld(n) for n in ("l_lre", "l_lim", "l_ldt", "l_bre", "l_bim", "l_cre", "l_cim", "l_d")]
            maskF, maskB = ld("c_maskF"), ld("c_maskB")
            allc = ["l_lre", "l_lim", "l_ldt", "l_bre", "l_bim", "l_cre", "l_cim", "l_d", "c_maskF", "c_maskB", "c_ident"]
            first = [True]

            def D(fn, eng="dve"):
                P.op(eng, fn, reads=["prep"] + (allc if first[0] else []), writes=["prep"])
                first[0] = False

            def v(name, shape=(128, 32), dt=F32):
                return sb2(name, list(shape), dt)
            dt_, xr, xi, mag, imag = v("dt"), v("xr"), v("xi"), v("mag"), v("imag")
            sn, cs, q, r, m = v("sn"), v("cs"), v("q"), v("r"), v("m")
            qi = v("qi", dt=I32)
            D(lambda e: e.activation(out=dt_[:], in_=ldt[:], func=AF.Exp), "act")
            D(lambda e: e.tensor_tensor(out=xr[:], in0=lre[:], in1=dt_[:], op=ALU.mult))
            D(lambda e: e.tensor_tensor(out=xi[:], in0=lim[:], in1=dt_[:], op=ALU.mult))
            D(lambda e: e.activation(out=mag[:], in_=xr[:], func=AF.Exp), "act")
            D(lambda e: e.activation(out=imag[:], in_=xr[:], func=AF.Exp, scale=-1.0), "act")
            for off, dst in ((0.0, sn), (PI / 2, cs)):
                D(lambda e, off=off: e.tensor_scalar(out=r[:], in0=xi[:], scalar1=off, scalar2=None, op0=ALU.add))
                D(lambda e: e.tensor_scalar(out=q[:], in0=r[:], scalar1=1.0 / (2 * PI), scalar2=0.5, op0=ALU.mult, op1=ALU.add))
                D(lambda e: e.tensor_copy(out=qi[:], in_=q[:]))
                D(lambda e: e.tensor_copy(out=q[:], in_=qi[:]))
                D(lambda e: e.scalar_tensor_tensor(out=r[:], in0=q[:], scalar=-2 * PI, in1=r[:], op0=ALU.mult, op1=ALU.add))
                D(lambda e: e.tensor_scalar(out=m[:], in0=r[:], scalar1=-PI, scalar2=2 * PI, op0=ALU.is_lt, op1=ALU.mult))
                D(lambda e: e.tensor_tensor(out=r[:], in0=r[:], in1=m[:], op=ALU.add))
                D(lambda e: e.tensor_scalar(out=m[:], in0=r[:], scalar1=PI, scalar2=-2 * PI, op0=ALU.is_gt, op1=ALU.mult))
                D(lambda e: e.tensor_tensor(out=r[:], in0=r[:], in1=m[:], op=ALU.add))
                D(lambda e: e.tensor_scalar(out=r[:], in0=r[:], scalar1=-3.1415925, scalar2=3.1415925, op0=ALU.max, op1=ALU.min))
                D(lambda e, dst=dst: e.activation(out=dst[:], in_=r[:], func=AF.Sin), "act")
            pwr, pwi, ipr, ipi = [v(n, (128, 32, 9)) for n in ("pwr", "pwi", "ipr", "ipi")]
            t1, t2 = v("t1"), v("t2")
            for (pr_, pi_, mg, sgn) in ((pwr, pwi, mag, 1.0), (ipr, ipi, imag, -1.0)):
                D(lambda e, pr_=pr_: e.memset(pr_[:, :, 0:1], 1.0))
                D(lambda e, pi_=pi_: e.memset(pi_[:, :, 0:1], 0.0))
                D(lambda e, pr_=pr_, mg=mg: e.tensor_tensor(out=pr_[:, :, 1], in0=mg[:], in1=cs[:], op=ALU.mult))
                D(lambda e, pi_=pi_, mg=mg, sgn=sgn: e.scalar_tensor_tensor(out=pi_[:, :, 1], in0=mg[:], scalar=sgn, in1=sn[:], op0=ALU.mult, op1=ALU.mult))
                for k in range(2, 9):
                    D(lambda e, pr_=pr_, k=k: e.tensor_tensor(out=t1[:], in0=pr_[:, :, k - 1], in1=pr_[:, :, 1], op=ALU.mult))
                    D(lambda e, pi_=pi_, k=k: e.tensor_tensor(out=t2[:], in0=pi_[:, :, k - 1], in1=pi_[:, :, 1], op=ALU.mult))
                    D(lambda e, pr_=pr_, k=k: e.tensor_tensor(out=pr_[:, :, k], in0=t1[:], in1=t2[:], op=ALU.subtract))
                    D(lambda e, pr_=pr_, pi_=pi_, k=k: e.tensor_tensor(out=t1[:], in0=pr_[:, :, k - 1], in1=pi_[:, :, 1], op=ALU.mult))
                    D(lambda e, pr_=pr_, pi_=pi_, k=k: e.tensor_tensor(out=t2[:], in0=pi_[:, :, k - 1], in1=pr_[:, :, 1], op=ALU.mult))
                    D(lambda e, pi_=pi_, k=k: e.tensor_tensor(out=pi_[:, :, k], in0=t1[:], in1=t2[:], op=ALU.add))
            for gh in range(2):
                for ri in range(2):
                    D(lambda e, gh=gh, ri=ri: e.tensor_copy(out=A1[:, gh, ri, :], in_=pwr[:, gh * 16:(gh + 1) * 16, 8]))
                    D(lambda e, gh=gh, ri=ri: e.tensor_scalar(out=Bm[:, gh, ri, :], in0=pwi[:, gh * 16:(gh + 1) * 16, 8], scalar1=(-1.0 if ri == 0 else 1.0),
                                                               scalar2=None, op0=ALU.mult))
            ur, ui = v("ur"), v("ui")
            D(lambda e: e.tensor_copy(out=ur[:], in_=pwr[:, :, 8]))
            D(lambda e: e.tensor_copy(out=ui[:], in_=pwi[:, :, 8]))
            g2 = lambda a, sl: a[sl].rearrange("p (a b) -> p a b", a=2)
            for k in range(LS):
                for d in range(2):
                    sl = slice(d * 64, (d + 1) * 64)
                    mi = k
                    for ri in range(2):
                        D(lambda e, sl=sl, mi=mi, ri=ri: e.tensor_copy(out=PA[sl, :, ri, :, mi], in_=g2(ur, sl)))
                        D(lambda e, sl=sl, mi=mi, ri=ri: e.tensor_scalar(out=PB[sl, :, ri, :, mi], in0=g2(ui, sl), scalar1=(-1.0 if ri == 0 else 1.0),
                                                                          scalar2=None, op0=ALU.mult))
                if k == LS - 1:
                    for ri in range(2):
                        D(lambda e, ri=ri: e.tensor_copy(out=ALs[:, :, ri, :], in_=ur[:].rearrange("p (a b) -> p a b", a=2)))
                        D(lambda e, ri=ri: e.tensor_scalar(out=BLs[:, :, ri, :], in0=ui[:].rearrange("p (a b) -> p a b", a=2), scalar1=(-1.0 if ri == 0 else 1.0),
                                                           scalar2=None, op0=ALU.mult))
                else:
                    D(lambda e: e.tensor_tensor(out=t1[:], in0=ur[:], in1=pwr[:, :, 8], op=ALU.mult))
                    D(lambda e: e.tensor_tensor(out=t2[:], in0=ui[:], in1=pwi[:, :, 8], op=ALU.mult))
                    D(lambda e: e.tensor_tensor(out=t1[:], in0=t1[:], in1=t2[:], op=ALU.subtract))
                    D(lambda e: e.tensor_tensor(out=t2[:], in0=ur[:], in1=pwi[:, :, 8], op=ALU.mult))
                    D(lambda e: e.tensor_tensor(out=ui[:], in0=ui[:], in1=pwr[:, :, 8], op=ALU.mult))
                    D(lambda e: e.tensor_tensor(out=ui[:], in0=ui[:], in1=t2[:], op=ALU.add))
                    D(lambda e: e.tensor_copy(out=ur[:], in_=t1[:]))
            nr, den, c_r, c_i = v("nr"), v("den"), v("c_r"), v("c_i")
            D(lambda e: e.tensor_scalar(out=nr[:], in0=pwr[:, :, 1], scalar1=-1.0, scalar2=None, op0=ALU.add))
            D(lambda e: e.tensor_tensor(out=t1[:], in0=lre[:], in1=lre[:], op=ALU.mult))
            D(lambda e: e.tensor_tensor(out=t2[:], in0=lim[:], in1=lim[:], op=ALU.mult))
            D(lambda e: e.tensor_tensor(out=den[:], in0=t1[:], in1=t2[:], op=ALU.add))
            D(lambda e: e.reciprocal(out=den[:], in_=den[:]))
            D(lambda e: e.tensor_tensor(out=t1[:], in0=nr[:], in1=lre[:], op=ALU.mult))
            D(lambda e: e.tensor_tensor(out=t2[:], in0=pwi[:, :, 1], in1=lim[:], op=ALU.mult))
            D(lambda e: e.tensor_tensor(out=t1[:], in0=t1[:], in1=t2[:], op=ALU.add))
            D(lambda e: e.tensor_tensor(out=c_r[:], in0=t1[:], in1=den[:], op=ALU.mult))
            D(lambda e: e.tensor_tensor(out=t1[:], in0=pwi[:, :, 1], in1=lre[:], op=ALU.mult))
            D(lambda e: e.tensor_tensor(out=t2[:], in0=nr[:], in1=lim[:], op=ALU.mult))
            D(lambda e: e.tensor_tensor(out=t1[:], in0=t1[:], in1=t2[:], op=ALU.subtract))
            D(lambda e: e.tensor_tensor(out=c_i[:], in0=t1[:], in1=den[:], op=ALU.mult))
            big = lambda n: v(n, (128, 32, 128))
            Xr, Xi, Hr, Hi, tmp = big("Xr"), big("Xi"), big("Hr"), big("Hi"), big("tmpb")
            bbr, bbi = v("bbr", (128, 32, 16)), v("bbi", (128, 32, 16))
            tb16 = v("tb16", (128, 32, 16))

            def cmul(o_r, o_i, ar, ai, br, bi, tm, neg_i=False):
                D(lambda e: e.tensor_tensor(out=o_r, in0=ar, in1=br, op=ALU.mult))
                D(lambda e: e.tensor_tensor(out=tm, in0=ai, in1=bi, op=ALU.mult))
                D(lambda e: e.tensor_tensor(out=o_r, in0=o_r, in1=tm, op=ALU.subtract))
                D(lambda e: e.tensor_tensor(out=o_i, in0=ar, in1=bi, op=ALU.mult))
                D(lambda e: e.tensor_tensor(out=tm, in0=ai, in1=br, op=ALU.mult))
                D(lambda e: e.tensor_tensor(out=o_i, in0=o_i, in1=tm, op=ALU.add))
                if neg_i:
                    D(lambda e: e.tensor_scalar(out=o_i, in0=o_i, scalar1=-1.0, scalar2=None, op0=ALU.mult))
            b16 = lambda a: a[:].unsqueeze(2).broadcast_to([128, 32, 16])
            cmul(bbr[:], bbi[:], b16(c_r), b16(c_i), bre[:], bim[:], tb16[:])
            sel = {n: v(n, (128, 32, 8)) for n in ("gGr", "gGi", "gHr", "gHi", "gIr", "gIi")}
            for j in range(8):
                for (dst_r, dst_i, sr, si, kf, kb) in (("gGr", "gGi", pwr, pwi, 7 - j, j), ("gHr", "gHi", pwr, pwi, j + 1, 8 - j),
                                                       ("gIr", "gIi", ipr, ipi, j + 1, 8 - j)):
                    for dname, src in ((dst_r, sr), (dst_i, si)):
                        D(lambda e, dname=dname, src=src, kf=kf, j=j: e.tensor_copy(out=sel[dname][0:64, :, j], in_=src[0:64, :, kf]))
                        D(lambda e, dname=dname, src=src, kb=kb, j=j: e.tensor_copy(out=sel[dname][64:128, :, j], in_=src[64:128, :, kb]))
            X4 = lambda a: a[:].rearrange("p g (j h) -> p g j h", j=8)
            pj = lambda a: a[:].unsqueeze(3).broadcast_to([128, 32, 8, 16])
            ph = lambda a: a[:].unsqueeze(2).broadcast_to([128, 32, 8, 16])
            cmul(X4(Xr), X4(Xi), pj(sel["gGr"]), pj(sel["gGi"]), ph(bbr), ph(bbi), X4(tmp))
            for g in range(32):
                bank, bres = K.bank()
                for ri, X in enumerate((Xr, Xi)):
                    P.op("pe", lambda e, bank=bank, X=X, g=g, ri=ri: e.transpose(out=bank[:, ri * 128:(ri + 1) * 128], in_=X[:, g, :], identity=ident[:]),
                         reads=["prep", "c_ident"], writes=[bres])
                K.evac(g, GS[:, g, :, :], bank[:, 0:256].rearrange("p (r m) -> p r m", r=2), [bres], [("GS", g)])
            P._add("dve", None, [("GS", g) for g in range(32)], ["prep"], False)
            cmul(X4(Hr), X4(Hi), pj(sel["gHr"]), pj(sel["gHi"]), ph(cre), ph(cim), X4(tmp), neg_i=True)
            D(lambda e: e.memset(H[:], 0.0), "pool")
            for d in range(2):
                sl = slice(d * 64, (d + 1) * 64)
                D(lambda e, sl=sl, d=d: e.tensor_copy(out=H[sl, d, :, 0, :], in_=Hr[sl]))
                D(lambda e, sl=sl, d=d: e.tensor_copy(out=H[sl, d, :, 1, :], in_=Hi[sl]))
            cmul(X4(Xr), X4(Xi), pj(sel["gIr"]), pj(sel["gIi"]), ph(bbr), ph(bbi), X4(tmp))
            tt1 = v("tt1", (128, 128))
            tt2 = v("tt2", (128, 128))
            for g in range(32):
                bk = []
                for d in range(2):
                    bank, bres = K.bank()
                    sl = slice(d * 64, (d + 1) * 64)
                    P.op("pe", lambda e, bank=bank, sl=sl, g=g: e.matmul(bank[:, 0:128], lhsT=Xr[sl, g, :], rhs=Hr[sl, g, :], start=True, stop=False),
                         reads=["prep"], writes=[bres])
                    P.op("pe", lambda e, bank=bank, sl=sl, g=g: e.matmul(bank[:, 0:128], lhsT=Xi[sl, g, :], rhs=Hi[sl, g, :], start=False, stop=True),
                         reads=["prep"], writes=[bres])
                    bk.append((bank, bres))
                P.op("dve", lambda e, b=bk[0][0]: e.tensor_tensor(out=tt1[:], in0=b[:, 0:128], in1=maskF[:], op=ALU.mult), reads=[bk[0][1], "prep"], writes=["tt1"])
                P.op("dve", lambda e, b=bk[1][0]: e.tensor_tensor(out=tt2[:], in0=b[:, 0:128], in1=maskB[:], op=ALU.mult), reads=[bk[1][1], "prep"], writes=["tt2"])
                P.op("dve", lambda e: e.tensor_tensor(out=tt1[:], in0=tt1[:], in1=tt2[:], op=ALU.add), reads=["tt1", "tt2"], writes=["tt1"])
                P.op("dve", lambda e, g=g: e.scalar_tensor_tensor(out=T[:, g, :], in0=ident[:], scalar=dtl[:, g:g + 1], in1=tt1[:], op0=ALU.mult, op1=ALU.add),
                     reads=["tt1", "prep", "c_ident"], writes=[("T", g)])
        K.barrier()
        if S5_CUT == 1:
            return
        SEL, SELT = K.load_const(sb, "c_sel"), K.load_const(sb, "c_selT")
        UTs = sb("UTs", [128, 4, SEQ], BF16)
        U = sb("U", [128, 16, 512], BF16)
        Z = sb("Z", [128, 2, 16, 512], BF16)
        YG = sb("YG", [128, 8, 512], BF16)
        stt = sb("st", [128, 2, 16, NSEG], F32)
        tA = sb("tA", [128, 2, 16, LS], F32)
        tB = sb("tB", [128, 2, 16, LS], F32)
        Ec = sb("Ec", [128, 2, 16, NSEG], F32)
        Esw = sb("Esw", [128, 2, 16, NSEG], F32)
        sg = [sb("sg%d" % i, [128, 512], F32) for i in range(1)]
        bo = [sb("bo%d" % i, [128, 2, 512], BF16) for i in range(1)]
        UT, BT = K.S("UT"), K.S("BT")
        zres = [("Z", ri, gi) for ri in range(2) for gi in range(16)]
        for s in range(NSEQ):
            tb = s * SEQ
            P.dma("sp", lambda e, tb=tb: e.dma_start(out=UTs[:], in_=fm_view(UT, tb, SEQ)), writes=[("UTs", fb) for fb in range(4)])
            for gh in range(2):
                for gi in range(16):
                    g = gh * 16 + gi
                    fb, gl = g // 8, g % 8
                    bank, bres = K.bank()
                    for j in range(8):
                        P.op("pe", lambda e, bank=bank, fb=fb, gl=gl, j=j: e.matmul(bank[:], lhsT=SEL[:, gl, j, :],
                                                                                  rhs=UTs[:, fb, :].rearrange("p (c j) -> p j c", j=8)[:, j, :],
                                                                                  start=(j == 0), stop=(j == 7)),
                             reads=[("UTs", fb), "c_sel"], writes=[bres])
                    K.evac(gi, U[:, gi, :], bank[:], [bres], [("U", gi)])
                if S5_CUT == 2:
                    continue
                for gi in range(16):
                    g = gh * 16 + gi
                    for ri in range(2):
                        bank, bres = K.bank()
                        P.op("pe", lambda e, bank=bank, g=g, ri=ri, gi=gi: e.matmul(bank[0:64, :], lhsT=GS[:, g, ri, 0:64], rhs=U[:, gi, :], start=True, stop=True,
                                                                                 tile_position=(0, 0)),
                             reads=[("GS", g), ("U", gi)], writes=[bres])
                        P.op("pe", lambda e, bank=bank, g=g, ri=ri, gi=gi: e.matmul(bank[64:128, :], lhsT=GS[:, g, ri, 64:128], rhs=U[:, gi, ::-1], start=True, stop=True,
                                                                                 tile_position=(0, 64)),
                             reads=[("GS", g), ("U", gi)], writes=[bres])
                        K.evac(ri, Z[:, ri, gi, :], bank[:], [bres], [("Z", ri, gi)])
                if S5_CUT == 3:
                    continue
                for d, eng in ((0, "dve"),):
                    sl = slice(0, 128)
                    zr = ("Zrec", d)
                    P._add(eng, None, zres, [zr], False)
                    Zv = Z[sl].rearrange("p r g (s m) -> p r g s m", m=LS)
                    bc = lambda a, sl=sl, gh=gh: a[sl, gh].unsqueeze(3).broadcast_to([128, 2, 16, NSEG])
                    r1, r2 = tA[sl, :, :, 0:NSEG], tB[sl, :, :, 0:NSEG]
                    P.op(eng, lambda e, sl=sl: e.memset(stt[sl], 0.0), writes=[("st", d)])
                    for step in range(min(S5_STEPS, LS)):
                        mcol = step if d == 0 else LS - 1 - step
                        P.op(eng, lambda e, sl=sl, r1=r1, bc=bc: e.tensor_tensor(out=r1, in0=bc(A1), in1=stt[sl], op=ALU.mult),
                             reads=[("st", d)], writes=[("r1", d)])
                        P.op(eng, lambda e, sl=sl, r2=r2, bc=bc: e.tensor_tensor(out=r2, in0=bc(Bm), in1=stt[sl, ::-1], op=ALU.mult),
                             reads=[("st", d)], writes=[("r2", d)])
                        P.op(eng, lambda e, r1=r1, r2=r2: e.tensor_tensor(out=r1, in0=r1, in1=r2, op=ALU.add),
                             reads=[("r1", d), ("r2", d)], writes=[("r1", d)])
                        P.op(eng, lambda e, sl=sl, r1=r1, Zv=Zv, mcol=mcol: e.tensor_tensor(out=stt[sl], in0=r1, in1=Zv[:, :, :, :, mcol], op=ALU.add),
                             reads=[("r1", d), zr], writes=[("st", d)])
                        P.op(eng, lambda e, sl=sl, Zv=Zv, mcol=mcol: e.tensor_copy(out=Zv[:, :, :, :, mcol], in_=stt[sl]), reads=[("st", d)], writes=[zr])
                    if S5_SUB < 2:
                        continue
                    mend = LS - 1 if d == 0 else 0
                    order = list(range(NSEG)) if d == 0 else list(range(NSEG - 1, -1, -1))
                    e1, e2 = tA[sl, :, :, 0], tB[sl, :, :, 0]
                    for n_, sg_ in enumerate(order):
                        if n_ == 0:
                            P.op(eng, lambda e, sl=sl, Zv=Zv, sg_=sg_, mend=mend: e.tensor_copy(out=Ec[sl, :, :, sg_], in_=Zv[:, :, :, sg_, mend]),
                                 reads=[zr], writes=[("Ec", d)])
                            continue
                        pv = order[n_ - 1]
                        P.op(eng, lambda e, sl=sl, e1=e1, pv=pv, gh=gh: e.tensor_tensor(out=e1, in0=ALs[sl, gh], in1=Ec[sl, :, :, pv], op=ALU.mult),
                             reads=[("Ec", d)], writes=[("r1", d)])
                        P.op(eng, lambda e, sl=sl, e2=e2, pv=pv, gh=gh: e.tensor_tensor(out=e2, in0=BLs[sl, gh], in1=Ec[sl, ::-1, :, pv], op=ALU.mult),
                             reads=[("Ec", d)], writes=[("r2", d)])
                        P.op(eng, lambda e, e1=e1, e2=e2: e.tensor_tensor(out=e1, in0=e1, in1=e2, op=ALU.add), reads=[("r1", d), ("r2", d)], writes=[("r1", d)])
                        P.op(eng, lambda e, sl=sl, e1=e1, Zv=Zv, sg_=sg_, mend=mend: e.tensor_tensor(out=Ec[sl, :, :, sg_], in0=e1, in1=Zv[:, :, :, sg_, mend], op=ALU.add),
                             reads=[("r1", d), zr, ("Ec", d)], writes=[("Ec", d)])
                    if S5_SUB < 3:
                        continue
                    f1, f2 = tA[sl], tB[sl]
                    for ri in range(2):
                        P.op(eng, lambda e, sl=sl, ri=ri: e.tensor_copy(out=Esw[sl, ri], in_=Ec[sl, 1 - ri]), reads=[("Ec", d)], writes=[("Esw", d)])
                    for sg_ in range(NSEG):
                        src = sg_ - 1 if d == 0 else sg_ + 1
                        if src < 0 or src >= NSEG:
                            continue
                        eb = lambda rev, src=src, sl=sl: (Esw[sl, :, :, src] if rev else Ec[sl, :, :, src]).unsqueeze(3).broadcast_to([128, 2, 16, LS])
                        P.op(eng, lambda e, sl=sl, f1=f1, eb=eb, gh=gh: e.tensor_tensor(out=f1, in0=PA[sl, gh], in1=eb(False), op=ALU.mult), reads=[("Ec", d)], writes=[("r1", d)])
                        P.op(eng, lambda e, sl=sl, f2=f2, eb=eb, gh=gh: e.tensor_tensor(out=f2, in0=PB[sl, gh], in1=eb(True), op=ALU.mult), reads=[("Esw", d)], writes=[("r2", d)])
                        P.op(eng, lambda e, f1=f1, f2=f2: e.tensor_tensor(out=f1, in0=f1, in1=f2, op=ALU.add), reads=[("r1", d), ("r2", d)], writes=[("r1", d)])
                        P.op(eng, lambda e, f1=f1, Zv=Zv, sg_=sg_: e.tensor_tensor(out=Zv[:, :, :, sg_, :], in0=Zv[:, :, :, sg_, :], in1=f1, op=ALU.add),
                             reads=[("r1", d), zr], writes=[zr])
                P._add("pe", None, [("Zrec", 0)], zres, False)
                if S5_CUT == 4:
                    continue
                for gi in range(16):
                    g = gh * 16 + gi
                    fb, gl = g // 8, g % 8
                    bank, bres = K.bank()
                    P.op("pe", lambda e, bank=bank, g=g, gi=gi: e.matmul(bank[:], lhsT=T[:, g, :], rhs=U[:, gi, :], start=True, stop=False),
                         reads=[("T", g), ("U", gi)], writes=[bres])
                    for d in range(2):
                        sl = slice(d * 64, (d + 1) * 64)
                        for ri in range(2):
                            if d == 0:
                                o_, z_ = bank[:, 1:512], Z[:, ri, gi, 0:511]
                            else:
                                o_, z_ = bank[:, 0:511], Z[:, ri, gi, 510::-1]
                            P.op("pe", lambda e, o_=o_, z_=z_, g=g, ri=ri, d=d: e.matmul(o_, lhsT=H[:, d, g, ri, :], rhs=z_, start=False, stop=(d == 1 and ri == 1)),
                                 reads=[("Z", ri, gi), "prep"], writes=[bres])
                    P.op("act", lambda e, bank=bank, gl=gl: e.activation(out=YG[:, gl, :], in_=bank[:], func=AF.Gelu), reads=[bres], writes=[("YG", gl)])
                    if gl == 7:
                        for i in range(8):
                            bank2, b2res = K.bank()
                            for gl2 in range(8):
                                P.op("pe", lambda e, bank2=bank2, gl2=gl2, i=i: e.matmul(bank2[:], lhsT=SELT[:, gl2, i, :], rhs=YG[:, gl2, :], start=(gl2 == 0), stop=(gl2 == 7)),
                                     reads=[("YG", gl2), "c_selT"], writes=[b2res])
                            K.evac(i, UTs[:, fb, :].rearrange("p (c j) -> p j c", j=8)[:, i, :], bank2[:], [b2res], [("UTs", fb)])
            for tt_ in range(SEQ // TT):
                c0 = tt_ * TT
                o, ores = bo[0], ("bo", 0)
                for oc in range(4):
                    bank, bres = K.bank()
                    for k in range(4):
                        P.op("pe", lambda e, bank=bank, k=k, oc=oc, c0=c0: e.matmul(bank[:], lhsT=wglu[:, k, oc * 128:(oc + 1) * 128], rhs=UTs[:, k, c0:c0 + TT],
                                                                                 start=(k == 0), stop=(k == 3)),
                             reads=["w_glu", ("UTs", k)], writes=[bres])
                    sgt, sgr = sg[0], ("sg", 0)
                    P.op("act", lambda e, sgt=sgt, bank=bank, oc=oc: e.activation(out=sgt[:], in_=bank[:], func=AF.Sigmoid, bias=bglu[:, oc:oc + 1], scale=1.0),
                         reads=[bres, "b_glu"], writes=[sgr])
                    P.op("dve", lambda e, sgt=sgt, o=o, oc=oc, c0=c0: e.tensor_tensor(out=o[:, oc % 2, :], in0=UTs[:, oc, c0:c0 + TT], in1=sgt[:], op=ALU.mult),
                         reads=[sgr, ("UTs", oc)], writes=[ores + (oc % 2,)])
                    if oc % 2 == 1:
                        P.dma("pool", lambda e, o=o, c0=c0, tb=tb, oc=oc: e.dma_start(out=fm_view(BT, tb + c0, TT)[:, oc - 1:oc + 1, :], in_=o[:]),
                              reads=[ores + (q,) for q in range(2)], writes=[("BT", tb + c0, oc)])
    K.barrier()


S5_STEPS = 512
S5_CUT = 0
S5_SUB = 3
PHASES["A2"] = phase_s5


ALL_PHASES = ["A0", "A1", "A2", "A3", "A4", "A5", "B0", "B1", "B3", "B4", "B5"]
LAUNCHES = [ALL_PHASES]


def kernel(**inputs):
    inp = {k: np.asarray(v) for k, v in inputs.items()}
    lay = host_layout(inp)
    x, p = inp["x"], inp["p"]
    core_inputs = []
    for c in range(NCORES):
        ci = {k: v for k, v in inp.items() if k not in ("x", "p")}
        ci.update(lay)
        ci["x"] = np.ascontiguousarray(x[NSEQ * c:NSEQ * (c + 1)].reshape(NTOK, D))
        ci["p"] = np.ascontiguousarray(p[:, NSEQ * c:NSEQ * (c + 1)].reshape(2, NTOK, 256))
        core_inputs.append(ci)
    res = run_launch(ALL_PHASES, core_inputs, ext_out=("OUT",))
    out = np.stack([np.asarray(r["OUT"]).reshape(NSEQ, SEQ, D) for r in res], axis=0).reshape(NCORES * NSEQ, SEQ, D)
    return out.astype(np.float32)
```
